# Optimizing a Trainium2 kernel written in Bass

```python
import jax, jax.numpy as jnp
from jax import lax
import numpy as np

D_MODEL = 2048
BATCH = 2
SEQ = 8192
DEPTH = 1

NORM_EPS = 1e-6
ROPE_THETA = 10000.0
GLA_HEADS = 4
GLA_DK = 256
GLA_DV = 512
GLA_GATE_RANK = 16
GLA_GATE_NORMALIZER = 16.0
GLA_CHUNK = 64
DSA_HEADS = 16
DSA_KV_HEADS = 4
DSA_HEAD_DIM = 128
IDX_HEADS = 16
IDX_DIM = 128
IDX_ROPE_DIM = 64
TOPK_MAX = 256
Q_BLOCK = 128
D_FF = 4 * D_MODEL
IN_SPLITS = (
    GLA_HEADS * GLA_DK,
    GLA_HEADS * GLA_DK,
    GLA_HEADS * GLA_DV,
    GLA_HEADS * GLA_DV,
    GLA_GATE_RANK,
    DSA_HEADS * DSA_HEAD_DIM,
    DSA_KV_HEADS * DSA_HEAD_DIM,
    DSA_KV_HEADS * DSA_HEAD_DIM,
    IDX_HEADS * IDX_DIM,
    IDX_DIM,
    IDX_HEADS,
    2 * D_MODEL,
)
D_IN_PROJ = sum(IN_SPLITS)

kernel_name = 'hybrid_gla_dsa_block'


def rmsnorm(x, g):
    xf = x.astype(jnp.float32)
    y = xf * lax.rsqrt(jnp.mean(xf * xf, axis=-1, keepdims=True) + NORM_EPS)
    return (y * g.astype(jnp.float32)).astype(x.dtype)


def rope(x, pos, rot_dim):
    half = rot_dim // 2
    inv_freq = ROPE_THETA ** (-jnp.arange(half, dtype=jnp.float32) * 2.0 / rot_dim)
    ang = pos.astype(jnp.float32)[:, None] * inv_freq[None, :]
    cos = jnp.cos(ang)[None, :, None, :]
    sin = jnp.sin(ang)[None, :, None, :]
    xf = x.astype(jnp.float32)
    x1 = xf[..., :half]
    x2 = xf[..., half:rot_dim]
    out = jnp.concatenate([x1 * cos - x2 * sin, x2 * cos + x1 * sin, xf[..., rot_dim:]], axis=-1)
    return out.astype(x.dtype)


def gla_chunked(q, k, v, log_a):
    B, S, H, dk = q.shape
    dv = v.shape[-1]
    C = GLA_CHUNK
    n = S // C

    def chunks(t):
        return t.astype(jnp.float32).reshape(B, n, C, H, t.shape[-1]).transpose(1, 0, 3, 2, 4)

    qc = chunks(q) * (dk ** -0.5)
    kc, vc, gc = chunks(k), chunks(v), chunks(log_a)
    causal = jnp.tril(jnp.ones((C, C), dtype=bool))[:, :, None]

    def step(state, inp):
        qi, ki, vi, gi = inp
        b = jnp.cumsum(gi, axis=2)
        o_inter = jnp.einsum('bhcd,bhde->bhce', qi * jnp.exp(b), state)
        diff = b[:, :, :, None, :] - b[:, :, None, :, :]
        decay = jnp.exp(jnp.where(causal, diff, -jnp.inf))
        scores = jnp.einsum('bhid,bhjd,bhijd->bhij', qi, ki, decay)
        o_intra = jnp.einsum('bhij,bhje->bhie', scores, vi)
        b_last = b[:, :, -1:, :]
        state = state * jnp.exp(b_last[:, :, 0, :])[..., None] + jnp.einsum('bhcd,bhce->bhde', ki * jnp.exp(b_last - b), vi)
        return state, o_inter + o_intra

    state0 = jnp.zeros((B, H, dk, dv), jnp.float32)
    _, out = lax.scan(step, state0, (qc, kc, vc, gc))
    return out.transpose(1, 0, 3, 2, 4).reshape(B, S, H, dv).astype(v.dtype)


def dsa_sparse_attention(q, k, v, q_idx, k_idx, w_idx, topk):
    B, S, H, hd = q.shape
    hkv = k.shape[2]
    grp = H // hkv
    nb = S // Q_BLOCK
    key_pos = jnp.arange(S, dtype=jnp.int32)
    q_pos = key_pos.reshape(nb, Q_BLOCK)
    gather_rows = jax.vmap(lambda t, i: t[i])
    k_idx_f = k_idx.astype(jnp.float32)

    def blocks(t):
        return t.reshape(B, nb, Q_BLOCK, *t.shape[2:]).swapaxes(0, 1)

    def one_block(args):
        qb, qib, wb, qp = args
        rel = jax.nn.relu(jnp.einsum('bthd,bsd->bths', qib.astype(jnp.float32), k_idx_f))
        score = jnp.einsum('bth,bths->bts', wb.astype(jnp.float32), rel)
        score = jnp.where(key_pos[None, None, :] <= qp[None, :, None], score, -jnp.inf)
        _, sel = lax.top_k(score, topk)
        valid = sel <= qp[None, :, None]
        ks = gather_rows(k, sel).astype(jnp.float32)
        vs = gather_rows(v, sel).astype(jnp.float32)
        qg = qb.astype(jnp.float32).reshape(B, Q_BLOCK, hkv, grp, hd)
        logits = jnp.einsum('btkgd,btskd->btkgs', qg, ks) * (hd ** -0.5)
        logits = jnp.where(valid[:, :, None, None, :], logits, -jnp.inf)
        p = jax.nn.softmax(logits, axis=-1)
        o = jnp.einsum('btkgs,btskd->btkgd', p, vs)
        return o.reshape(B, Q_BLOCK, H, hd).astype(v.dtype)

    out = lax.map(one_block, (blocks(q), blocks(q_idx), blocks(w_idx), q_pos))
    return out.swapaxes(0, 1).reshape(B, S, H, hd)


def hybrid_block(x, norm1_g, w_in, gla_wg2, gla_bg, gla_norm_g, w_proj_gla, q_norm_g, k_norm_g,
                 idx_k_norm_g, w_proj_dsa, b_gate, w_out, norm2_g, w_ff1, w_ff2):
    B, S, _ = x.shape
    pos = jnp.arange(S, dtype=jnp.int32)
    topk = min(TOPK_MAX, S // 4)

    h = rmsnorm(x, norm1_g)
    proj = h @ w_in
    points = np.cumsum(IN_SPLITS)[:-1].tolist()
    gq, gk, gv, gr, glr, dq, dk, dv, iq, ik, iw, gates = jnp.split(proj, points, axis=-1)

    log_a = jax.nn.log_sigmoid((glr @ gla_wg2 + gla_bg).astype(jnp.float32)) / GLA_GATE_NORMALIZER
    o_gla = gla_chunked(gq.reshape(B, S, GLA_HEADS, GLA_DK), gk.reshape(B, S, GLA_HEADS, GLA_DK),
                        gv.reshape(B, S, GLA_HEADS, GLA_DV), log_a.reshape(B, S, GLA_HEADS, GLA_DK))
    o_gla = rmsnorm(o_gla, gla_norm_g) * jax.nn.silu(gr.reshape(B, S, GLA_HEADS, GLA_DV))
    y_gla = o_gla.reshape(B, S, GLA_HEADS * GLA_DV) @ w_proj_gla

    dq = rope(rmsnorm(dq.reshape(B, S, DSA_HEADS, DSA_HEAD_DIM), q_norm_g), pos, DSA_HEAD_DIM)
    dk = rope(rmsnorm(dk.reshape(B, S, DSA_KV_HEADS, DSA_HEAD_DIM), k_norm_g), pos, DSA_HEAD_DIM)
    dv = dv.reshape(B, S, DSA_KV_HEADS, DSA_HEAD_DIM)
    iq = rope(iq.reshape(B, S, IDX_HEADS, IDX_DIM), pos, IDX_ROPE_DIM) * (IDX_DIM ** -0.5)
    ik = rope(rmsnorm(ik, idx_k_norm_g)[:, :, None, :], pos, IDX_ROPE_DIM)[:, :, 0, :]
    iw = iw * (IDX_HEADS ** -0.5)
    o_dsa = dsa_sparse_attention(dq, dk, dv, iq, ik, iw, topk)
    y_dsa = o_dsa.reshape(B, S, DSA_HEADS * DSA_HEAD_DIM) @ w_proj_dsa

    g_gla, g_dsa = jnp.split(jax.nn.sigmoid(gates + b_gate), 2, axis=-1)
    x = x + (g_gla * y_gla + g_dsa * y_dsa) @ w_out

    h2 = rmsnorm(x, norm2_g)
    return x + jnp.square(jax.nn.relu(h2 @ w_ff1)) @ w_ff2


def setup_inputs(seed: int = 0) -> dict:
    key = jax.random.key(seed)
    ks = jax.random.split(key, 16)

    def nrm(k, shape, scale):
        return jax.random.normal(k, shape, jnp.float32) * scale

    def gain(k, n):
        return 1.0 + nrm(k, (DEPTH, n), 0.02)

    return {
        'x': nrm(ks[0], (BATCH, SEQ, D_MODEL), 1.0),
        'norm1_g': gain(ks[1], D_MODEL),
        'w_in': nrm(ks[2], (DEPTH, D_MODEL, D_IN_PROJ), D_MODEL ** -0.5),
        'gla_wg2': nrm(ks[3], (DEPTH, GLA_GATE_RANK, GLA_HEADS * GLA_DK), GLA_GATE_RANK ** -0.5),
        'gla_bg': nrm(ks[4], (DEPTH, GLA_HEADS * GLA_DK), 0.1),
        'gla_norm_g': gain(ks[5], GLA_DV),
        'w_proj_gla': nrm(ks[6], (DEPTH, GLA_HEADS * GLA_DV, D_MODEL), (GLA_HEADS * GLA_DV) ** -0.5),
        'q_norm_g': gain(ks[7], DSA_HEAD_DIM),
        'k_norm_g': gain(ks[8], DSA_HEAD_DIM),
        'idx_k_norm_g': gain(ks[9], IDX_DIM),
        'w_proj_dsa': nrm(ks[10], (DEPTH, DSA_HEADS * DSA_HEAD_DIM, D_MODEL), (DSA_HEADS * DSA_HEAD_DIM) ** -0.5),
        'b_gate': nrm(ks[11], (DEPTH, 2 * D_MODEL), 0.02),
        'w_out': nrm(ks[12], (DEPTH, D_MODEL, D_MODEL), D_MODEL ** -0.5),
        'norm2_g': gain(ks[13], D_MODEL),
        'w_ff1': nrm(ks[14], (DEPTH, D_MODEL, D_FF), D_MODEL ** -0.5),
        'w_ff2': nrm(ks[15], (DEPTH, D_FF, D_MODEL), D_FF ** -0.5),
    }


def reference(x, norm1_g, w_in, gla_wg2, gla_bg, gla_norm_g, w_proj_gla, q_norm_g, k_norm_g,
              idx_k_norm_g, w_proj_dsa, b_gate, w_out, norm2_g, w_ff1, w_ff2):
    for l in range(DEPTH):
        x = hybrid_block(x, norm1_g[l], w_in[l], gla_wg2[l], gla_bg[l], gla_norm_g[l], w_proj_gla[l],
                         q_norm_g[l], k_norm_g[l], idx_k_norm_g[l], w_proj_dsa[l], b_gate[l], w_out[l],
                         norm2_g[l], w_ff1[l], w_ff2[l])
    return x
```

```python
import math
import numpy as np
from contextlib import ExitStack
import concourse.bass as bass
import concourse.mybir as mybir
from concourse.bass_utils import run_bass_kernel_spmd

F32 = mybir.dt.float32
BF16 = mybir.dt.bfloat16
AF = mybir.ActivationFunctionType
ALU = mybir.AluOpType
AX = mybir.AxisListType

D = 2048
SEQ = 8192
NTALL = 64
NOWN = 16
TOWN = 2048
DIN = 15520
DFF = 8192
O_GQ, O_GK, O_GV, O_GR, O_GLR, O_DQ, O_DK, O_DV, O_IQ, O_IK, O_IW, O_GATES = (
    0, 1024, 2048, 4096, 6144, 6160, 8208, 8720, 9232, 11280, 11408, 11424)
EPS = 1e-6
NEG = -1.0e30
TOPK = 256
NIT = 18

EPOCH = 30000
DMA_EPOCH = 1800
ENGS = ('pe', 'act', 'dve', 'pool', 'sp')


class R:
    __slots__ = ('name', 'w', 'rs')

    def __init__(self, name=''):
        self.name = name
        self.w = None
        self.rs = []


class Op:
    __slots__ = ('eng', 'fn', 'deps', 'key', 'kidx', 'sig', 'sidx')

    def __init__(self, eng, fn, key):
        self.eng = eng
        self.fn = fn
        self.deps = []
        self.key = key
        self.kidx = 0
        self.sig = False
        self.sidx = 0


class Sched:
    _blkc = [0]

    def __init__(self):
        self.ops = {e: [] for e in ENGS}
        self.kcount = {}
        self.klast = {}
        self.allres = []

    def res(self, name=''):
        r = R(name)
        self.allres.append(r)
        return r

    def add(self, eng, fn, reads=(), writes=(), key=None):
        op = Op(eng, fn, key)
        deps = {}
        for r in reads:
            if r.w is not None:
                deps[id(r.w)] = r.w
        for w in writes:
            if w.w is not None:
                deps[id(w.w)] = w.w
            for o in w.rs:
                deps[id(o)] = o
        if key is not None:
            p = self.klast.get(key)
            if p is not None:
                deps[id(p)] = p
            self.klast[key] = op
            self.kcount[key] = self.kcount.get(key, 0) + 1
            op.kidx = self.kcount[key]
        for d in deps.values():
            if d is op:
                continue
            if d.key is None and d.eng == 'pe' and eng == 'pe' and key is None:
                continue
            op.deps.append(d)
            if d.key is None:
                d.sig = True
        for r in reads:
            r.rs.append(op)
        for w in writes:
            w.w = op
            w.rs = []
        self.ops[eng].append(op)
        return op

    def pe(self, fn, reads=(), writes=()):
        return self.add('pe', fn, reads, writes)

    def act(self, fn, reads=(), writes=()):
        return self.add('act', fn, reads, writes)

    def dve(self, fn, reads=(), writes=()):
        return self.add('dve', fn, reads, writes)

    def pool(self, fn, reads=(), writes=()):
        return self.add('pool', fn, reads, writes)

    def dma(self, eng, key, fn, reads=(), writes=()):
        return self.add(eng, fn, reads, writes, key=key)

    def emit(self, nc):
        blk = self._blkc[0]
        self._blkc[0] += 1
        nsig = {}
        for e in ENGS:
            c = 0
            if self.ops[e] and self.ops[e][-1].key is None:
                self.ops[e][-1].sig = True
            for op in self.ops[e]:
                if op.key is None and op.sig:
                    c += 1
                    op.sidx = c
            nsig[e] = c
        handles = []

        def alloc(name):
            h = nc.alloc_semaphore(name=name)
            handles.append(h)
            return h

        if True:
            esem = {}
            for e in ENGS:
                n = (nsig[e] + EPOCH - 1) // EPOCH
                esem[e] = [alloc(f"s{blk}_{e}_{i}") for i in range(n)]
            ksem = {}
            for k, cnt in self.kcount.items():
                n = (cnt + DMA_EPOCH - 1) // DMA_EPOCH
                ksem[k] = [alloc(f"k{blk}_{k}_{i}") for i in range(n)]

            def sem_of(d):
                if d.key is None:
                    i = d.sidx - 1
                    return esem[d.eng][i // EPOCH], (i % EPOCH) + 1
                i = d.kidx - 1
                return ksem[d.key][i // DMA_EPOCH], 16 * ((i % DMA_EPOCH) + 1)

            def run(ename, eng):
                waited = {}
                for op in self.ops[ename]:
                    for d in op.deps:
                        s, v = sem_of(d)
                        if waited.get(id(s), 0) >= v:
                            continue
                        waited[id(s)] = v
                        eng.wait_ge(s, v)
                    ins = op.fn(eng)
                    if op.key is not None:
                        s, v = sem_of(op)
                        ins.then_inc(s, 16)
                    elif op.sig:
                        s, v = sem_of(op)
                        ins.then_inc(s, 1)
                for e2 in ENGS:
                    if nsig[e2] > 0:
                        i = nsig[e2] - 1
                        s, v = esem[e2][i // EPOCH], (i % EPOCH) + 1
                        if waited.get(id(s), 0) < v:
                            eng.wait_ge(s, v)
                for k in self.kcount:
                    s, v = sem_of(self.klast[k])
                    if waited.get(id(s), 0) < v:
                        eng.wait_ge(s, v)

            with nc.Block() as block:
                block.tensor(lambda e: run('pe', e))
                block.scalar(lambda e: run('act', e))
                block.vector(lambda e: run('dve', e))
                block.gpsimd(lambda e: run('pool', e))
                block.sync(lambda e: run('sp', e))
        nc.all_engine_barrier()
        nc.clear_and_free_semaphores(handles)
        nc.all_engine_barrier()
        n_ops = {e: len(self.ops[e]) for e in ENGS}
        self.ops = {e: [] for e in ENGS}
        self.kcount = {}
        self.klast = {}
        for r in self.allres:
            r.w = None
            r.rs = []
        self.allres = []
        return n_ops


class Buf:
    __slots__ = ('t', 'r')

    def __init__(self, t, r):
        self.t = t
        self.r = r


STORE_ENG = 'act'


class K:
    def __init__(self, nc, debug=False):
        self.nc = nc
        self.S = Sched()
        self.es = None
        self.uid = 0
        self.debug = debug

    def sb(self, name, shape, dt):
        self.uid += 1
        t = self.es.enter_context(self.nc.sbuf_tensor(f"{name}_{self.uid}", shape, dt))
        return Buf(t, self.S.res(name))

    def psum(self, name, shape, dt):
        self.uid += 1
        t = self.es.enter_context(self.nc.psum_tensor(f"{name}_{self.uid}", shape, dt))
        return Buf(t, self.S.res(name))


def build_program(debug=False):
    nc = bass.Bass("TRN2", target_bir_lowering=False)
    kb = K(nc, debug)
    S = kb.S

    def din(name, shape, dt=F32):
        return nc.dram_tensor(name, shape, dt, kind="ExternalInput").ap()

    def dscr(name, shape, dt=BF16):
        kind = "ExternalOutput" if debug else "Internal"
        return nc.dram_tensor(name, shape, dt, kind=kind).ap()

    xfull = din("xfull", [SEQ, D])
    xown = din("xown", [TOWN, D])
    w_in = din("w_in", [D, DIN])
    gla_wg2 = din("gla_wg2", [16, 1024])
    gla_bg = din("gla_bg", [1, 1024])
    gla_norm_g = din("gla_norm_g", [1, 512])
    w_proj_gla = din("w_proj_gla", [D, D])
    q_norm_g = din("q_norm_g", [1, 128])
    k_norm_g = din("k_norm_g", [1, 128])
    idx_k_norm_g = din("idx_k_norm_g", [1, 128])
    w_proj_dsa = din("w_proj_dsa", [D, D])
    b_gate = din("b_gate", [1, 4096])
    w_out = din("w_out", [D, D])
    norm1_g = din("norm1_g", [1, D])
    norm2_g = din("norm2_g", [1, D])
    w_ff1 = din("w_ff1", [D, DFF])
    w_ff2 = din("w_ff2", [DFF, D])
    c_ident = din("c_ident", [128, 128])
    c_flags = din("c_flags", [128, 4])
    c_maskd = din("c_maskd", [128, 512])
    c_uincl = din("c_uincl", [128, 128])
    c_ustrict = din("c_ustrict", [128, 128])
    c_pdsaT = din("c_pdsaT", [128, 128])
    c_pidxT = din("c_pidxT", [128, 128])
    c_cosK = din("c_cosK", [128, SEQ])
    c_sinK = din("c_sinK", [128, SEQ])
    c_cosIK = din("c_cosIK", [128, SEQ])
    c_sinIK = din("c_sinIK", [128, SEQ])
    c_cosQ = din("c_cosQ", [128, TOWN])
    c_sinQ = din("c_sinQ", [128, TOWN])
    c_cosIQ = din("c_cosIQ", [128, TOWN])
    c_sinIQ = din("c_sinIQ", [128, TOWN])

    out = nc.dram_tensor("out", [TOWN, D], F32, kind="ExternalOutput").ap()

    KT = dscr("KT", [4, 128, SEQ])
    VV = dscr("VV", [SEQ, 512])
    IKT = dscr("IKT", [128, SEQ])
    SNAP = dscr("SNAP", [NTALL, 128, 4096])
    QT = dscr("QT", [1024, TOWN])
    KGT = dscr("KGT", [1024, TOWN])
    VG = dscr("VG", [TOWN, 2048])
    GRT = dscr("GRT", [2048, TOWN])
    DQT = dscr("DQT", [2048, TOWN])
    IQT = dscr("IQT", [2048, TOWN])
    IWS = dscr("IWS", [TOWN, 16], F32)
    GT = dscr("GT", [4096, TOWN])
    OGT = dscr("OGT", [2048, TOWN])
    MT = dscr("MT", [2048, TOWN])
    ODT = dscr("ODT", [2048, TOWN])
    MT2 = dscr("MT2", [2048, TOWN])
    X1 = dscr("X1", [TOWN, D], F32)
    HTS = dscr("HTS", [NTALL, 128, 2048])
    EQT = dscr("EQT", [1024, TOWN])
    EKT = dscr("EKT", [1024, TOWN])

    dr = {}

    def DR(name):
        if name not in dr:
            dr[name] = S.res(name)
        return dr[name]

    class Common:
        pass

    def setup_common(p, npsf=6):
        c = Common()
        c.p = p
        c.identf = kb.sb("identf", [128, 128], F32)
        c.identb = kb.sb("identb", [128, 128], BF16)
        c.onesb = kb.sb("onesb", [128, 128], BF16)
        c.cst = kb.sb("cst", [128, 8], F32)
        S.dma('sp', 'cid', lambda e: e.dma_start(out=c.identf.t[:], in_=c_ident), writes=[c.identf.r])
        S.dve(lambda e: e.tensor_copy(out=c.identb.t[:], in_=c.identf.t[:]), reads=[c.identf.r], writes=[c.identb.r])
        S.dve(lambda e: e.memset(c.onesb.t[:], 1.0), writes=[c.onesb.r])
        vals = [EPS, 1.0, math.log(1.0 / 16.0), 0.0, 0.5, float(TOPK), 0.0, 0.0]
        for i, v in enumerate(vals):
            S.dve(lambda e, i=i, v=v: e.memset(c.cst.t[:, i:i + 1], v), writes=[c.cst.r])
        c.ps = [kb.psum(f"ps{i}", [128, 512], F32) for i in range(npsf)]
        c.pb = [kb.psum(f"pb{i}", [128, 1024], BF16) for i in range(8 - npsf)]
        c.psi = 0
        c.wslots = None
        c.wi = 0
        return c

    def next_ps(c):
        b = c.ps[c.psi % len(c.ps)]
        c.psi += 1
        return b

    def load_cols(c, name, src_row, n):
        t = kb.sb(name, [128, n], F32)
        S.dma('sp', 'cols_' + name,
              lambda e: e.dma_start(out=t.t[:], in_=src_row.rearrange("o (c p) -> p (o c)", p=128),
                                    allow_slow_non_contiguous=True), writes=[t.r])
        return t

    def make_norm_ctx(c, nslots=2):
        n = Common()
        n.xt = [kb.sb(f"xt{i}", [128, D], F32) for i in range(nslots)]
        n.junk = kb.sb("junk", [128, D], BF16)
        n.ss = [kb.sb(f"ss{i}", [128, 4], F32) for i in range(nslots)]
        n.h = [kb.sb(f"h{i}", [128, D], BF16) for i in range(nslots)]
        n.i = 0
        return n

    def norm_T(c, n, x_ap, gcol, dst_fn, dst_r, xkeep=None):
        sl = n.i % len(n.xt)
        n.i += 1
        xt, ss, h = n.xt[sl], n.ss[sl], n.h[sl]
        S.dma('sp', f'x{sl}', lambda e: e.dma_start(out=xt.t[:], in_=x_ap), writes=[xt.r])
        S.act(lambda e: e.activation(out=n.junk.t[:], in_=xt.t[:], func=AF.Square, accum_out=ss.t[:, 0:1]),
              reads=[xt.r], writes=[n.junk.r, ss.r])
        S.act(lambda e: e.activation(out=ss.t[:, 1:2], in_=ss.t[:, 0:1], func=AF.Ln, scale=1.0 / D,
                                     bias=c.cst.t[:, 0:1]), reads=[ss.r, c.cst.r], writes=[ss.r])
        S.act(lambda e: e.activation(out=ss.t[:, 2:3], in_=ss.t[:, 1:2], func=AF.Exp, scale=-0.5), reads=[ss.r], writes=[ss.r])
        S.dve(lambda e: e.tensor_scalar(out=h.t[:], in0=xt.t[:], scalar1=ss.t[:, 2:3], scalar2=None, op0=ALU.mult),
              reads=[xt.r, ss.r], writes=[h.r])
        for half in range(2):
            pb = c.pb[half % len(c.pb)]
            for k in range(8):
                kc = half * 8 + k
                S.pe(lambda e, k=k, kc=kc, pb=pb: e.transpose(out=pb.t[:, k * 128:(k + 1) * 128],
                                                              in_=h.t[:, kc * 128:(kc + 1) * 128], identity=c.identb.t[:]),
                     reads=[h.r, c.identb.r], writes=[pb.r])
            S.dve(lambda e, half=half, pb=pb: e.tensor_tensor(
                out=dst_fn(half * 8, half * 8 + 8), in0=pb.t[:, 0:1024].rearrange("p (a b) -> p a b", b=128),
                in1=gcol.t[:, half * 8:half * 8 + 8].unsqueeze(2).to_broadcast([128, 8, 128]), op=ALU.mult),
                reads=[pb.r, gcol.r], writes=[dst_r])

    def load_w(c, wt, W, r0, nkc, c0, ncols, key):
        src = W[r0:r0 + nkc * 128, c0:c0 + ncols].rearrange("(k p) c -> p k c", p=128)
        S.dma('pool', key, lambda e: e.dma_start(out=wt.t[:, 0:nkc, 0:ncols], in_=src), writes=[wt.r])

    def make_wslots(c, n=3, kc=16, cols=512):
        c.wslots = [kb.sb(f"wsl{i}", [128, kc, cols], BF16) for i in range(n)]
        c.wi = 0

    def next_w(c):
        i = c.wi % len(c.wslots)
        c.wi += 1
        return c.wslots[i], f"w{i}"

    def gemm_fm(c, W, c0, ncols, act_fn, act_rs, T, epi, grp=512):
        pending = None
        for g0 in range(0, ncols, grp):
            gn = min(grp, ncols - g0)
            wt, key = next_w(c)
            load_w(c, wt, W, 0, 16, c0 + g0, gn, key)
            for cb0 in range(0, gn, 128):
                m = min(128, gn - cb0)
                for t0 in range(0, T, 512):
                    tn = min(512, T - t0)
                    ps = next_ps(c)
                    for kc in range(16):
                        S.pe(lambda e, kc=kc, cb0=cb0, m=m, t0=t0, tn=tn, ps=ps, wt=wt:
                             e.matmul(out=ps.t[0:m, 0:tn], lhsT=wt.t[:, kc, cb0:cb0 + m], rhs=act_fn(kc, t0, tn),
                                      start=(kc == 0), stop=(kc == 15)),
                             reads=[wt.r] + list(act_rs), writes=[ps.r])
                    if pending is not None:
                        pending()
                    pending = (lambda cb=(g0 + cb0) // 128, t0=t0, tn=tn, ps=ps, m=m: epi(cb, t0, tn, ps, m))
        if pending is not None:
            pending()

    def gemm_tm(c, W, c0, ncols, KC, act_fn, act_rs, ntt, epi, grp=512):
        for g0 in range(0, ncols, grp):
            gn = min(grp, ncols - g0)
            pss = [next_ps(c) for _ in range(ntt)]
            for ks in range(0, KC, 16):
                wt, key = next_w(c)
                load_w(c, wt, W, ks * 128, 16, c0 + g0, gn, key)
                for tt in range(ntt):
                    ps = pss[tt]
                    for k in range(16):
                        kc = ks + k
                        S.pe(lambda e, kc=kc, k=k, tt=tt, ps=ps, wt=wt, gn=gn:
                             e.matmul(out=ps.t[:, 0:gn], lhsT=act_fn(kc, tt), rhs=wt.t[:, k, 0:gn],
                                      start=(kc == 0), stop=(kc == KC - 1)),
                             reads=[wt.r] + list(act_rs), writes=[ps.r])
            for tt in range(ntt):
                epi(tt, g0, gn, pss[tt])

    def make_rope_ctx(c, N):
        r = Common()
        r.N = N
        r.sq = kb.sb("r_sq", [128, N], BF16)
        r.rstd = kb.sb("r_rstd", [128, N], F32)
        r.xnb = kb.sb("r_xnb", [128, N], BF16)
        r.t1 = kb.sb("r_t1", [128, N], F32)
        r.t2 = kb.sb("r_t2", [128, N], F32)
        return r

    def headnorm_rope(c, r, ps, n, gain, cosb, sinb, cs_ap, pmatT, dst, dst_r, ps2=None, ps3=None):
        if gain is not None:
            S.act(lambda e: e.activation(out=r.sq.t[:, 0:n], in_=ps.t[:, 0:n], func=AF.Square), reads=[ps.r], writes=[r.sq.r])
            ps2 = ps2 if ps2 is not None else next_ps(c)
            S.pe(lambda e: e.matmul(out=ps2.t[:, 0:n], lhsT=c.onesb.t[:], rhs=r.sq.t[:, 0:n], start=True, stop=True),
                 reads=[c.onesb.r, r.sq.r], writes=[ps2.r])
            S.act(lambda e: e.activation(out=r.t1.t[:, 0:n], in_=ps2.t[:, 0:n], func=AF.Ln, scale=1.0 / 128,
                                         bias=c.cst.t[:, 0:1]), reads=[ps2.r, c.cst.r], writes=[r.t1.r])
            S.act(lambda e: e.activation(out=r.rstd.t[:, 0:n], in_=r.t1.t[:, 0:n], func=AF.Exp, scale=-0.5),
                  reads=[r.t1.r], writes=[r.rstd.r])
            S.dve(lambda e: e.scalar_tensor_tensor(out=r.xnb.t[:, 0:n], in0=ps.t[:, 0:n], scalar=gain.t[:, 0:1],
                                                   in1=r.rstd.t[:, 0:n], op0=ALU.mult, op1=ALU.mult),
                  reads=[ps.r, gain.r, r.rstd.r], writes=[r.xnb.r])
        else:
            S.act(lambda e: e.activation(out=r.xnb.t[:, 0:n], in_=ps.t[:, 0:n], func=AF.Copy), reads=[ps.r], writes=[r.xnb.r])
        ps3 = ps3 if ps3 is not None else next_ps(c)
        S.pe(lambda e: e.matmul(out=ps3.t[:, 0:n], lhsT=pmatT.t[:], rhs=r.xnb.t[:, 0:n], start=True, stop=True),
             reads=[pmatT.r, r.xnb.r], writes=[ps3.r])
        S.dve(lambda e: e.tensor_tensor(out=r.t1.t[:, 0:n], in0=r.xnb.t[:, 0:n], in1=cs_ap(cosb), op=ALU.mult),
              reads=[r.xnb.r, cosb.r], writes=[r.t1.r])
        S.dve(lambda e: e.tensor_tensor(out=r.t2.t[:, 0:n], in0=ps3.t[:, 0:n], in1=cs_ap(sinb), op=ALU.mult),
              reads=[ps3.r, sinb.r], writes=[r.t2.r])
        S.dve(lambda e: e.tensor_tensor(out=dst, in0=r.t1.t[:, 0:n], in1=r.t2.t[:, 0:n], op=ALU.add),
              reads=[r.t1.r, r.t2.r], writes=[dst_r])

    def load_pmat(c, name, src):
        f = kb.sb(name + "f", [128, 128], F32)
        b = kb.sb(name + "b", [128, 128], BF16)
        S.dma('sp', 'pm_' + name, lambda e: e.dma_start(out=f.t[:], in_=src), writes=[f.r])
        S.dve(lambda e: e.tensor_copy(out=b.t[:], in_=f.t[:]), reads=[f.r], writes=[b.r])
        return b

    def glr_logsig(c, g, hT_fn, hT_r):
        psg = next_ps(c)
        for kc in range(16):
            S.pe(lambda e, kc=kc: e.matmul(out=psg.t[0:16, 0:128], lhsT=g.wl.t[:, kc, :], rhs=hT_fn(kc),
                                           start=(kc == 0), stop=(kc == 15)), reads=[g.wl.r, hT_r], writes=[psg.r])
        S.act(lambda e: e.activation(out=g.glrT.t[:], in_=psg.t[0:16, 0:128], func=AF.Copy), reads=[psg.r], writes=[g.glrT.r])
        for hf in range(2):
            psz = next_ps(c)
            S.pe(lambda e, hf=hf, psz=psz: e.matmul(out=psz.t[:], lhsT=g.glrT.t[:], rhs=g.wg2b.t[:, hf * 512:(hf + 1) * 512],
                                                    start=True, stop=False), reads=[g.glrT.r, g.wg2b.r], writes=[psz.r])
            S.pe(lambda e, hf=hf, psz=psz: e.matmul(out=psz.t[:], lhsT=g.onesrow.t[:], rhs=g.bgb.t[:, hf * 512:(hf + 1) * 512],
                                                    start=False, stop=True), reads=[g.onesrow.r, g.bgb.r], writes=[psz.r])
            S.act(lambda e, hf=hf, psz=psz: e.activation(out=g.ez.t[:, hf * 512:(hf + 1) * 512], in_=psz.t[:], func=AF.Exp, scale=-1.0),
                  reads=[psz.r], writes=[g.ez.r])
        S.act(lambda e: e.activation(out=g.L.t[:], in_=g.ez.t[:], func=AF.Ln, bias=c.cst.t[:, 1:2], scale=1.0),
              reads=[g.ez.r, c.cst.r], writes=[g.L.r])

    def make_glr_ctx(c):
        g = Common()
        g.wl = kb.sb("g_wl", [128, 16, 16], BF16)
        load_w(c, g.wl, w_in, 0, 16, O_GLR, 16, 'wl')
        g.wg2b = kb.sb("g_wg2b", [16, 1024], BF16)
        S.dma('pool', 'wg2', lambda e: e.dma_start(out=g.wg2b.t[:], in_=gla_wg2), writes=[g.wg2b.r])
        g.bgb = kb.sb("g_bgb", [1, 1024], BF16)
        S.dma('pool', 'bg', lambda e: e.dma_start(out=g.bgb.t[:], in_=gla_bg), writes=[g.bgb.r])
        g.onesrow = kb.sb("g_onesrow", [1, 128], BF16)
        S.dve(lambda e: e.memset(g.onesrow.t[:], 1.0), writes=[g.onesrow.r])
        g.glrT = kb.sb("g_glrT", [16, 128], BF16)
        g.ez = kb.sb("g_ez", [128, 1024], F32)
        g.L = kb.sb("g_L", [128, 1024], F32)
        return g

    def phase_A1():
        c = setup_common('A1', npsf=6)
        n = make_norm_ctx(c)
        g = make_glr_ctx(c)
        g1 = load_cols(c, "g1col", norm1_g, 16)
        wk = kb.sb("a_wk", [128, 16, 1024], BF16)
        wv = kb.sb("a_wv", [128, 16, 2048], BF16)
        for q in range(2):
            src = w_in[:, O_GK + q * 512:O_GK + (q + 1) * 512].rearrange("(k p) c -> p k c", p=128)
            S.dma('pool', f'wk{q}', lambda e, q=q, src=src: e.dma_start(out=wk.t[:, :, q * 512:(q + 1) * 512], in_=src), writes=[wk.r])
        for q in range(4):
            src = w_in[:, O_GV + q * 512:O_GV + (q + 1) * 512].rearrange("(k p) c -> p k c", p=128)
            S.dma('pool', f'wv{q}', lambda e, q=q, src=src: e.dma_start(out=wv.t[:, :, q * 512:(q + 1) * 512], in_=src), writes=[wv.r])
        ustr = kb.sb("a_ustr", [128, 128], F32)
        S.dma('sp', 'ustr', lambda e: e.dma_start(out=ustr.t[:], in_=c_ustrict), writes=[ustr.r])
        onesc = kb.sb("a_onesc", [128, 1], F32)
        S.dve(lambda e: e.memset(onesc.t[:], 1.0), writes=[onesc.r])
        hT = [kb.sb(f"a_hT{i}", [128, 16, 128], BF16) for i in range(2)]
        state = kb.sb("a_state", [128, 8, 512], F32)
        S.dve(lambda e: e.memset(state.t[:], 0.0), writes=[state.r])
        stb = [kb.sb(f"a_stb{i}", [128, 8, 512], BF16) for i in range(2)]
        e1 = kb.sb("a_e1", [128, 1024], F32)
        khat = kb.sb("a_khat", [128, 1024], BF16)
        gvb = kb.sb("a_gvb", [128, 2048], BF16)
        dec = kb.sb("a_dec", [128, 8], F32)
        B = c.ps
        for tt in range(NTALL):
            h = hT[tt % 2]
            norm_T(c, n, xfull[tt * 128:(tt + 1) * 128, :], g1, lambda k0, k1, h=h: h.t[:, k0:k1, :], h.r)
            S.dma(STORE_ENG, f'hts{tt % 2}', lambda e, h=h, tt=tt: e.dma_start(out=HTS[tt], in_=h.t[:].rearrange("p a b -> p (a b)")),
                  reads=[h.r], writes=[DR('HTS')])
            for kc in range(16):
                S.pe(lambda e, kc=kc, h=h: e.matmul(out=B[2].t[0:16, 0:128], lhsT=g.wl.t[:, kc, :], rhs=h.t[:, kc, :], start=(kc == 0), stop=(kc == 15)),
                     reads=[g.wl.r, h.r], writes=[B[2].r])
            S.act(lambda e: e.activation(out=g.glrT.t[:], in_=B[2].t[0:16, 0:128], func=AF.Copy), reads=[B[2].r], writes=[g.glrT.r])
            for hf in range(2):
                for kc in range(16):
                    S.pe(lambda e, kc=kc, hf=hf, h=h: e.matmul(out=B[hf].t[:], lhsT=h.t[:, kc, :], rhs=wk.t[:, kc, hf * 512:(hf + 1) * 512],
                                                               start=(kc == 0), stop=(kc == 15)), reads=[h.r, wk.r], writes=[B[hf].r])
            for hf in range(2):
                psz = B[2 + hf]
                S.pe(lambda e, hf=hf, psz=psz: e.matmul(out=psz.t[:], lhsT=g.glrT.t[:], rhs=g.wg2b.t[:, hf * 512:(hf + 1) * 512], start=True, stop=False),
                     reads=[g.glrT.r, g.wg2b.r], writes=[psz.r])
                S.pe(lambda e, hf=hf, psz=psz: e.matmul(out=psz.t[:], lhsT=g.onesrow.t[:], rhs=g.bgb.t[:, hf * 512:(hf + 1) * 512], start=False, stop=True),
                     reads=[g.onesrow.r, g.bgb.r], writes=[psz.r])
                S.act(lambda e, hf=hf, psz=psz: e.activation(out=g.ez.t[:, hf * 512:(hf + 1) * 512], in_=psz.t[:], func=AF.Exp, scale=-1.0),
                      reads=[psz.r], writes=[g.ez.r])
            for q in range(4):
                ps = B[4 + q % 2]
                for kc in range(16):
                    S.pe(lambda e, kc=kc, q=q, ps=ps, h=h: e.matmul(out=ps.t[:], lhsT=h.t[:, kc, :], rhs=wv.t[:, kc, q * 512:(q + 1) * 512],
                                                                    start=(kc == 0), stop=(kc == 15)), reads=[h.r, wv.r], writes=[ps.r])
                if q % 2 == 0:
                    S.act(lambda e, q=q, ps=ps: e.activation(out=gvb.t[:, q * 512:(q + 1) * 512], in_=ps.t[:], func=AF.Copy), reads=[ps.r], writes=[gvb.r])
                else:
                    S.dve(lambda e, q=q, ps=ps: e.tensor_copy(out=gvb.t[:, q * 512:(q + 1) * 512], in_=ps.t[:]), reads=[ps.r], writes=[gvb.r])
            S.act(lambda e: e.activation(out=g.L.t[:], in_=g.ez.t[:], func=AF.Ln, bias=c.cst.t[:, 1:2], scale=1.0),
                  reads=[g.ez.r, c.cst.r], writes=[g.L.r])
            for hf in range(2):
                ps = B[2 + hf]
                S.pe(lambda e, hf=hf, ps=ps: e.matmul(out=ps.t[:], lhsT=ustr.t[:], rhs=g.L.t[:, hf * 512:(hf + 1) * 512], start=True, stop=True),
                     reads=[ustr.r, g.L.r], writes=[ps.r])
                S.act(lambda e, hf=hf, ps=ps: e.activation(out=e1.t[:, hf * 512:(hf + 1) * 512], in_=ps.t[:], func=AF.Exp, scale=-1.0 / 16),
                      reads=[ps.r], writes=[e1.r])
            psd = B[4]
            for cc in range(8):
                S.pe(lambda e, cc=cc: e.matmul(out=psd.t[:, cc:cc + 1], lhsT=g.L.t[:, cc * 128:(cc + 1) * 128], rhs=onesc.t[:], start=True, stop=True),
                     reads=[g.L.r, onesc.r], writes=[psd.r])
            S.act(lambda e: e.activation(out=dec.t[:], in_=psd.t[:, 0:8], func=AF.Exp, scale=-1.0 / 16), reads=[psd.r], writes=[dec.r])
            for hf in range(2):
                S.dve(lambda e, hf=hf: e.tensor_tensor(out=khat.t[:, hf * 512:(hf + 1) * 512], in0=B[hf].t[:], in1=e1.t[:, hf * 512:(hf + 1) * 512], op=ALU.mult),
                      reads=[B[hf].r, e1.r], writes=[khat.r])
            sb_ = stb[tt % 2]
            S.act(lambda e, sb_=sb_: e.activation(out=sb_.t[:, 0:4, :], in_=state.t[:, 0:4, :], func=AF.Copy), reads=[state.r], writes=[sb_.r])
            S.dve(lambda e, sb_=sb_: e.tensor_copy(out=sb_.t[:, 4:8, :], in_=state.t[:, 4:8, :]), reads=[state.r], writes=[sb_.r])
            S.dma(STORE_ENG, f'snap{tt % 2}', lambda e, sb_=sb_, tt=tt: e.dma_start(out=SNAP[tt], in_=sb_.t[:].rearrange("p a b -> p (a b)")),
                  reads=[sb_.r], writes=[DR('SNAP')])
            rot = [B[2], B[3], B[5], B[4]]
            for ix in range(8):
                hh = ix // 2
                ps = rot[ix % 4]
                S.pe(lambda e, ix=ix, hh=hh, ps=ps: e.matmul(out=ps.t[:], lhsT=khat.t[:, ix * 128:(ix + 1) * 128], rhs=gvb.t[:, hh * 512:(hh + 1) * 512],
                                                             start=True, stop=True), reads=[khat.r, gvb.r], writes=[ps.r])
                S.dve(lambda e, ix=ix, ps=ps: e.scalar_tensor_tensor(out=state.t[:, ix, :], in0=state.t[:, ix, :], scalar=dec.t[:, ix:ix + 1],
                                                                     in1=ps.t[:], op0=ALU.mult, op1=ALU.add),
                      reads=[state.r, dec.r, ps.r], writes=[state.r])

    def phase_A2():
        c = setup_common('A2', npsf=6)
        r = make_rope_ctx(c, 512)
        kng = load_cols(c, "kng", k_norm_g, 1)
        ikng = load_cols(c, "ikng", idx_k_norm_g, 1)
        pd = load_pmat(c, "pdsa", c_pdsaT)
        pi = load_pmat(c, "pidx", c_pidxT)
        wdk = kb.sb("b_wdk", [128, 16, 512], BF16)
        wdv = kb.sb("b_wdv", [128, 16, 512], BF16)
        wik = kb.sb("b_wik", [128, 16, 128], BF16)
        load_w(c, wdk, w_in, 0, 16, O_DK, 512, 'wdk')
        load_w(c, wdv, w_in, 0, 16, O_DV, 512, 'wdv')
        load_w(c, wik, w_in, 0, 16, O_IK, 128, 'wik')
        hT = [kb.sb(f"b_hT{i}", [128, 16, 512], BF16) for i in range(2)]
        tabs = [[kb.sb(f"b_tab{i}_{s}", [128, 512], F32) for i in range(4)] for s in range(2)]
        vst = [kb.sb(f"b_vst{i}", [128, 512], BF16) for i in range(2)]
        kst = [kb.sb(f"b_kst{i}", [128, 5, 512], BF16) for i in range(2)]
        B = c.ps
        vcnt = 0
        for st4 in range(NTALL // 4):
            h = hT[st4 % 2]
            tb = tabs[st4 % 2]
            for t in range(4):
                tt = st4 * 4 + t
                S.dma('sp', f'hld{st4 % 2}_{t}', lambda e, h=h, t=t, tt=tt: e.dma_start(
                    out=h.t[:, :, t * 128:(t + 1) * 128], in_=HTS[tt].rearrange("p (a b) -> p a b", b=128)), reads=[DR('HTS')], writes=[h.r])
            for i, src in enumerate((c_cosK, c_sinK, c_cosIK, c_sinIK)):
                S.dma('sp', f'tab{i}_{st4 % 2}', lambda e, i=i, src=src, tb=tb, st4=st4: e.dma_start(out=tb[i].t[:], in_=src[:, st4 * 512:(st4 + 1) * 512]),
                      writes=[tb[i].r])
            ks = kst[st4 % 2]

            def proj(gg, h=h):
                ps = B[gg % 2]
                wr = wdk.r if gg < 4 else wik.r
                for kc in range(16):
                    S.pe(lambda e, kc=kc, ps=ps, gg=gg: e.matmul(
                        out=ps.t[:, 0:512], lhsT=(wdk.t[:, kc, gg * 128:(gg + 1) * 128] if gg < 4 else wik.t[:, kc, :]),
                        rhs=h.t[:, kc, :], start=(kc == 0), stop=(kc == 15)), reads=[h.r, wr], writes=[ps.r])

            def epil(gg, tb=tb, ks=ks):
                ps = B[gg % 2]
                if gg < 4:
                    headnorm_rope(c, r, ps, 512, kng, tb[0], tb[1], lambda b: b.t[:], pd, ks.t[:, gg, :], ks.r, ps2=B[2], ps3=B[3])
                else:
                    headnorm_rope(c, r, ps, 512, ikng, tb[2], tb[3], lambda b: b.t[:], pi, ks.t[:, gg, :], ks.r, ps2=B[2], ps3=B[3])

            def vproj(t, h=h, st4=st4):
                nonlocal vcnt
                tt = st4 * 4 + t
                ps = B[4 + vcnt % 2]
                vs = vst[vcnt % 2]
                slot = vcnt % 2
                vcnt += 1
                for kc in range(16):
                    S.pe(lambda e, kc=kc, ps=ps, t=t: e.matmul(out=ps.t[:], lhsT=h.t[:, kc, t * 128:(t + 1) * 128], rhs=wdv.t[:, kc, :], start=(kc == 0), stop=(kc == 15)),
                         reads=[h.r, wdv.r], writes=[ps.r])
                S.act(lambda e, ps=ps, vs=vs: e.activation(out=vs.t[:], in_=ps.t[:], func=AF.Copy), reads=[ps.r], writes=[vs.r])
                S.dma(STORE_ENG, f'vst{slot}', lambda e, vs=vs, tt=tt: e.dma_start(out=VV[tt * 128:(tt + 1) * 128, :], in_=vs.t[:]),
                      reads=[vs.r], writes=[DR('VV')])

            proj(0)
            for gg in range(5):
                if gg + 1 < 5:
                    proj(gg + 1)
                if gg < 4:
                    vproj(gg)
                epil(gg)
            S.dma(STORE_ENG, f'kst{st4 % 2}', lambda e, ks=ks, st4=st4: e.dma_start(
                out=KT[:, :, st4 * 512:(st4 + 1) * 512].rearrange("g p t -> p g t"), in_=ks.t[:, 0:4, :]), reads=[ks.r], writes=[DR('KT')])
            S.dma(STORE_ENG, f'ikst{st4 % 2}', lambda e, ks=ks, st4=st4: e.dma_start(out=IKT[:, st4 * 512:(st4 + 1) * 512], in_=ks.t[:, 4, :]),
                  reads=[ks.r], writes=[DR('IKT')])


    def phase_B():
        c = setup_common('B', npsf=6)
        make_wslots(c, 2)
        n = make_norm_ctx(c, 1)
        g = make_glr_ctx(c)
        r = make_rope_ctx(c, 512)
        g1 = load_cols(c, "g1col", norm1_g, 16)
        qng = load_cols(c, "qng", q_norm_g, 1)
        bgc = load_cols(c, "bgc", b_gate, 32)
        pd = load_pmat(c, "pdsa", c_pdsaT)
        pi = load_pmat(c, "pidx", c_pidxT)
        uincl = kb.sb("c_uincl", [128, 128], F32)
        S.dma('sp', 'uincl', lambda e: e.dma_start(out=uincl.t[:], in_=c_uincl), writes=[uincl.r])
        hT = kb.sb("hTown", [128, 16, TOWN], BF16)
        eqs = [kb.sb(f"eqs{i}", [128, 8, 128], BF16) for i in range(2)]
        eks = [kb.sb(f"eks{i}", [128, 8, 128], BF16) for i in range(2)]
        for i in range(NOWN):
            norm_T(c, n, xown[i * 128:(i + 1) * 128, :], g1, lambda k0, k1, i=i: hT.t[:, k0:k1, i * 128:(i + 1) * 128], hT.r)
        for i in range(NOWN):
            glr_logsig(c, g, lambda kc, i=i: hT.t[:, kc, i * 128:(i + 1) * 128], hT.r)
            for hf in range(2):
                ps = next_ps(c)
                for q in range(4):
                    cc = hf * 4 + q
                    S.pe(lambda e, cc=cc, q=q, ps=ps: e.matmul(out=ps.t[:, q * 128:(q + 1) * 128], lhsT=g.L.t[:, cc * 128:(cc + 1) * 128], rhs=uincl.t[:],
                                                               start=True, stop=True), reads=[g.L.r, uincl.r], writes=[ps.r])
                eq, ek = eqs[i % 2], eks[i % 2]
                S.act(lambda e, hf=hf, ps=ps, eq=eq: e.activation(out=eq.t[:, hf * 4:(hf + 1) * 4, :],
                                                                in_=ps.t[:].rearrange("p (a b) -> p a b", b=128), func=AF.Exp,
                                                                scale=-1.0 / 16, bias=c.cst.t[:, 2:3]), reads=[ps.r, c.cst.r], writes=[eq.r])
                S.act(lambda e, hf=hf, ps=ps, ek=ek: e.activation(out=ek.t[:, hf * 4:(hf + 1) * 4, :],
                                                                in_=ps.t[:].rearrange("p (a b) -> p a b", b=128), func=AF.Exp,
                                                                scale=1.0 / 16), reads=[ps.r], writes=[ek.r])
            S.dma(STORE_ENG, f'eqs{i % 2}', lambda e, eq=eq, i=i: e.dma_start(out=EQT[:, i * 128:(i + 1) * 128].rearrange("(a p) t -> p a t", p=128), in_=eq.t[:]),
                  reads=[eq.r], writes=[DR('EQT')])
            S.dma(STORE_ENG, f'eks{i % 2}', lambda e, ek=ek, i=i: e.dma_start(out=EKT[:, i * 128:(i + 1) * 128].rearrange("(a p) t -> p a t", p=128), in_=ek.t[:]),
                  reads=[ek.r], writes=[DR('EKT')])
        act_fn = lambda kc, t0, tn: hT.t[:, kc, t0:t0 + tn]
        stg = [kb.sb(f"stg{i}", [128, 512], BF16) for i in range(3)]
        stc = [0]

        def stage():
            b = stg[stc[0] % 3]
            k = f"stg{stc[0] % 3}"
            stc[0] += 1
            return b, k

        def store_fm(dst, drn, cb, t0, tn, st, key, m=128):
            S.dma(STORE_ENG, key, lambda e: e.dma_start(out=dst[cb * 128:cb * 128 + m, t0:t0 + tn], in_=st.t[0:m, 0:tn]),
                  reads=[st.r], writes=[DR(drn)])

        esl = [kb.sb(f"esl{i}", [128, 512], BF16) for i in range(3)]
        ecnt = [0]

        def epi_mul(E, En, dst, drn):
            def epi(cb, t0, tn, ps, m):
                st, key = stage()
                sl = ecnt[0] % 3
                ecnt[0] += 1
                eb_ = esl[sl]
                S.dma('sp', f'esl{sl}', lambda e: e.dma_start(out=eb_.t[:, 0:tn], in_=E[cb * 128:(cb + 1) * 128, t0:t0 + tn]),
                      reads=[DR(En)], writes=[eb_.r])
                S.dve(lambda e: e.tensor_tensor(out=st.t[:, 0:tn], in0=ps.t[:, 0:tn], in1=eb_.t[:, 0:tn], op=ALU.mult),
                      reads=[ps.r, eb_.r], writes=[st.r])
                store_fm(dst, drn, cb, t0, tn, st, key)
            return epi

        gemm_fm(c, w_in, O_GQ, 1024, act_fn, [hT.r], TOWN, epi_mul(EQT, 'EQT', QT, 'QT'))
        gemm_fm(c, w_in, O_GK, 1024, act_fn, [hT.r], TOWN, epi_mul(EKT, 'EKT', KGT, 'KGT'))

        def epi_silu(cb, t0, tn, ps, m):
            st, key = stage()
            S.act(lambda e: e.activation(out=st.t[:, 0:tn], in_=ps.t[:, 0:tn], func=AF.Silu), reads=[ps.r], writes=[st.r])
            store_fm(GRT, 'GRT', cb, t0, tn, st, key)
        gemm_fm(c, w_in, O_GR, 2048, act_fn, [hT.r], TOWN, epi_silu)

        def epi_gate(cb, t0, tn, ps, m):
            st, key = stage()
            S.act(lambda e: e.activation(out=st.t[:, 0:tn], in_=ps.t[:, 0:tn], func=AF.Sigmoid, bias=bgc.t[:, cb:cb + 1], scale=1.0),
                  reads=[ps.r, bgc.r], writes=[st.r])
            store_fm(GT, 'GT', cb, t0, tn, st, key)
        gemm_fm(c, w_in, O_GATES, 4096, act_fn, [hT.r], TOWN, epi_gate)

        tabq = [[kb.sb(f"tabq{i}_{s}", [128, 512], F32) for i in range(2)] for s in range(2)]
        tcnt = [0]

        def epi_rope(gain, csrc, ssrc, pm, dst, drn):
            def epi(cb, t0, tn, ps, m):
                st, key = stage()
                sl = tcnt[0] % 2
                tcnt[0] += 1
                tb = tabq[sl]
                S.dma('sp', f'tq0_{sl}', lambda e: e.dma_start(out=tb[0].t[:, 0:tn], in_=csrc[:, t0:t0 + tn]), writes=[tb[0].r])
                S.dma('sp', f'tq1_{sl}', lambda e: e.dma_start(out=tb[1].t[:, 0:tn], in_=ssrc[:, t0:t0 + tn]), writes=[tb[1].r])
                headnorm_rope(c, r, ps, tn, gain, tb[0], tb[1], lambda b: b.t[:, 0:tn], pm, st.t[:, 0:tn], st.r)
                store_fm(dst, drn, cb, t0, tn, st, key)
            return epi
        gemm_fm(c, w_in, O_DQ, 2048, act_fn, [hT.r], TOWN, epi_rope(qng, c_cosQ, c_sinQ, pd, DQT, 'DQT'))
        gemm_fm(c, w_in, O_IQ, 2048, act_fn, [hT.r], TOWN, epi_rope(None, c_cosIQ, c_sinIQ, pi, IQT, 'IQT'))

        act_tm = lambda kc, tt: hT.t[:, kc, tt * 128:(tt + 1) * 128]

        def epi_gv(tt, g0, gn, ps):
            st, key = stage()
            S.act(lambda e: e.activation(out=st.t[:, 0:gn], in_=ps.t[:, 0:gn], func=AF.Copy), reads=[ps.r], writes=[st.r])
            S.dma(STORE_ENG, key, lambda e: e.dma_start(out=VG[tt * 128:(tt + 1) * 128, g0:g0 + gn], in_=st.t[:, 0:gn]),
                  reads=[st.r], writes=[DR('VG')])
        for t4 in range(0, NOWN, 4):
            gemm_tm(c, w_in, O_GV, 2048, 16, lambda kc, tt, t4=t4: act_tm(kc, t4 + tt), [hT.r], 4,
                    lambda tt, g0, gn, ps, t4=t4: epi_gv(t4 + tt, g0, gn, ps))
        iwst = kb.sb("iwst", [128, NOWN, 16], F32)

        def epi_iw(tt, g0, gn, ps):
            S.act(lambda e: e.activation(out=iwst.t[:, tt, :], in_=ps.t[:, 0:16], func=AF.Copy, scale=float(16 ** -0.5 * 128 ** -0.5)),
                  reads=[ps.r], writes=[iwst.r])
        for t4 in range(0, NOWN, 4):
            gemm_tm(c, w_in, O_IW, 16, 16, lambda kc, tt, t4=t4: act_tm(kc, t4 + tt), [hT.r], 4,
                    lambda tt, g0, gn, ps, t4=t4: epi_iw(t4 + tt, g0, gn, ps))
        S.dma(STORE_ENG, 'iws', lambda e: e.dma_start(out=IWS.rearrange("(a p) h -> p a h", p=128), in_=iwst.t[:]),
              reads=[iwst.r], writes=[DR('IWS')])

    def phase_C():
        c = setup_common('C', npsf=7)
        gng = load_cols(c, "gng", gla_norm_g, 4)
        flg = kb.sb("flg", [128, 4], F32)
        S.dma('sp', 'flg', lambda e: e.dma_start(out=flg.t[:], in_=c_flags), writes=[flg.r])
        uf = kb.sb("uf", [128, 128], F32)
        ub = kb.sb("ub", [128, 128], BF16)
        S.dma('sp', 'uincl', lambda e: e.dma_start(out=uf.t[:], in_=c_uincl), writes=[uf.r])
        S.dve(lambda e: e.tensor_copy(out=ub.t[:], in_=uf.t[:]), reads=[uf.r], writes=[ub.r])
        NS = 2
        qT = [kb.sb(f"c_qT{i}", [128, 2, 128], BF16) for i in range(NS)]
        kT = [kb.sb(f"c_kT{i}", [128, 2, 128], BF16) for i in range(NS)]
        v = [kb.sb(f"c_v{i}", [128, 512], BF16) for i in range(NS)]
        cand = [kb.sb(f"c_cand{i}", [128, 4, 1024], BF16) for i in range(NS)]
        snf = kb.sb("c_snf", [128, 1024], F32)
        snb = kb.sb("c_snb", [128, 1024], BF16)
        aT = kb.sb("c_aT", [128, 128], BF16)
        sq = kb.sb("c_sq", [128, 512], BF16)
        rstd = kb.sb("c_rstd", [128, 128], F32)
        grt = [kb.sb(f"c_grt{i}", [128, 4, 128], BF16) for i in range(NS)]
        og = kb.sb("c_og", [128, 4, 128], F32)
        ogb = [kb.sb(f"c_ogb{i}", [128, 4, 128], BF16) for i in range(NS)]
        iters = [(i, hh) for i in range(NOWN) for hh in range(4)]
        ctxs = {}

        def stage1(it):
            i, hh = iters[it]
            s = it % NS
            q_, k_, v_, cd, gr_ = qT[s], kT[s], v[s], cand[s], grt[s]
            S.dma('sp', f'cq{s}', lambda e: e.dma_start(
                out=q_.t[:], in_=QT[hh * 256:(hh + 1) * 256, i * 128:(i + 1) * 128].rearrange("(a p) t -> p a t", p=128)),
                reads=[DR('QT')], writes=[q_.r])
            S.dma('sp', f'ck{s}', lambda e: e.dma_start(
                out=k_.t[:], in_=KGT[hh * 256:(hh + 1) * 256, i * 128:(i + 1) * 128].rearrange("(a p) t -> p a t", p=128)),
                reads=[DR('KGT')], writes=[k_.r])
            S.dma('sp', f'cv{s}', lambda e: e.dma_start(out=v_.t[:], in_=VG[i * 128:(i + 1) * 128, hh * 512:(hh + 1) * 512]),
                  reads=[DR('VG')], writes=[v_.r])
            S.dma('sp', f'cc{s}', lambda e: e.dma_start(
                out=cd.t[:], in_=SNAP[4 * i:4 * i + 4, :, hh * 1024:(hh + 1) * 1024].rearrange("c p f -> p c f")),
                reads=[DR('SNAP')], writes=[cd.r])
            S.dma('sp', f'cg{s}', lambda e: e.dma_start(
                out=gr_.t[:], in_=GRT[hh * 512:(hh + 1) * 512, i * 128:(i + 1) * 128].rearrange("(a p) t -> p a t", p=128)),
                reads=[DR('GRT')], writes=[gr_.r])
            S.dve(lambda e: e.tensor_scalar(out=snf.t[:], in0=cd.t[:, 0, :], scalar1=flg.t[:, 0:1], scalar2=None, op0=ALU.mult),
                  reads=[cd.r, flg.r], writes=[snf.r])
            for cc in range(1, 4):
                dst = snf if cc < 3 else snb
                S.dve(lambda e, cc=cc, dst=dst: e.scalar_tensor_tensor(out=dst.t[:], in0=cd.t[:, cc, :], scalar=flg.t[:, cc:cc + 1],
                                                                       in1=snf.t[:], op0=ALU.mult, op1=ALU.add),
                      reads=[cd.r, flg.r, snf.r], writes=[dst.r])
            psa = next_ps(c)
            for kh in range(2):
                S.pe(lambda e, kh=kh: e.matmul(out=psa.t[:, 0:128], lhsT=k_.t[:, kh, :], rhs=q_.t[:, kh, :], start=(kh == 0), stop=(kh == 1)),
                     reads=[q_.r, k_.r], writes=[psa.r])
            S.dve(lambda e: e.tensor_tensor(out=aT.t[:], in0=psa.t[:, 0:128], in1=ub.t[:], op=ALU.mult), reads=[psa.r, ub.r], writes=[aT.r])
            pso = next_ps(c)
            for ec in range(4):
                for kh in range(2):
                    S.pe(lambda e, ec=ec, kh=kh: e.matmul(out=pso.t[:, ec * 128:(ec + 1) * 128], lhsT=snb.t[:, kh * 512 + ec * 128:kh * 512 + (ec + 1) * 128],
                                                          rhs=q_.t[:, kh, :], start=(kh == 0), stop=False), reads=[snb.r, q_.r], writes=[pso.r])
                S.pe(lambda e, ec=ec: e.matmul(out=pso.t[:, ec * 128:(ec + 1) * 128], lhsT=v_.t[:, ec * 128:(ec + 1) * 128], rhs=aT.t[:],
                                               start=False, stop=True), reads=[v_.r, aT.r], writes=[pso.r])
            ctxs[it] = pso

        def stage2(it):
            i, hh = iters[it]
            s = it % NS
            gr_, ob = grt[s], ogb[s]
            pso = ctxs.pop(it)
            S.act(lambda e: e.activation(out=sq.t[:], in_=pso.t[:], func=AF.Square), reads=[pso.r], writes=[sq.r])
            pss = next_ps(c)
            for ec in range(4):
                S.pe(lambda e, ec=ec: e.matmul(out=pss.t[:, 0:128], lhsT=c.onesb.t[:], rhs=sq.t[:, ec * 128:(ec + 1) * 128], start=(ec == 0), stop=(ec == 3)),
                     reads=[c.onesb.r, sq.r], writes=[pss.r])
            S.act(lambda e: e.activation(out=rstd.t[:], in_=pss.t[:, 0:128], func=AF.Sqrt, scale=1.0 / 512, bias=c.cst.t[:, 0:1]),
                  reads=[pss.r, c.cst.r], writes=[rstd.r])
            S.dve(lambda e: e.reciprocal(out=rstd.t[:], in_=rstd.t[:]), reads=[rstd.r], writes=[rstd.r])
            for ec in range(4):
                S.dve(lambda e, ec=ec: e.scalar_tensor_tensor(out=og.t[:, ec, :], in0=pso.t[:, ec * 128:(ec + 1) * 128], scalar=gng.t[:, ec:ec + 1],
                                                              in1=rstd.t[:], op0=ALU.mult, op1=ALU.mult), reads=[pso.r, gng.r, rstd.r], writes=[og.r])
            S.dve(lambda e: e.tensor_tensor(out=ob.t[:], in0=og.t[:], in1=gr_.t[:], op=ALU.mult), reads=[og.r, gr_.r], writes=[ob.r])
            S.dma(STORE_ENG, f'cog{s}', lambda e: e.dma_start(
                out=OGT[hh * 512:(hh + 1) * 512, i * 128:(i + 1) * 128].rearrange("(a p) t -> p a t", p=128), in_=ob.t[:]),
                reads=[ob.r], writes=[DR('OGT')])

        stage1(0)
        for it in range(len(iters)):
            if it + 1 < len(iters):
                stage1(it + 1)
            stage2(it)

    def phase_proj(src, srcn, W, gate_off, addsrc, dst, dstn, tag):
        c = setup_common(tag, npsf=6)
        make_wslots(c, 3)
        aT = kb.sb("p_aT", [128, 16, TOWN], BF16)
        for q in range(4):
            S.dma('sp', f'pa{q}', lambda e, q=q: e.dma_start(out=aT.t[:, q * 4:(q + 1) * 4, :],
                                                            in_=src[q * 512:(q + 1) * 512, :].rearrange("(a p) t -> p a t", p=128)),
                  reads=[DR(srcn)], writes=[aT.r])
        gt = [kb.sb(f"p_gt{i}", [128, 512], BF16) for i in range(3)]
        ad = [kb.sb(f"p_ad{i}", [128, 512], BF16) for i in range(3)]
        tmp = [kb.sb(f"p_tmp{i}", [128, 512], F32) for i in range(2)]
        st = [kb.sb(f"p_st{i}", [128, 512], BF16) for i in range(3)]
        cnt = [0]

        def epi(cb, t0, tn, ps, m):
            s = cnt[0] % 3
            cnt[0] += 1
            g_, a_, o_ = gt[s], ad[s], st[s]
            S.dma('sp', f'pg{s}', lambda e: e.dma_start(out=g_.t[:, 0:tn], in_=GT[gate_off + cb * 128:gate_off + (cb + 1) * 128, t0:t0 + tn]),
                  reads=[DR('GT')], writes=[g_.r])
            if addsrc is None:
                S.dve(lambda e: e.tensor_tensor(out=o_.t[:, 0:tn], in0=ps.t[:, 0:tn], in1=g_.t[:, 0:tn], op=ALU.mult),
                      reads=[ps.r, g_.r], writes=[o_.r])
            else:
                t_ = tmp[cnt[0] % 2]
                S.dma('sp', f'pd{s}', lambda e: e.dma_start(out=a_.t[:, 0:tn], in_=addsrc[0][cb * 128:(cb + 1) * 128, t0:t0 + tn]),
                      reads=[DR(addsrc[1])], writes=[a_.r])
                S.dve(lambda e: e.tensor_tensor(out=t_.t[:, 0:tn], in0=ps.t[:, 0:tn], in1=g_.t[:, 0:tn], op=ALU.mult),
                      reads=[ps.r, g_.r], writes=[t_.r])
                S.dve(lambda e: e.tensor_tensor(out=o_.t[:, 0:tn], in0=t_.t[:, 0:tn], in1=a_.t[:, 0:tn], op=ALU.add),
                      reads=[t_.r, a_.r], writes=[o_.r])
            S.dma(STORE_ENG, f'po{s}', lambda e: e.dma_start(out=dst[cb * 128:(cb + 1) * 128, t0:t0 + tn], in_=o_.t[:, 0:tn]),
                  reads=[o_.r], writes=[DR(dstn)])
        gemm_fm(c, W, 0, 2048, lambda kc, t0, tn: aT.t[:, kc, t0:t0 + tn], [aT.r], TOWN, epi)

    def phase_E():
        c = setup_common('E', npsf=7)
        ikt = kb.sb("e_ikt", [128, SEQ], BF16)
        for q in range(4):
            S.dma('sp', f'eik{q}', lambda e, q=q: e.dma_start(out=ikt.t[:, q * 2048:(q + 1) * 2048], in_=IKT[:, q * 2048:(q + 1) * 2048]),
                  reads=[DR('IKT')], writes=[ikt.r])
        mskd = kb.sb("e_mskd", [128, 512], F32)
        S.dma('sp', 'emk', lambda e: e.dma_start(out=mskd.t[:], in_=c_maskd), writes=[mskd.r])
        pw2 = kb.sb("e_pw2", [128, NIT + 2], F32)
        for k in range(NIT + 2):
            S.dve(lambda e, k=k: e.memset(pw2.t[:, k:k + 1], float(2.0 ** -(k + 1))), writes=[pw2.r])
        iq = [kb.sb(f"e_iq{i}", [128, 16, 128], BF16) for i in range(2)]
        iw = [kb.sb(f"e_iw{i}", [128, 16], F32) for i in range(2)]
        wab = [kb.sb(f"e_wab{i}", [128, 16], F32) for i in range(2)]
        wsg = [kb.sb(f"e_wsg{i}", [128, 16], F32) for i in range(2)]
        dg = [kb.sb(f"e_dg{i}", [128, 16, 128], BF16) for i in range(2)]
        rb = [kb.sb(f"e_rb{i}", [128, 512], BF16) for i in range(4)]
        Sb = [kb.sb(f"e_S{i}", [128, SEQ], F32) for i in range(2)]
        junk = kb.sb("e_junk", [128, SEQ], BF16)
        Mb = kb.sb("e_M", [128, SEQ], BF16)
        MT_ = [kb.sb(f"e_MT{i}", [128, NTALL, 128], BF16) for i in range(2)]
        bsl = [kb.sb(f"e_bs{i}", [128, 8], F32) for i in range(2)]
        skl = [kb.sb(f"e_sk{i}", [128, NIT + 2], F32) for i in range(2)]
        midl = [kb.sb(f"e_mid{i}", [128, NIT + 2], F32) for i in range(2)]
        cntl = [kb.sb(f"e_cnt{i}", [128, NIT + 2], F32) for i in range(2)]
        gl = [kb.sb(f"e_g{i}", [128, NIT + 2], F32) for i in range(2)]
        dq = [kb.sb(f"e_dq{i}", [128, 512], BF16) for i in range(2)]
        ktc = [kb.sb(f"e_ktc{i}", [128, 512], BF16) for i in range(3)]
        vc = [kb.sb(f"e_vc{i}", [128, 4, 128], BF16) for i in range(3)]
        eb = [kb.sb(f"e_eb{i}", [128, 512], BF16) for i in range(4)]
        rden = kb.sb("e_rden", [128, 512], F32)
        od = [kb.sb(f"e_od{i}", [128, 512], BF16) for i in range(2)]
        psS = c.ps[6]
        psO = c.ps[5]
        psD = c.ps[4]
        lg = c.ps[0:4]
        pbT = c.pb[0]
        hsc = float(128 ** -0.5)
        BIGM = 30000.0
        mbias = kb.sb("e_mbias", [128, 1], F32)
        S.dve(lambda e: e.memset(mbias.t[:], -BIGM), writes=[mbias.r])
        cnts = {'rb': 0, 'eb': 0, 'kv': 0, 'dq': 0, 'od': 0}

        def idx_steps(i):
            steps = []
            iq_, iw_, wab_, wsg_, dg_, Sb_ = iq[i % 2], iw[i % 2], wab[i % 2], wsg[i % 2], dg[i % 2], Sb[i % 2]

            def prep(bank):
                S.dma('sp', f'eiq{i % 2}', lambda e: e.dma_start(out=iq_.t[:], in_=IQT[:, i * 128:(i + 1) * 128].rearrange("(h p) t -> p h t", p=128)),
                      reads=[DR('IQT')], writes=[iq_.r])
                S.dma('sp', f'eiw{i % 2}', lambda e: e.dma_start(out=iw_.t[:], in_=IWS[i * 128:(i + 1) * 128, :]), reads=[DR('IWS')], writes=[iw_.r])
                S.act(lambda e: e.activation(out=wab_.t[:], in_=iw_.t[:], func=AF.Abs), reads=[iw_.r], writes=[wab_.r])
                S.act(lambda e: e.activation(out=wsg_.t[:], in_=iw_.t[:], func=AF.Sign), reads=[iw_.r], writes=[wsg_.r])
                for h in range(16):
                    S.pool(lambda e, h=h: e.tensor_scalar(out=dg_.t[:, h, :], in0=c.identb.t[:], scalar1=wsg_.t[:, h:h + 1], scalar2=None, op0=ALU.mult),
                           reads=[c.identb.r, wsg_.r], writes=[dg_.r])
            steps.append((prep, None, False))
            for k4 in range(i + 1):
                for h in range(16):
                    def A(bank, h=h, k4=k4):
                        S.pe(lambda e: e.matmul(out=bank.t[:], lhsT=iq_.t[:, h, :], rhs=ikt.t[:, k4 * 512:(k4 + 1) * 512], start=True, stop=True),
                             reads=[iq_.r, ikt.r], writes=[bank.r])

                    def BC(bank, h=h, k4=k4):
                        r_ = rb[cnts['rb'] % 4]
                        cnts['rb'] += 1
                        if h % 4 == 3:
                            S.dve(lambda e: e.tensor_scalar(out=r_.t[:], in0=bank.t[:], scalar1=wab_.t[:, h:h + 1], scalar2=0.0, op0=ALU.mult, op1=ALU.max),
                                  reads=[bank.r, wab_.r], writes=[r_.r])
                        else:
                            S.act(lambda e: e.activation(out=r_.t[:], in_=bank.t[:], func=AF.Relu, scale=wab_.t[:, h:h + 1]),
                                  reads=[bank.r, wab_.r], writes=[r_.r])
                        S.pe(lambda e: e.matmul(out=psS.t[:], lhsT=dg_.t[:, h, :], rhs=r_.t[:], start=(h == 0), stop=(h == 15)),
                             reads=[dg_.r, r_.r], writes=[psS.r])
                        if h == 15:
                            S.act(lambda e: e.activation(out=Sb_.t[:, k4 * 512:(k4 + 1) * 512], in_=psS.t[:], func=AF.Copy), reads=[psS.r], writes=[Sb_.r])
                    steps.append((A, BC))
            return steps

        def bis_steps(i):
            steps = []
            nk = 512 * (i + 1)
            nkb = 4 * (i + 1)
            Sb_, bs, sk, mid, cnt, g_, MTo = Sb[i % 2], bsl[i % 2], skl[i % 2], midl[i % 2], cntl[i % 2], gl[i % 2], MT_[i % 2]

            def init(bank):
                S.dve(lambda e: e.tensor_reduce(out=bs.t[:, 0:1], in_=Sb_.t[:, 0:nk], axis=AX.X, op=ALU.min), reads=[Sb_.r], writes=[bs.r])
                S.dve(lambda e: e.tensor_reduce(out=bs.t[:, 1:2], in_=Sb_.t[:, 0:nk], axis=AX.X, op=ALU.max), reads=[Sb_.r], writes=[bs.r])
                S.dve(lambda e: e.tensor_tensor(out=Sb_.t[:, nk - 512:nk], in0=Sb_.t[:, nk - 512:nk], in1=mskd.t[:], op=ALU.add),
                      reads=[Sb_.r, mskd.r], writes=[Sb_.r])
                S.dve(lambda e: e.tensor_scalar(out=bs.t[:, 0:1], in0=bs.t[:, 0:1], scalar1=-1.0, scalar2=None, op0=ALU.add), reads=[bs.r], writes=[bs.r])
                S.dve(lambda e: e.tensor_tensor(out=bs.t[:, 2:3], in0=bs.t[:, 1:2], in1=bs.t[:, 0:1], op=ALU.subtract), reads=[bs.r], writes=[bs.r])
                S.dve(lambda e: e.tensor_scalar(out=sk.t[:], in0=pw2.t[:], scalar1=bs.t[:, 2:3], scalar2=None, op0=ALU.mult), reads=[pw2.r, bs.r], writes=[sk.r])
                S.dve(lambda e: e.tensor_tensor(out=mid.t[:, 0:1], in0=bs.t[:, 0:1], in1=sk.t[:, 0:1], op=ALU.add), reads=[bs.r, sk.r], writes=[mid.r])
            steps.append((None, init))
            for k in range(NIT):
                def it(bank, k=k):
                    S.dve(lambda e: e.tensor_scalar(out=junk.t[:, 0:nk], in0=Sb_.t[:, 0:nk], scalar1=mid.t[:, k:k + 1], scalar2=0.0, op0=ALU.is_ge, op1=ALU.add,
                                                    accum_out=cnt.t[:, k:k + 1]), reads=[Sb_.r, mid.r], writes=[junk.r, cnt.r])
                    S.dve(lambda e: e.tensor_scalar(out=g_.t[:, k:k + 1], in0=cnt.t[:, k:k + 1], scalar1=float(TOPK), scalar2=sk.t[:, k:k + 1],
                                                    op0=ALU.is_ge, op1=ALU.mult), reads=[cnt.r, sk.r], writes=[g_.r])
                    kk = k + 1 if k < NIT - 1 else k
                    S.dve(lambda e: e.scalar_tensor_tensor(out=mid.t[:, k + 1:k + 2], in0=mid.t[:, k:k + 1], scalar=sk.t[:, kk:kk + 1], in1=g_.t[:, k:k + 1],
                                                           op0=ALU.subtract, op1=ALU.add), reads=[mid.r, sk.r, g_.r], writes=[mid.r])
                steps.append((None, it))

            def mk(bank):
                S.dve(lambda e: e.tensor_scalar(out=Mb.t[:, 0:nk], in0=Sb_.t[:, 0:nk], scalar1=mid.t[:, NIT:NIT + 1], scalar2=None, op0=ALU.is_ge),
                      reads=[Sb_.r, mid.r], writes=[Mb.r])
            steps.append((None, mk))
            for k8 in range(0, nkb, 8):
                def tr(bank, k8=k8):
                    nn = min(8, nkb - k8)
                    for k in range(nn):
                        S.pe(lambda e, k=k: e.transpose(out=pbT.t[:, k * 128:(k + 1) * 128], in_=Mb.t[:, (k8 + k) * 128:(k8 + k + 1) * 128], identity=c.identb.t[:]),
                             reads=[Mb.r, c.identb.r], writes=[pbT.r])
                    S.act(lambda e: e.activation(out=MTo.t[:, k8:k8 + nn, :], in_=pbT.t[:, 0:nn * 128].rearrange("p (a b) -> p a b", b=128), func=AF.Identity,
                                                 scale=BIGM, bias=mbias.t[:, 0:1]), reads=[pbT.r, mbias.r], writes=[MTo.r])
                steps.append((None, tr))
            return steps

        def attn_steps(i):
            steps = []
            nkb = 4 * (i + 1)
            MTi = MT_[i % 2]
            for gq in range(4):
                st = {}
                for kbi in range(nkb):
                    k4, sub = kbi // 4, kbi % 4

                    def A(bank, gq=gq, kbi=kbi, k4=k4, sub=sub, st=st):
                        if kbi == 0:
                            st['dq'] = dq[cnts['dq'] % 2]
                            dslot = cnts['dq'] % 2
                            cnts['dq'] += 1
                            dq_ = st['dq']
                            S.dma('sp', f'edq{dslot}', lambda e: e.dma_start(
                                out=dq_.t[:].rearrange("p (h t) -> p h t", t=128), in_=DQT[gq * 512:(gq + 1) * 512, i * 128:(i + 1) * 128].rearrange("(h p) t -> p h t", p=128)),
                                reads=[DR('DQT')], writes=[dq_.r])
                        if sub == 0:
                            s3 = cnts['kv'] % 3
                            cnts['kv'] += 1
                            st[('kt', k4)] = ktc[s3]
                            st[('v', k4)] = vc[s3]
                            kt_, v_ = ktc[s3], vc[s3]
                            S.dma('sp', f'ekt{s3}', lambda e: e.dma_start(out=kt_.t[:], in_=KT[gq, :, k4 * 512:(k4 + 1) * 512]), reads=[DR('KT')], writes=[kt_.r])
                            S.dma('sp', f'evc{s3}', lambda e: e.dma_start(
                                out=v_.t[:], in_=VV[k4 * 512:(k4 + 1) * 512, gq * 128:(gq + 1) * 128].rearrange("(a p) d -> p a d", p=128)),
                                reads=[DR('VV')], writes=[v_.r])
                        kt_, dq_ = st[('kt', k4)], st['dq']
                        S.pe(lambda e: e.matmul(out=bank.t[:], lhsT=kt_.t[:, sub * 128:(sub + 1) * 128], rhs=dq_.t[:], start=True, stop=False),
                             reads=[kt_.r, dq_.r], writes=[bank.r])
                        for hq in range(4):
                            S.pe(lambda e, hq=hq: e.matmul(out=bank.t[:, hq * 128:(hq + 1) * 128], lhsT=c.identb.t[:], rhs=MTi.t[:, kbi, :], start=False, stop=(hq == 3)),
                                 reads=[c.identb.r, MTi.r], writes=[bank.r])

                    def BC(bank, gq=gq, kbi=kbi, k4=k4, sub=sub, st=st):
                        e_ = eb[cnts['eb'] % 4]
                        cnts['eb'] += 1
                        v_ = st[('v', k4)]
                        S.act(lambda e: e.activation(out=e_.t[:], in_=bank.t[:], func=AF.Exp, scale=hsc), reads=[bank.r], writes=[e_.r])
                        S.pe(lambda e: e.matmul(out=psO.t[:], lhsT=v_.t[:, sub, :], rhs=e_.t[:], start=(kbi == 0), stop=(kbi == nkb - 1)),
                             reads=[v_.r, e_.r], writes=[psO.r])
                        S.pe(lambda e: e.matmul(out=psD.t[:], lhsT=c.onesb.t[:], rhs=e_.t[:], start=(kbi == 0), stop=(kbi == nkb - 1)),
                             reads=[c.onesb.r, e_.r], writes=[psD.r])
                        if kbi == nkb - 1:
                            od_ = od[cnts['od'] % 2]
                            oslot = cnts['od'] % 2
                            cnts['od'] += 1
                            S.dve(lambda e: e.reciprocal(out=rden.t[:], in_=psD.t[:]), reads=[psD.r], writes=[rden.r])
                            S.dve(lambda e: e.tensor_tensor(out=od_.t[:], in0=psO.t[:], in1=rden.t[:], op=ALU.mult), reads=[psO.r, rden.r], writes=[od_.r])
                            S.dma(STORE_ENG, f'eod{oslot}', lambda e: e.dma_start(
                                out=ODT[gq * 512:(gq + 1) * 512, i * 128:(i + 1) * 128].rearrange("(h p) t -> p h t", p=128),
                                in_=od_.t[:].rearrange("p (h t) -> p h t", t=128)), reads=[od_.r], writes=[DR('ODT')])
                    steps.append((A, BC))
            return steps

        def merge(lists):
            lists = [l for l in lists if l]
            pos = [0] * len(lists)
            outl = []
            while True:
                best, bf_ = None, None
                for li, l in enumerate(lists):
                    if pos[li] < len(l):
                        f = (pos[li] + 1) / len(l)
                        if best is None or f < bf_:
                            best, bf_ = li, f
                if best is None:
                    break
                outl.append(lists[best][pos[best]])
                pos[best] += 1
            return outl

        DEPTH = 3
        bankctr = [0]

        def run_steps(steps):
            assigned = [None] * len(steps)
            for p in range(len(steps) + DEPTH):
                if p < len(steps):
                    A = steps[p][0]
                    if A is not None:
                        if len(steps[p]) > 2 and not steps[p][2]:
                            A(None)
                        else:
                            assigned[p] = lg[bankctr[0] % 4]
                            bankctr[0] += 1
                            A(assigned[p])
                q = p - DEPTH
                if q >= 0:
                    BC = steps[q][1]
                    if BC is not None:
                        BC(assigned[q])

        for ss in range(NOWN + 2):
            lists = []
            if ss < NOWN:
                lists.append(idx_steps(ss))
            if 1 <= ss <= NOWN:
                lists.append(bis_steps(ss - 1))
            if ss >= 2:
                lists.append(attn_steps(ss - 2))
            run_steps(merge(lists))

    def phase_G():
        c = setup_common('G', npsf=8)
        make_wslots(c, 3)
        mT = kb.sb("g_mT", [128, 16, TOWN], BF16)
        for q in range(4):
            S.dma('sp', f'ga{q}', lambda e, q=q: e.dma_start(out=mT.t[:, q * 4:(q + 1) * 4, :],
                                                            in_=MT2[q * 512:(q + 1) * 512, :].rearrange("(a p) t -> p a t", p=128)),
                  reads=[DR('MT2')], writes=[mT.r])
        xs = [kb.sb(f"g_xs{i}", [128, 512], F32) for i in range(3)]
        os_ = [kb.sb(f"g_os{i}", [128, 512], F32) for i in range(3)]
        cnt = [0]

        def epi(tt, g0, gn, ps):
            s = cnt[0] % 3
            cnt[0] += 1
            x_, o_ = xs[s], os_[s]
            S.dma('sp', f'gx{s}', lambda e: e.dma_start(out=x_.t[:], in_=xown[tt * 128:(tt + 1) * 128, g0:g0 + gn]), writes=[x_.r])
            S.dve(lambda e: e.tensor_tensor(out=o_.t[:], in0=ps.t[:], in1=x_.t[:], op=ALU.add), reads=[ps.r, x_.r], writes=[o_.r])
            S.dma(STORE_ENG, f'go{s}', lambda e: e.dma_start(out=X1[tt * 128:(tt + 1) * 128, g0:g0 + gn], in_=o_.t[:]), reads=[o_.r], writes=[DR('X1')])
        for t4 in range(0, NOWN, 4):
            gemm_tm(c, w_out, 0, 2048, 16, lambda kc, tt, t4=t4: mT.t[:, kc, (t4 + tt) * 128:(t4 + tt + 1) * 128], [mT.r], 4,
                    lambda tt, g0, gn, ps, t4=t4: epi(t4 + tt, g0, gn, ps))

    def phase_H():
        c = setup_common('H', npsf=6)
        make_wslots(c, 2)
        n = make_norm_ctx(c, 1)
        g2 = load_cols(c, "g2col", norm2_g, 16)
        h2 = kb.sb("h_h2T", [128, 16, TOWN], BF16)
        for i in range(NOWN):
            norm_T(c, n, X1[i * 128:(i + 1) * 128, :], g2, lambda k0, k1, i=i: h2.t[:, k0:k1, i * 128:(i + 1) * 128], h2.r)
        fT = kb.sb("h_fT", [128, 64, 512], BF16)
        rl = [kb.sb(f"h_rl{i}", [128, 512], F32) for i in range(2)]
        xs = [kb.sb(f"h_xs{i}", [128, 512], F32) for i in range(3)]
        os_ = [kb.sb(f"h_os{i}", [128, 512], F32) for i in range(3)]
        cnt = [0]
        for q in range(4):
            def epi1(cb, t0, tn, ps, m):
                r_ = rl[cnt[0] % 2]
                cnt[0] += 1
                S.act(lambda e: e.activation(out=r_.t[:], in_=ps.t[:], func=AF.Relu), reads=[ps.r], writes=[r_.r])
                S.dve(lambda e: e.tensor_tensor(out=fT.t[:, cb, :], in0=r_.t[:], in1=r_.t[:], op=ALU.mult), reads=[r_.r], writes=[fT.r])
            gemm_fm(c, w_ff1, 0, DFF, lambda kc, t0, tn, q=q: h2.t[:, kc, q * 512 + t0:q * 512 + t0 + tn], [h2.r], 512, epi1)

            def epi2(tt, g0, gn, ps, q=q):
                s = cnt[0] % 3
                cnt[0] += 1
                x_, o_ = xs[s], os_[s]
                row = (q * 4 + tt) * 128
                S.dma('sp', f'hx{s}', lambda e: e.dma_start(out=x_.t[:], in_=X1[row:row + 128, g0:g0 + gn]), reads=[DR('X1')], writes=[x_.r])
                S.dve(lambda e: e.tensor_tensor(out=o_.t[:], in0=ps.t[:], in1=x_.t[:], op=ALU.add), reads=[ps.r, x_.r], writes=[o_.r])
                S.dma(STORE_ENG, f'ho{s}', lambda e: e.dma_start(out=out[row:row + 128, g0:g0 + gn], in_=o_.t[:]), reads=[o_.r], writes=[DR('out')])
            gemm_tm(c, w_ff2, 0, 2048, 64, lambda kc, tt: fT.t[:, kc, tt * 128:(tt + 1) * 128], [fT.r], 4, epi2)

    def run_phase(fn, *a):
        with ExitStack() as es:
            kb.es = es
            fn(*a)
            n = S.emit(nc)
        dr.clear()
        return n

    stats = {}
    import os
    phases = os.environ.get("KPHASES", "A1,A2,B,C,D,E,F,G,H").split(",")
    if "A1" in phases:
        stats['A1'] = run_phase(phase_A1)
    if "A2" in phases:
        stats['A2'] = run_phase(phase_A2)
    if "B" in phases:
        stats['B'] = run_phase(phase_B)
    if "C" in phases:
        stats['C'] = run_phase(phase_C)
    if "D" in phases:
        stats['D'] = run_phase(phase_proj, OGT, 'OGT', w_proj_gla, 0, None, MT, 'MT', 'D')
    if "E" in phases:
        stats['E'] = run_phase(phase_E)
    if "F" in phases:
        stats['F'] = run_phase(phase_proj, ODT, 'ODT', w_proj_dsa, 2048, (MT, 'MT'), MT2, 'MT2', 'F')
    if "G" in phases:
        stats['G'] = run_phase(phase_G)
    if "H" in phases:
        stats['H'] = run_phase(phase_H)
    return nc, stats


def _rope_tables(pos, rot_dim, full_dim=128):
    half = rot_dim // 2
    inv = (10000.0 ** (-np.arange(half, dtype=np.float32) * 2.0 / rot_dim)).astype(np.float32)
    ang = pos.astype(np.float32)[None, :] * inv[:, None]
    cos = np.ones((full_dim, pos.shape[0]), np.float32)
    sin = np.zeros((full_dim, pos.shape[0]), np.float32)
    cos[0:half] = np.cos(ang)
    cos[half:rot_dim] = np.cos(ang)
    sin[0:half] = np.sin(ang)
    sin[half:rot_dim] = np.sin(ang)
    return cos, sin


def _pmatT(rot_dim, full_dim=128):
    half = rot_dim // 2
    P = np.zeros((full_dim, full_dim), np.float32)
    for d in range(half):
        P[d, d + half] = -1.0
        P[d + half, d] = 1.0
    return np.ascontiguousarray(P.T)


def make_in_maps(inputs):
    x = np.asarray(inputs['x'], np.float32)
    f = lambda k: np.ascontiguousarray(np.asarray(inputs[k], np.float32)[0])
    row = lambda k: np.ascontiguousarray(np.asarray(inputs[k], np.float32)[0][None, :])
    shared = {
        'w_in': f('w_in'), 'gla_wg2': f('gla_wg2'), 'gla_bg': row('gla_bg'), 'gla_norm_g': row('gla_norm_g'),
        'w_proj_gla': f('w_proj_gla'), 'q_norm_g': row('q_norm_g'), 'k_norm_g': row('k_norm_g'),
        'idx_k_norm_g': row('idx_k_norm_g'), 'w_proj_dsa': f('w_proj_dsa'), 'b_gate': row('b_gate'),
        'w_out': f('w_out'), 'norm1_g': row('norm1_g'), 'norm2_g': row('norm2_g'), 'w_ff1': f('w_ff1'), 'w_ff2': f('w_ff2'),
    }
    jj = np.arange(128)
    shared['c_ident'] = np.eye(128, dtype=np.float32)
    shared['c_uincl'] = (jj[:, None] <= jj[None, :]).astype(np.float32)
    shared['c_ustrict'] = (jj[:, None] > jj[None, :]).astype(np.float32)
    shared['c_pdsaT'] = _pmatT(128)
    shared['c_pidxT'] = _pmatT(64)
    posall = np.arange(SEQ)
    shared['c_cosK'], shared['c_sinK'] = _rope_tables(posall, 128)
    shared['c_cosIK'], shared['c_sinIK'] = _rope_tables(posall, 64)
    maps = []
    for core in range(8):
        b, j = core // 4, core % 4
        m = dict(shared)
        m['xfull'] = np.ascontiguousarray(x[b])
        tiles = [4 * i + j for i in range(NOWN)]
        rows = np.concatenate([np.arange(t * 128, (t + 1) * 128) for t in tiles])
        m['xown'] = np.ascontiguousarray(x[b][rows])
        fl = np.zeros((128, 4), np.float32)
        fl[:, j] = 1.0
        m['c_flags'] = fl
        t = np.arange(128)[:, None]
        sp = np.arange(512)[None, :]
        m['c_maskd'] = np.where(sp <= j * 128 + t, 0.0, NEG).astype(np.float32)
        m['c_cosQ'], m['c_sinQ'] = _rope_tables(rows, 128)
        m['c_cosIQ'], m['c_sinIQ'] = _rope_tables(rows, 64)
        maps.append(m)
    return maps


_CACHE = {}


def kernel(**inputs):
    if 'nc' not in _CACHE:
        _CACHE['nc'] = build_program(False)[0]
    nc = _CACHE['nc']
    maps = make_in_maps(inputs)
    res = run_bass_kernel_spmd(nc, maps, core_ids=list(range(8)))
    outf = np.zeros((2, SEQ, D), np.float32)
    for core in range(8):
        b, j = core // 4, core % 4
        o = np.asarray(res.results[core]['out'], np.float32)
        for i in range(NOWN):
            t = 4 * i + j
            outf[b, t * 128:(t + 1) * 128, :] = o[i * 128:(i + 1) * 128, :]
    return outf
```

```python
import math
import numpy as np
from contextlib import ExitStack
import concourse.bass as bass
import concourse.mybir as mybir
from concourse.bass_utils import run_bass_kernel_spmd

F32 = mybir.dt.float32
BF16 = mybir.dt.bfloat16
AF = mybir.ActivationFunctionType
ALU = mybir.AluOpType
AX = mybir.AxisListType

D = 2048
SEQ = 8192
NTALL = 64
NOWN = 16
TOWN = 2048
DIN = 15520
DFF = 8192
O_GQ, O_GK, O_GV, O_GR, O_GLR, O_DQ, O_DK, O_DV, O_IQ, O_IK, O_IW, O_GATES = (
    0, 1024, 2048, 4096, 6144, 6160, 8208, 8720, 9232, 11280, 11408, 11424)
EPS = 1e-6
NEG = -1.0e30
TOPK = 256
NIT = 18

EPOCH = 30000
DMA_EPOCH = 1800
ENGS = ('pe', 'act', 'dve', 'pool', 'sp')


class R:
    __slots__ = ('name', 'w', 'rs')

    def __init__(self, name=''):
        self.name = name
        self.w = None
        self.rs = []


class Op:
    __slots__ = ('eng', 'fn', 'deps', 'key', 'kidx', 'sig', 'sidx')

    def __init__(self, eng, fn, key):
        self.eng = eng
        self.fn = fn
        self.deps = []
        self.key = key
        self.kidx = 0
        self.sig = False
        self.sidx = 0


class Sched:
    _blkc = [0]

    def __init__(self):
        self.ops = {e: [] for e in ENGS}
        self.kcount = {}
        self.klast = {}
        self.allres = []

    def res(self, name=''):
        r = R(name)
        self.allres.append(r)
        return r

    def add(self, eng, fn, reads=(), writes=(), key=None):
        op = Op(eng, fn, key)
        deps = {}
        for r in reads:
            if r.w is not None:
                deps[id(r.w)] = r.w
        for w in writes:
            if w.w is not None:
                deps[id(w.w)] = w.w
            for o in w.rs:
                deps[id(o)] = o
        if key is not None:
            p = self.klast.get(key)
            if p is not None:
                deps[id(p)] = p
            self.klast[key] = op
            self.kcount[key] = self.kcount.get(key, 0) + 1
            op.kidx = self.kcount[key]
        for d in deps.values():
            if d is op:
                continue
            if d.key is None and d.eng == 'pe' and eng == 'pe' and key is None:
                continue
            op.deps.append(d)
            if d.key is None:
                d.sig = True
        for r in reads:
            r.rs.append(op)
        for w in writes:
            w.w = op
            w.rs = []
        self.ops[eng].append(op)
        return op

    def pe(self, fn, reads=(), writes=()):
        return self.add('pe', fn, reads, writes)

    def act(self, fn, reads=(), writes=()):
        return self.add('act', fn, reads, writes)

    def dve(self, fn, reads=(), writes=()):
        return self.add('dve', fn, reads, writes)

    def pool(self, fn, reads=(), writes=()):
        return self.add('pool', fn, reads, writes)

    def dma(self, eng, key, fn, reads=(), writes=()):
        return self.add(eng, fn, reads, writes, key=key)

    def emit(self, nc):
        blk = self._blkc[0]
        self._blkc[0] += 1
        nsig = {}
        for e in ENGS:
            c = 0
            if self.ops[e] and self.ops[e][-1].key is None:
                self.ops[e][-1].sig = True
            for op in self.ops[e]:
                if op.key is None and op.sig:
                    c += 1
                    op.sidx = c
            nsig[e] = c
        handles = []

        def alloc(name):
            h = nc.alloc_semaphore(name=name)
            handles.append(h)
            return h

        if True:
            esem = {}
            for e in ENGS:
                n = (nsig[e] + EPOCH - 1) // EPOCH
                esem[e] = [alloc(f"s{blk}_{e}_{i}") for i in range(n)]
            ksem = {}
            for k, cnt in self.kcount.items():
                n = (cnt + DMA_EPOCH - 1) // DMA_EPOCH
                ksem[k] = [alloc(f"k{blk}_{k}_{i}") for i in range(n)]

            def sem_of(d):
                if d.key is None:
                    i = d.sidx - 1
                    return esem[d.eng][i // EPOCH], (i % EPOCH) + 1
                i = d.kidx - 1
                return ksem[d.key][i // DMA_EPOCH], 16 * ((i % DMA_EPOCH) + 1)

            def run(ename, eng):
                waited = {}
                for op in self.ops[ename]:
                    for d in op.deps:
                        s, v = sem_of(d)
                        if waited.get(id(s), 0) >= v:
                            continue
                        waited[id(s)] = v
                        eng.wait_ge(s, v)
                    ins = op.fn(eng)
                    if op.key is not None:
                        s, v = sem_of(op)
                        ins.then_inc(s, 16)
                    elif op.sig:
                        s, v = sem_of(op)
                        ins.then_inc(s, 1)
                for e2 in ENGS:
                    if nsig[e2] > 0:
                        i = nsig[e2] - 1
                        s, v = esem[e2][i // EPOCH], (i % EPOCH) + 1
                        if waited.get(id(s), 0) < v:
                            eng.wait_ge(s, v)
                for k in self.kcount:
                    s, v = sem_of(self.klast[k])
                    if waited.get(id(s), 0) < v:
                        eng.wait_ge(s, v)

            with nc.Block() as block:
                block.tensor(lambda e: run('pe', e))
                block.scalar(lambda e: run('act', e))
                block.vector(lambda e: run('dve', e))
                block.gpsimd(lambda e: run('pool', e))
                block.sync(lambda e: run('sp', e))
        nc.all_engine_barrier()
        nc.clear_and_free_semaphores(handles)
        nc.all_engine_barrier()
        n_ops = {e: len(self.ops[e]) for e in ENGS}
        self.ops = {e: [] for e in ENGS}
        self.kcount = {}
        self.klast = {}
        for r in self.allres:
            r.w = None
            r.rs = []
        self.allres = []
        return n_ops


class Buf:
    __slots__ = ('t', 'r')

    def __init__(self, t, r):
        self.t = t
        self.r = r


STORE_ENG = 'act'


class K:
    def __init__(self, nc, debug=False):
        self.nc = nc
        self.S = Sched()
        self.es = None
        self.uid = 0
        self.debug = debug

    def sb(self, name, shape, dt):
        self.uid += 1
        t = self.es.enter_context(self.nc.sbuf_tensor(f"{name}_{self.uid}", shape, dt))
        return Buf(t, self.S.res(name))

    def psum(self, name, shape, dt):
        self.uid += 1
        t = self.es.enter_context(self.nc.psum_tensor(f"{name}_{self.uid}", shape, dt))
        return Buf(t, self.S.res(name))


def build_program(debug=False):
    nc = bass.Bass("TRN2", target_bir_lowering=False)
    kb = K(nc, debug)
    S = kb.S

    def din(name, shape, dt=F32):
        return nc.dram_tensor(name, shape, dt, kind="ExternalInput").ap()

    def dscr(name, shape, dt=BF16):
        kind = "ExternalOutput" if debug else "Internal"
        return nc.dram_tensor(name, shape, dt, kind=kind).ap()

    xfull = din("xfull", [SEQ, D])
    xown = din("xown", [TOWN, D])
    w_in = din("w_in", [D, DIN])
    gla_wg2 = din("gla_wg2", [16, 1024])
    gla_bg = din("gla_bg", [1, 1024])
    gla_norm_g = din("gla_norm_g", [1, 512])
    w_proj_gla = din("w_proj_gla", [D, D])
    q_norm_g = din("q_norm_g", [1, 128])
    k_norm_g = din("k_norm_g", [1, 128])
    idx_k_norm_g = din("idx_k_norm_g", [1, 128])
    w_proj_dsa = din("w_proj_dsa", [D, D])
    b_gate = din("b_gate", [1, 4096])
    w_out = din("w_out", [D, D])
    norm1_g = din("norm1_g", [1, D])
    norm2_g = din("norm2_g", [1, D])
    w_ff1 = din("w_ff1", [D, DFF])
    w_ff2 = din("w_ff2", [DFF, D])
    c_ident = din("c_ident", [128, 128])
    c_flags = din("c_flags", [128, 4])
    c_maskd = din("c_maskd", [128, 512])
    c_uincl = din("c_uincl", [128, 128])
    c_ustrict = din("c_ustrict", [128, 128])
    c_pdsaT = din("c_pdsaT", [128, 128])
    c_pidxT = din("c_pidxT", [128, 128])
    c_cosK = din("c_cosK", [128, SEQ])
    c_sinK = din("c_sinK", [128, SEQ])
    c_cosIK = din("c_cosIK", [128, SEQ])
    c_sinIK = din("c_sinIK", [128, SEQ])
    c_cosQ = din("c_cosQ", [128, TOWN])
    c_sinQ = din("c_sinQ", [128, TOWN])
    c_cosIQ = din("c_cosIQ", [128, TOWN])
    c_sinIQ = din("c_sinIQ", [128, TOWN])

    out = nc.dram_tensor("out", [TOWN, D], F32, kind="ExternalOutput").ap()

    KT = dscr("KT", [4, 128, SEQ])
    VV = dscr("VV", [SEQ, 512])
    IKT = dscr("IKT", [128, SEQ])
    SNAP = dscr("SNAP", [NTALL, 128, 4096])
    QT = dscr("QT", [1024, TOWN])
    KGT = dscr("KGT", [1024, TOWN])
    VG = dscr("VG", [TOWN, 2048])
    GRT = dscr("GRT", [2048, TOWN])
    DQT = dscr("DQT", [2048, TOWN])
    IQT = dscr("IQT", [2048, TOWN])
    IWS = dscr("IWS", [TOWN, 16], F32)
    GT = dscr("GT", [4096, TOWN])
    OGT = dscr("OGT", [2048, TOWN])
    MT = dscr("MT", [2048, TOWN])
    ODT = dscr("ODT", [2048, TOWN])
    MT2 = dscr("MT2", [2048, TOWN])
    X1 = dscr("X1", [TOWN, D], F32)
    HTS = dscr("HTS", [NTALL, 128, 2048])
    EQT = dscr("EQT", [1024, TOWN])
    EKT = dscr("EKT", [1024, TOWN])

    dr = {}

    def DR(name):
        if name not in dr:
            dr[name] = S.res(name)
        return dr[name]

    class Common:
        pass

    def setup_common(p, npsf=6):
        c = Common()
        c.p = p
        c.identf = kb.sb("identf", [128, 128], F32)
        c.identb = kb.sb("identb", [128, 128], BF16)
        c.onesb = kb.sb("onesb", [128, 128], BF16)
        c.cst = kb.sb("cst", [128, 8], F32)
        S.dma('sp', 'cid', lambda e: e.dma_start(out=c.identf.t[:], in_=c_ident), writes=[c.identf.r])
        S.dve(lambda e: e.tensor_copy(out=c.identb.t[:], in_=c.identf.t[:]), reads=[c.identf.r], writes=[c.identb.r])
        S.dve(lambda e: e.memset(c.onesb.t[:], 1.0), writes=[c.onesb.r])
        vals = [EPS, 1.0, math.log(1.0 / 16.0), 0.0, 0.5, float(TOPK), 0.0, 0.0]
        for i, v in enumerate(vals):
            S.dve(lambda e, i=i, v=v: e.memset(c.cst.t[:, i:i + 1], v), writes=[c.cst.r])
        c.ps = [kb.psum(f"ps{i}", [128, 512], F32) for i in range(npsf)]
        c.pb = [kb.psum(f"pb{i}", [128, 1024], BF16) for i in range(8 - npsf)]
        c.psi = 0
        c.wslots = None
        c.wi = 0
        return c

    def next_ps(c):
        b = c.ps[c.psi % len(c.ps)]
        c.psi += 1
        return b

    def load_cols(c, name, src_row, n):
        t = kb.sb(name, [128, n], F32)
        S.dma('sp', 'cols_' + name,
              lambda e: e.dma_start(out=t.t[:], in_=src_row.rearrange("o (c p) -> p (o c)", p=128),
                                    allow_slow_non_contiguous=True), writes=[t.r])
        return t

    def make_norm_ctx(c, nslots=2):
        n = Common()
        n.xt = [kb.sb(f"xt{i}", [128, D], F32) for i in range(nslots)]
        n.junk = kb.sb("junk", [128, D], BF16)
        n.ss = [kb.sb(f"ss{i}", [128, 4], F32) for i in range(nslots)]
        n.h = [kb.sb(f"h{i}", [128, D], BF16) for i in range(nslots)]
        n.i = 0
        return n

    def norm_pre(c, n, x_ap):
        sl = n.i % len(n.xt)
        n.i += 1
        xt, ss, h = n.xt[sl], n.ss[sl], n.h[sl]
        S.dma('sp', f'x{sl}', lambda e: e.dma_start(out=xt.t[:], in_=x_ap), writes=[xt.r])
        S.act(lambda e: e.activation(out=n.junk.t[:], in_=xt.t[:], func=AF.Square, accum_out=ss.t[:, 0:1]),
              reads=[xt.r], writes=[n.junk.r, ss.r])
        S.act(lambda e: e.activation(out=ss.t[:, 1:2], in_=ss.t[:, 0:1], func=AF.Ln, scale=1.0 / D,
                                     bias=c.cst.t[:, 0:1]), reads=[ss.r, c.cst.r], writes=[ss.r])
        S.act(lambda e: e.activation(out=ss.t[:, 2:3], in_=ss.t[:, 1:2], func=AF.Exp, scale=-0.5), reads=[ss.r], writes=[ss.r])
        S.dve(lambda e: e.tensor_scalar(out=h.t[:], in0=xt.t[:], scalar1=ss.t[:, 2:3], scalar2=None, op0=ALU.mult),
              reads=[xt.r, ss.r], writes=[h.r])
        return sl

    def norm_post(c, n, sl, gcol, dst_fn, dst_r):
        h = n.h[sl]
        for half in range(2):
            pb = c.pb[half % len(c.pb)]
            for k in range(8):
                kc = half * 8 + k
                S.pe(lambda e, k=k, kc=kc, pb=pb: e.transpose(out=pb.t[:, k * 128:(k + 1) * 128],
                                                              in_=h.t[:, kc * 128:(kc + 1) * 128], identity=c.identb.t[:]),
                     reads=[h.r, c.identb.r], writes=[pb.r])
            S.dve(lambda e, half=half, pb=pb: e.tensor_tensor(
                out=dst_fn(half * 8, half * 8 + 8), in0=pb.t[:, 0:1024].rearrange("p (a b) -> p a b", b=128),
                in1=gcol.t[:, half * 8:half * 8 + 8].unsqueeze(2).to_broadcast([128, 8, 128]), op=ALU.mult),
                reads=[pb.r, gcol.r], writes=[dst_r])

    def norm_T(c, n, x_ap, gcol, dst_fn, dst_r, xkeep=None):
        sl = norm_pre(c, n, x_ap)
        norm_post(c, n, sl, gcol, dst_fn, dst_r)

    def load_w(c, wt, W, r0, nkc, c0, ncols, key):
        src = W[r0:r0 + nkc * 128, c0:c0 + ncols].rearrange("(k p) c -> p k c", p=128)
        S.dma('pool', key, lambda e: e.dma_start(out=wt.t[:, 0:nkc, 0:ncols], in_=src), writes=[wt.r])

    def make_wslots(c, n=3, kc=16, cols=512):
        c.wslots = [kb.sb(f"wsl{i}", [128, kc, cols], BF16) for i in range(n)]
        c.wi = 0

    def next_w(c):
        i = c.wi % len(c.wslots)
        c.wi += 1
        return c.wslots[i], f"w{i}"

    def gemm_fm(c, W, c0, ncols, act_fn, act_rs, T, epi, grp=512):
        pending = None
        for g0 in range(0, ncols, grp):
            gn = min(grp, ncols - g0)
            wt, key = next_w(c)
            load_w(c, wt, W, 0, 16, c0 + g0, gn, key)
            for cb0 in range(0, gn, 128):
                m = min(128, gn - cb0)
                for t0 in range(0, T, 512):
                    tn = min(512, T - t0)
                    ps = next_ps(c)
                    for kc in range(16):
                        S.pe(lambda e, kc=kc, cb0=cb0, m=m, t0=t0, tn=tn, ps=ps, wt=wt:
                             e.matmul(out=ps.t[0:m, 0:tn], lhsT=wt.t[:, kc, cb0:cb0 + m], rhs=act_fn(kc, t0, tn),
                                      start=(kc == 0), stop=(kc == 15)),
                             reads=[wt.r] + list(act_rs), writes=[ps.r])
                    if pending is not None:
                        pending()
                    pending = (lambda cb=(g0 + cb0) // 128, t0=t0, tn=tn, ps=ps, m=m: epi(cb, t0, tn, ps, m))
        if pending is not None:
            pending()

    def gemm_tm(c, W, c0, ncols, KC, act_fn, act_rs, ntt, epi, grp=512):
        for g0 in range(0, ncols, grp):
            gn = min(grp, ncols - g0)
            pss = [next_ps(c) for _ in range(ntt)]
            for ks in range(0, KC, 16):
                wt, key = next_w(c)
                load_w(c, wt, W, ks * 128, 16, c0 + g0, gn, key)
                for tt in range(ntt):
                    ps = pss[tt]
                    for k in range(16):
                        kc = ks + k
                        S.pe(lambda e, kc=kc, k=k, tt=tt, ps=ps, wt=wt, gn=gn:
                             e.matmul(out=ps.t[:, 0:gn], lhsT=act_fn(kc, tt), rhs=wt.t[:, k, 0:gn],
                                      start=(kc == 0), stop=(kc == KC - 1)),
                             reads=[wt.r] + list(act_rs), writes=[ps.r])
            for tt in range(ntt):
                epi(tt, g0, gn, pss[tt])

    def make_rope_ctx(c, N):
        r = Common()
        r.N = N
        r.sq = kb.sb("r_sq", [128, N], BF16)
        r.rstd = kb.sb("r_rstd", [128, N], F32)
        r.xnb = kb.sb("r_xnb", [128, N], BF16)
        r.t1 = kb.sb("r_t1", [128, N], F32)
        r.t2 = kb.sb("r_t2", [128, N], F32)
        return r

    def headnorm_rope(c, r, ps, n, gain, cosb, sinb, cs_ap, pmatT, dst, dst_r, ps2=None, ps3=None):
        if gain is not None:
            S.act(lambda e: e.activation(out=r.sq.t[:, 0:n], in_=ps.t[:, 0:n], func=AF.Square), reads=[ps.r], writes=[r.sq.r])
            ps2 = ps2 if ps2 is not None else next_ps(c)
            S.pe(lambda e: e.matmul(out=ps2.t[:, 0:n], lhsT=c.onesb.t[:], rhs=r.sq.t[:, 0:n], start=True, stop=True),
                 reads=[c.onesb.r, r.sq.r], writes=[ps2.r])
            S.act(lambda e: e.activation(out=r.t1.t[:, 0:n], in_=ps2.t[:, 0:n], func=AF.Ln, scale=1.0 / 128,
                                         bias=c.cst.t[:, 0:1]), reads=[ps2.r, c.cst.r], writes=[r.t1.r])
            S.act(lambda e: e.activation(out=r.rstd.t[:, 0:n], in_=r.t1.t[:, 0:n], func=AF.Exp, scale=-0.5),
                  reads=[r.t1.r], writes=[r.rstd.r])
            S.dve(lambda e: e.scalar_tensor_tensor(out=r.xnb.t[:, 0:n], in0=ps.t[:, 0:n], scalar=gain.t[:, 0:1],
                                                   in1=r.rstd.t[:, 0:n], op0=ALU.mult, op1=ALU.mult),
                  reads=[ps.r, gain.r, r.rstd.r], writes=[r.xnb.r])
        else:
            S.act(lambda e: e.activation(out=r.xnb.t[:, 0:n], in_=ps.t[:, 0:n], func=AF.Copy), reads=[ps.r], writes=[r.xnb.r])
        ps3 = ps3 if ps3 is not None else next_ps(c)
        S.pe(lambda e: e.matmul(out=ps3.t[:, 0:n], lhsT=pmatT.t[:], rhs=r.xnb.t[:, 0:n], start=True, stop=True),
             reads=[pmatT.r, r.xnb.r], writes=[ps3.r])
        S.dve(lambda e: e.tensor_tensor(out=r.t1.t[:, 0:n], in0=r.xnb.t[:, 0:n], in1=cs_ap(cosb), op=ALU.mult),
              reads=[r.xnb.r, cosb.r], writes=[r.t1.r])
        S.dve(lambda e: e.tensor_tensor(out=r.t2.t[:, 0:n], in0=ps3.t[:, 0:n], in1=cs_ap(sinb), op=ALU.mult),
              reads=[ps3.r, sinb.r], writes=[r.t2.r])
        S.dve(lambda e: e.tensor_tensor(out=dst, in0=r.t1.t[:, 0:n], in1=r.t2.t[:, 0:n], op=ALU.add),
              reads=[r.t1.r, r.t2.r], writes=[dst_r])

    def load_pmat(c, name, src):
        f = kb.sb(name + "f", [128, 128], F32)
        b = kb.sb(name + "b", [128, 128], BF16)
        S.dma('sp', 'pm_' + name, lambda e: e.dma_start(out=f.t[:], in_=src), writes=[f.r])
        S.dve(lambda e: e.tensor_copy(out=b.t[:], in_=f.t[:]), reads=[f.r], writes=[b.r])
        return b

    def glr_logsig(c, g, hT_fn, hT_r):
        psg = next_ps(c)
        for kc in range(16):
            S.pe(lambda e, kc=kc: e.matmul(out=psg.t[0:16, 0:128], lhsT=g.wl.t[:, kc, :], rhs=hT_fn(kc),
                                           start=(kc == 0), stop=(kc == 15)), reads=[g.wl.r, hT_r], writes=[psg.r])
        S.act(lambda e: e.activation(out=g.glrT.t[:], in_=psg.t[0:16, 0:128], func=AF.Copy), reads=[psg.r], writes=[g.glrT.r])
        for hf in range(2):
            psz = next_ps(c)
            S.pe(lambda e, hf=hf, psz=psz: e.matmul(out=psz.t[:], lhsT=g.glrT.t[:], rhs=g.wg2b.t[:, hf * 512:(hf + 1) * 512],
                                                    start=True, stop=False), reads=[g.glrT.r, g.wg2b.r], writes=[psz.r])
            S.pe(lambda e, hf=hf, psz=psz: e.matmul(out=psz.t[:], lhsT=g.onesrow.t[:], rhs=g.bgb.t[:, hf * 512:(hf + 1) * 512],
                                                    start=False, stop=True), reads=[g.onesrow.r, g.bgb.r], writes=[psz.r])
            S.act(lambda e, hf=hf, psz=psz: e.activation(out=g.ez.t[:, hf * 512:(hf + 1) * 512], in_=psz.t[:], func=AF.Exp, scale=-1.0),
                  reads=[psz.r], writes=[g.ez.r])
        S.act(lambda e: e.activation(out=g.L.t[:], in_=g.ez.t[:], func=AF.Ln, bias=c.cst.t[:, 1:2], scale=1.0),
              reads=[g.ez.r, c.cst.r], writes=[g.L.r])

    def make_glr_ctx(c):
        g = Common()
        g.wl = kb.sb("g_wl", [128, 16, 16], BF16)
        load_w(c, g.wl, w_in, 0, 16, O_GLR, 16, 'wl')
        g.wg2b = kb.sb("g_wg2b", [16, 1024], BF16)
        S.dma('pool', 'wg2', lambda e: e.dma_start(out=g.wg2b.t[:], in_=gla_wg2), writes=[g.wg2b.r])
        g.bgb = kb.sb("g_bgb", [1, 1024], BF16)
        S.dma('pool', 'bg', lambda e: e.dma_start(out=g.bgb.t[:], in_=gla_bg), writes=[g.bgb.r])
        g.onesrow = kb.sb("g_onesrow", [1, 128], BF16)
        S.dve(lambda e: e.memset(g.onesrow.t[:], 1.0), writes=[g.onesrow.r])
        g.glrT = kb.sb("g_glrT", [16, 128], BF16)
        g.ez = kb.sb("g_ez", [128, 1024], F32)
        g.L = kb.sb("g_L", [128, 1024], F32)
        return g

    def phase_A1():
        c = setup_common('A1', npsf=6)
        n = make_norm_ctx(c)
        g = make_glr_ctx(c)
        g1 = load_cols(c, "g1col", norm1_g, 16)
        wk = kb.sb("a_wk", [128, 16, 1024], BF16)
        wv = kb.sb("a_wv", [128, 16, 2048], BF16)
        for q in range(2):
            src = w_in[:, O_GK + q * 512:O_GK + (q + 1) * 512].rearrange("(k p) c -> p k c", p=128)
            S.dma('pool', f'wk{q}', lambda e, q=q, src=src: e.dma_start(out=wk.t[:, :, q * 512:(q + 1) * 512], in_=src), writes=[wk.r])
        for q in range(4):
            src = w_in[:, O_GV + q * 512:O_GV + (q + 1) * 512].rearrange("(k p) c -> p k c", p=128)
            S.dma('pool', f'wv{q}', lambda e, q=q, src=src: e.dma_start(out=wv.t[:, :, q * 512:(q + 1) * 512], in_=src), writes=[wv.r])
        ustr = kb.sb("a_ustr", [128, 128], F32)
        S.dma('sp', 'ustr', lambda e: e.dma_start(out=ustr.t[:], in_=c_ustrict), writes=[ustr.r])
        onesc = kb.sb("a_onesc", [128, 1], F32)
        S.dve(lambda e: e.memset(onesc.t[:], 1.0), writes=[onesc.r])
        hT = [kb.sb(f"a_hT{i}", [128, 16, 128], BF16) for i in range(2)]
        state = kb.sb("a_state", [128, 8, 512], F32)
        S.dve(lambda e: e.memset(state.t[:], 0.0), writes=[state.r])
        stb = [kb.sb(f"a_stb{i}", [128, 8, 512], BF16) for i in range(2)]
        e1 = kb.sb("a_e1", [128, 1024], F32)
        khat = kb.sb("a_khat", [128, 1024], BF16)
        gvb = kb.sb("a_gvb", [128, 2048], BF16)
        dec = kb.sb("a_dec", [128, 8], F32)
        B = c.ps
        pending_state = [None]
        nsl = norm_pre(c, n, xfull[0:128, :])
        for tt in range(NTALL):
            h = hT[tt % 2]
            norm_post(c, n, nsl, g1, lambda k0, k1, h=h: h.t[:, k0:k1, :], h.r)
            if tt + 1 < NTALL:
                nsl = norm_pre(c, n, xfull[(tt + 1) * 128:(tt + 2) * 128, :])
            if pending_state[0] is not None:
                pending_state[0]()
                pending_state[0] = None
            S.dma(STORE_ENG, f'hts{tt % 2}', lambda e, h=h, tt=tt: e.dma_start(out=HTS[tt], in_=h.t[:].rearrange("p a b -> p (a b)")),
                  reads=[h.r], writes=[DR('HTS')])
            for kc in range(16):
                S.pe(lambda e, kc=kc, h=h: e.matmul(out=B[2].t[0:16, 0:128], lhsT=g.wl.t[:, kc, :], rhs=h.t[:, kc, :], start=(kc == 0), stop=(kc == 15)),
                     reads=[g.wl.r, h.r], writes=[B[2].r])
            S.act(lambda e: e.activation(out=g.glrT.t[:], in_=B[2].t[0:16, 0:128], func=AF.Copy), reads=[B[2].r], writes=[g.glrT.r])
            for hf in range(2):
                for kc in range(16):
                    S.pe(lambda e, kc=kc, hf=hf, h=h: e.matmul(out=B[hf].t[:], lhsT=h.t[:, kc, :], rhs=wk.t[:, kc, hf * 512:(hf + 1) * 512],
                                                               start=(kc == 0), stop=(kc == 15)), reads=[h.r, wk.r], writes=[B[hf].r])
            for hf in range(2):
                psz = B[2 + hf]
                S.pe(lambda e, hf=hf, psz=psz: e.matmul(out=psz.t[:], lhsT=g.glrT.t[:], rhs=g.wg2b.t[:, hf * 512:(hf + 1) * 512], start=True, stop=False),
                     reads=[g.glrT.r, g.wg2b.r], writes=[psz.r])
                S.pe(lambda e, hf=hf, psz=psz: e.matmul(out=psz.t[:], lhsT=g.onesrow.t[:], rhs=g.bgb.t[:, hf * 512:(hf + 1) * 512], start=False, stop=True),
                     reads=[g.onesrow.r, g.bgb.r], writes=[psz.r])
                S.act(lambda e, hf=hf, psz=psz: e.activation(out=g.ez.t[:, hf * 512:(hf + 1) * 512], in_=psz.t[:], func=AF.Exp, scale=-1.0),
                      reads=[psz.r], writes=[g.ez.r])
            for q in range(4):
                ps = B[4 + q % 2]
                for kc in range(16):
                    S.pe(lambda e, kc=kc, q=q, ps=ps, h=h: e.matmul(out=ps.t[:], lhsT=h.t[:, kc, :], rhs=wv.t[:, kc, q * 512:(q + 1) * 512],
                                                                    start=(kc == 0), stop=(kc == 15)), reads=[h.r, wv.r], writes=[ps.r])
                if q % 2 == 0:
                    S.act(lambda e, q=q, ps=ps: e.activation(out=gvb.t[:, q * 512:(q + 1) * 512], in_=ps.t[:], func=AF.Copy), reads=[ps.r], writes=[gvb.r])
                else:
                    S.dve(lambda e, q=q, ps=ps: e.tensor_copy(out=gvb.t[:, q * 512:(q + 1) * 512], in_=ps.t[:]), reads=[ps.r], writes=[gvb.r])
            S.act(lambda e: e.activation(out=g.L.t[:], in_=g.ez.t[:], func=AF.Ln, bias=c.cst.t[:, 1:2], scale=1.0),
                  reads=[g.ez.r, c.cst.r], writes=[g.L.r])
            for hf in range(2):
                ps = B[2 + hf]
                S.pe(lambda e, hf=hf, ps=ps: e.matmul(out=ps.t[:], lhsT=ustr.t[:], rhs=g.L.t[:, hf * 512:(hf + 1) * 512], start=True, stop=True),
                     reads=[ustr.r, g.L.r], writes=[ps.r])
                S.act(lambda e, hf=hf, ps=ps: e.activation(out=e1.t[:, hf * 512:(hf + 1) * 512], in_=ps.t[:], func=AF.Exp, scale=-1.0 / 16),
                      reads=[ps.r], writes=[e1.r])
            psd = B[4]
            for cc in range(8):
                S.pe(lambda e, cc=cc: e.matmul(out=psd.t[:, cc:cc + 1], lhsT=g.L.t[:, cc * 128:(cc + 1) * 128], rhs=onesc.t[:], start=True, stop=True),
                     reads=[g.L.r, onesc.r], writes=[psd.r])
            S.act(lambda e: e.activation(out=dec.t[:], in_=psd.t[:, 0:8], func=AF.Exp, scale=-1.0 / 16), reads=[psd.r], writes=[dec.r])
            for hf in range(2):
                S.dve(lambda e, hf=hf: e.tensor_tensor(out=khat.t[:, hf * 512:(hf + 1) * 512], in0=B[hf].t[:], in1=e1.t[:, hf * 512:(hf + 1) * 512], op=ALU.mult),
                      reads=[B[hf].r, e1.r], writes=[khat.r])
            def state_step(tt=tt):
                sb_ = stb[tt % 2]
                S.act(lambda e: e.activation(out=sb_.t[:, 0:4, :], in_=state.t[:, 0:4, :], func=AF.Copy), reads=[state.r], writes=[sb_.r])
                S.dve(lambda e: e.tensor_copy(out=sb_.t[:, 4:8, :], in_=state.t[:, 4:8, :]), reads=[state.r], writes=[sb_.r])
                S.dma(STORE_ENG, f'snap{tt % 2}', lambda e: e.dma_start(out=SNAP[tt], in_=sb_.t[:].rearrange("p a b -> p (a b)")),
                      reads=[sb_.r], writes=[DR('SNAP')])
                rot = [B[2], B[3], B[5], B[4]]
                for ix in range(8):
                    hh = ix // 2
                    ps = rot[ix % 4]
                    S.pe(lambda e, ix=ix, hh=hh, ps=ps: e.matmul(out=ps.t[:], lhsT=khat.t[:, ix * 128:(ix + 1) * 128], rhs=gvb.t[:, hh * 512:(hh + 1) * 512],
                                                                 start=True, stop=True), reads=[khat.r, gvb.r], writes=[ps.r])
                    S.dve(lambda e, ix=ix, ps=ps: e.scalar_tensor_tensor(out=state.t[:, ix, :], in0=state.t[:, ix, :], scalar=dec.t[:, ix:ix + 1],
                                                                         in1=ps.t[:], op0=ALU.mult, op1=ALU.add),
                          reads=[state.r, dec.r, ps.r], writes=[state.r])
            pending_state[0] = state_step
        pending_state[0]()

    def phase_A2():
        c = setup_common('A2', npsf=6)
        r = make_rope_ctx(c, 512)
        kng = load_cols(c, "kng", k_norm_g, 1)
        ikng = load_cols(c, "ikng", idx_k_norm_g, 1)
        pd = load_pmat(c, "pdsa", c_pdsaT)
        pi = load_pmat(c, "pidx", c_pidxT)
        wdk = kb.sb("b_wdk", [128, 16, 512], BF16)
        wdv = kb.sb("b_wdv", [128, 16, 512], BF16)
        wik = kb.sb("b_wik", [128, 16, 128], BF16)
        load_w(c, wdk, w_in, 0, 16, O_DK, 512, 'wdk')
        load_w(c, wdv, w_in, 0, 16, O_DV, 512, 'wdv')
        load_w(c, wik, w_in, 0, 16, O_IK, 128, 'wik')
        hT = [kb.sb(f"b_hT{i}", [128, 16, 512], BF16) for i in range(2)]
        tabs = [[kb.sb(f"b_tab{i}_{s}", [128, 512], F32) for i in range(4)] for s in range(2)]
        vst = [kb.sb(f"b_vst{i}", [128, 512], BF16) for i in range(2)]
        kst = [kb.sb(f"b_kst{i}", [128, 5, 512], BF16) for i in range(2)]
        B = c.ps
        vcnt = 0
        for st4 in range(NTALL // 4):
            h = hT[st4 % 2]
            tb = tabs[st4 % 2]
            for t in range(4):
                tt = st4 * 4 + t
                S.dma('sp', f'hld{st4 % 2}_{t}', lambda e, h=h, t=t, tt=tt: e.dma_start(
                    out=h.t[:, :, t * 128:(t + 1) * 128], in_=HTS[tt].rearrange("p (a b) -> p a b", b=128)), reads=[DR('HTS')], writes=[h.r])
            for i, src in enumerate((c_cosK, c_sinK, c_cosIK, c_sinIK)):
                S.dma('sp', f'tab{i}_{st4 % 2}', lambda e, i=i, src=src, tb=tb, st4=st4: e.dma_start(out=tb[i].t[:], in_=src[:, st4 * 512:(st4 + 1) * 512]),
                      writes=[tb[i].r])
            ks = kst[st4 % 2]

            def proj(gg, h=h):
                ps = B[gg % 2]
                wr = wdk.r if gg < 4 else wik.r
                for kc in range(16):
                    S.pe(lambda e, kc=kc, ps=ps, gg=gg: e.matmul(
                        out=ps.t[:, 0:512], lhsT=(wdk.t[:, kc, gg * 128:(gg + 1) * 128] if gg < 4 else wik.t[:, kc, :]),
                        rhs=h.t[:, kc, :], start=(kc == 0), stop=(kc == 15)), reads=[h.r, wr], writes=[ps.r])

            def epil(gg, tb=tb, ks=ks):
                ps = B[gg % 2]
                if gg < 4:
                    headnorm_rope(c, r, ps, 512, kng, tb[0], tb[1], lambda b: b.t[:], pd, ks.t[:, gg, :], ks.r, ps2=B[2], ps3=B[3])
                else:
                    headnorm_rope(c, r, ps, 512, ikng, tb[2], tb[3], lambda b: b.t[:], pi, ks.t[:, gg, :], ks.r, ps2=B[2], ps3=B[3])

            def vproj(t, h=h, st4=st4):
                nonlocal vcnt
                tt = st4 * 4 + t
                ps = B[4 + vcnt % 2]
                vs = vst[vcnt % 2]
                slot = vcnt % 2
                vcnt += 1
                for kc in range(16):
                    S.pe(lambda e, kc=kc, ps=ps, t=t: e.matmul(out=ps.t[:], lhsT=h.t[:, kc, t * 128:(t + 1) * 128], rhs=wdv.t[:, kc, :], start=(kc == 0), stop=(kc == 15)),
                         reads=[h.r, wdv.r], writes=[ps.r])
                S.act(lambda e, ps=ps, vs=vs: e.activation(out=vs.t[:], in_=ps.t[:], func=AF.Copy), reads=[ps.r], writes=[vs.r])
                S.dma(STORE_ENG, f'vst{slot}', lambda e, vs=vs, tt=tt: e.dma_start(out=VV[tt * 128:(tt + 1) * 128, :], in_=vs.t[:]),
                      reads=[vs.r], writes=[DR('VV')])

            proj(0)
            for gg in range(5):
                if gg + 1 < 5:
                    proj(gg + 1)
                if gg < 4:
                    vproj(gg)
                epil(gg)
            S.dma(STORE_ENG, f'kst{st4 % 2}', lambda e, ks=ks, st4=st4: e.dma_start(
                out=KT[:, :, st4 * 512:(st4 + 1) * 512].rearrange("g p t -> p g t"), in_=ks.t[:, 0:4, :]), reads=[ks.r], writes=[DR('KT')])
            S.dma(STORE_ENG, f'ikst{st4 % 2}', lambda e, ks=ks, st4=st4: e.dma_start(out=IKT[:, st4 * 512:(st4 + 1) * 512], in_=ks.t[:, 4, :]),
                  reads=[ks.r], writes=[DR('IKT')])


    def phase_B():
        c = setup_common('B', npsf=6)
        make_wslots(c, 2)
        n = make_norm_ctx(c, 1)
        g = make_glr_ctx(c)
        r = make_rope_ctx(c, 512)
        g1 = load_cols(c, "g1col", norm1_g, 16)
        qng = load_cols(c, "qng", q_norm_g, 1)
        bgc = load_cols(c, "bgc", b_gate, 32)
        pd = load_pmat(c, "pdsa", c_pdsaT)
        pi = load_pmat(c, "pidx", c_pidxT)
        uincl = kb.sb("c_uincl", [128, 128], F32)
        S.dma('sp', 'uincl', lambda e: e.dma_start(out=uincl.t[:], in_=c_uincl), writes=[uincl.r])
        hT = kb.sb("hTown", [128, 16, TOWN], BF16)
        eqs = [kb.sb(f"eqs{i}", [128, 8, 128], BF16) for i in range(2)]
        eks = [kb.sb(f"eks{i}", [128, 8, 128], BF16) for i in range(2)]
        for i in range(NOWN):
            norm_T(c, n, xown[i * 128:(i + 1) * 128, :], g1, lambda k0, k1, i=i: hT.t[:, k0:k1, i * 128:(i + 1) * 128], hT.r)
        for i in range(NOWN):
            glr_logsig(c, g, lambda kc, i=i: hT.t[:, kc, i * 128:(i + 1) * 128], hT.r)
            for hf in range(2):
                ps = next_ps(c)
                for q in range(4):
                    cc = hf * 4 + q
                    S.pe(lambda e, cc=cc, q=q, ps=ps: e.matmul(out=ps.t[:, q * 128:(q + 1) * 128], lhsT=g.L.t[:, cc * 128:(cc + 1) * 128], rhs=uincl.t[:],
                                                               start=True, stop=True), reads=[g.L.r, uincl.r], writes=[ps.r])
                eq, ek = eqs[i % 2], eks[i % 2]
                S.act(lambda e, hf=hf, ps=ps, eq=eq: e.activation(out=eq.t[:, hf * 4:(hf + 1) * 4, :],
                                                                in_=ps.t[:].rearrange("p (a b) -> p a b", b=128), func=AF.Exp,
                                                                scale=-1.0 / 16, bias=c.cst.t[:, 2:3]), reads=[ps.r, c.cst.r], writes=[eq.r])
                S.act(lambda e, hf=hf, ps=ps, ek=ek: e.activation(out=ek.t[:, hf * 4:(hf + 1) * 4, :],
                                                                in_=ps.t[:].rearrange("p (a b) -> p a b", b=128), func=AF.Exp,
                                                                scale=1.0 / 16), reads=[ps.r], writes=[ek.r])
            S.dma(STORE_ENG, f'eqs{i % 2}', lambda e, eq=eq, i=i: e.dma_start(out=EQT[:, i * 128:(i + 1) * 128].rearrange("(a p) t -> p a t", p=128), in_=eq.t[:]),
                  reads=[eq.r], writes=[DR('EQT')])
            S.dma(STORE_ENG, f'eks{i % 2}', lambda e, ek=ek, i=i: e.dma_start(out=EKT[:, i * 128:(i + 1) * 128].rearrange("(a p) t -> p a t", p=128), in_=ek.t[:]),
                  reads=[ek.r], writes=[DR('EKT')])
        act_fn = lambda kc, t0, tn: hT.t[:, kc, t0:t0 + tn]
        stg = [kb.sb(f"stg{i}", [128, 512], BF16) for i in range(3)]
        stc = [0]

        def stage():
            b = stg[stc[0] % 3]
            k = f"stg{stc[0] % 3}"
            stc[0] += 1
            return b, k

        def store_fm(dst, drn, cb, t0, tn, st, key, m=128):
            S.dma(STORE_ENG, key, lambda e: e.dma_start(out=dst[cb * 128:cb * 128 + m, t0:t0 + tn], in_=st.t[0:m, 0:tn]),
                  reads=[st.r], writes=[DR(drn)])

        esl = [kb.sb(f"esl{i}", [128, 512], BF16) for i in range(3)]
        ecnt = [0]

        def epi_mul(E, En, dst, drn):
            def epi(cb, t0, tn, ps, m):
                st, key = stage()
                sl = ecnt[0] % 3
                ecnt[0] += 1
                eb_ = esl[sl]
                S.dma('sp', f'esl{sl}', lambda e: e.dma_start(out=eb_.t[:, 0:tn], in_=E[cb * 128:(cb + 1) * 128, t0:t0 + tn]),
                      reads=[DR(En)], writes=[eb_.r])
                S.dve(lambda e: e.tensor_tensor(out=st.t[:, 0:tn], in0=ps.t[:, 0:tn], in1=eb_.t[:, 0:tn], op=ALU.mult),
                      reads=[ps.r, eb_.r], writes=[st.r])
                store_fm(dst, drn, cb, t0, tn, st, key)
            return epi

        gemm_fm(c, w_in, O_GQ, 1024, act_fn, [hT.r], TOWN, epi_mul(EQT, 'EQT', QT, 'QT'))
        gemm_fm(c, w_in, O_GK, 1024, act_fn, [hT.r], TOWN, epi_mul(EKT, 'EKT', KGT, 'KGT'))

        def epi_silu(cb, t0, tn, ps, m):
            st, key = stage()
            S.act(lambda e: e.activation(out=st.t[:, 0:tn], in_=ps.t[:, 0:tn], func=AF.Silu), reads=[ps.r], writes=[st.r])
            store_fm(GRT, 'GRT', cb, t0, tn, st, key)
        gemm_fm(c, w_in, O_GR, 2048, act_fn, [hT.r], TOWN, epi_silu)

        def epi_gate(cb, t0, tn, ps, m):
            st, key = stage()
            S.act(lambda e: e.activation(out=st.t[:, 0:tn], in_=ps.t[:, 0:tn], func=AF.Sigmoid, bias=bgc.t[:, cb:cb + 1], scale=1.0),
                  reads=[ps.r, bgc.r], writes=[st.r])
            store_fm(GT, 'GT', cb, t0, tn, st, key)
        gemm_fm(c, w_in, O_GATES, 4096, act_fn, [hT.r], TOWN, epi_gate)

        tabq = [[kb.sb(f"tabq{i}_{s}", [128, 512], F32) for i in range(2)] for s in range(2)]
        tcnt = [0]

        def epi_rope(gain, csrc, ssrc, pm, dst, drn):
            def epi(cb, t0, tn, ps, m):
                st, key = stage()
                sl = tcnt[0] % 2
                tcnt[0] += 1
                tb = tabq[sl]
                S.dma('sp', f'tq0_{sl}', lambda e: e.dma_start(out=tb[0].t[:, 0:tn], in_=csrc[:, t0:t0 + tn]), writes=[tb[0].r])
                S.dma('sp', f'tq1_{sl}', lambda e: e.dma_start(out=tb[1].t[:, 0:tn], in_=ssrc[:, t0:t0 + tn]), writes=[tb[1].r])
                headnorm_rope(c, r, ps, tn, gain, tb[0], tb[1], lambda b: b.t[:, 0:tn], pm, st.t[:, 0:tn], st.r)
                store_fm(dst, drn, cb, t0, tn, st, key)
            return epi
        gemm_fm(c, w_in, O_DQ, 2048, act_fn, [hT.r], TOWN, epi_rope(qng, c_cosQ, c_sinQ, pd, DQT, 'DQT'))
        gemm_fm(c, w_in, O_IQ, 2048, act_fn, [hT.r], TOWN, epi_rope(None, c_cosIQ, c_sinIQ, pi, IQT, 'IQT'))

        act_tm = lambda kc, tt: hT.t[:, kc, tt * 128:(tt + 1) * 128]

        def epi_gv(tt, g0, gn, ps):
            st, key = stage()
            S.act(lambda e: e.activation(out=st.t[:, 0:gn], in_=ps.t[:, 0:gn], func=AF.Copy), reads=[ps.r], writes=[st.r])
            S.dma(STORE_ENG, key, lambda e: e.dma_start(out=VG[tt * 128:(tt + 1) * 128, g0:g0 + gn], in_=st.t[:, 0:gn]),
                  reads=[st.r], writes=[DR('VG')])
        for t4 in range(0, NOWN, 4):
            gemm_tm(c, w_in, O_GV, 2048, 16, lambda kc, tt, t4=t4: act_tm(kc, t4 + tt), [hT.r], 4,
                    lambda tt, g0, gn, ps, t4=t4: epi_gv(t4 + tt, g0, gn, ps))
        iwst = kb.sb("iwst", [128, NOWN, 16], F32)

        def epi_iw(tt, g0, gn, ps):
            S.act(lambda e: e.activation(out=iwst.t[:, tt, :], in_=ps.t[:, 0:16], func=AF.Copy, scale=float(16 ** -0.5 * 128 ** -0.5)),
                  reads=[ps.r], writes=[iwst.r])
        for t4 in range(0, NOWN, 4):
            gemm_tm(c, w_in, O_IW, 16, 16, lambda kc, tt, t4=t4: act_tm(kc, t4 + tt), [hT.r], 4,
                    lambda tt, g0, gn, ps, t4=t4: epi_iw(t4 + tt, g0, gn, ps))
        S.dma(STORE_ENG, 'iws', lambda e: e.dma_start(out=IWS.rearrange("(a p) h -> p a h", p=128), in_=iwst.t[:]),
              reads=[iwst.r], writes=[DR('IWS')])

    def phase_C():
        c = setup_common('C', npsf=7)
        gng = load_cols(c, "gng", gla_norm_g, 4)
        flg = kb.sb("flg", [128, 4], F32)
        S.dma('sp', 'flg', lambda e: e.dma_start(out=flg.t[:], in_=c_flags), writes=[flg.r])
        uf = kb.sb("uf", [128, 128], F32)
        ub = kb.sb("ub", [128, 128], BF16)
        S.dma('sp', 'uincl', lambda e: e.dma_start(out=uf.t[:], in_=c_uincl), writes=[uf.r])
        S.dve(lambda e: e.tensor_copy(out=ub.t[:], in_=uf.t[:]), reads=[uf.r], writes=[ub.r])
        NS = 2
        qT = [kb.sb(f"c_qT{i}", [128, 2, 128], BF16) for i in range(NS)]
        kT = [kb.sb(f"c_kT{i}", [128, 2, 128], BF16) for i in range(NS)]
        v = [kb.sb(f"c_v{i}", [128, 512], BF16) for i in range(NS)]
        cand = [kb.sb(f"c_cand{i}", [128, 4, 1024], BF16) for i in range(NS)]
        snf = kb.sb("c_snf", [128, 1024], F32)
        snb = kb.sb("c_snb", [128, 1024], BF16)
        aT = kb.sb("c_aT", [128, 128], BF16)
        sq = kb.sb("c_sq", [128, 512], BF16)
        rstd = kb.sb("c_rstd", [128, 128], F32)
        grt = [kb.sb(f"c_grt{i}", [128, 4, 128], BF16) for i in range(NS)]
        og = kb.sb("c_og", [128, 4, 128], F32)
        ogb = [kb.sb(f"c_ogb{i}", [128, 4, 128], BF16) for i in range(NS)]
        iters = [(i, hh) for i in range(NOWN) for hh in range(4)]
        ctxs = {}

        def stage1(it):
            i, hh = iters[it]
            s = it % NS
            q_, k_, v_, cd, gr_ = qT[s], kT[s], v[s], cand[s], grt[s]
            S.dma('sp', f'cq{s}', lambda e: e.dma_start(
                out=q_.t[:], in_=QT[hh * 256:(hh + 1) * 256, i * 128:(i + 1) * 128].rearrange("(a p) t -> p a t", p=128)),
                reads=[DR('QT')], writes=[q_.r])
            S.dma('sp', f'ck{s}', lambda e: e.dma_start(
                out=k_.t[:], in_=KGT[hh * 256:(hh + 1) * 256, i * 128:(i + 1) * 128].rearrange("(a p) t -> p a t", p=128)),
                reads=[DR('KGT')], writes=[k_.r])
            S.dma('sp', f'cv{s}', lambda e: e.dma_start(out=v_.t[:], in_=VG[i * 128:(i + 1) * 128, hh * 512:(hh + 1) * 512]),
                  reads=[DR('VG')], writes=[v_.r])
            S.dma('sp', f'cc{s}', lambda e: e.dma_start(
                out=cd.t[:], in_=SNAP[4 * i:4 * i + 4, :, hh * 1024:(hh + 1) * 1024].rearrange("c p f -> p c f")),
                reads=[DR('SNAP')], writes=[cd.r])
            S.dma('sp', f'cg{s}', lambda e: e.dma_start(
                out=gr_.t[:], in_=GRT[hh * 512:(hh + 1) * 512, i * 128:(i + 1) * 128].rearrange("(a p) t -> p a t", p=128)),
                reads=[DR('GRT')], writes=[gr_.r])
            S.dve(lambda e: e.tensor_scalar(out=snf.t[:], in0=cd.t[:, 0, :], scalar1=flg.t[:, 0:1], scalar2=None, op0=ALU.mult),
                  reads=[cd.r, flg.r], writes=[snf.r])
            for cc in range(1, 4):
                dst = snf if cc < 3 else snb
                S.dve(lambda e, cc=cc, dst=dst: e.scalar_tensor_tensor(out=dst.t[:], in0=cd.t[:, cc, :], scalar=flg.t[:, cc:cc + 1],
                                                                       in1=snf.t[:], op0=ALU.mult, op1=ALU.add),
                      reads=[cd.r, flg.r, snf.r], writes=[dst.r])
            psa = next_ps(c)
            for kh in range(2):
                S.pe(lambda e, kh=kh: e.matmul(out=psa.t[:, 0:128], lhsT=k_.t[:, kh, :], rhs=q_.t[:, kh, :], start=(kh == 0), stop=(kh == 1)),
                     reads=[q_.r, k_.r], writes=[psa.r])
            S.dve(lambda e: e.tensor_tensor(out=aT.t[:], in0=psa.t[:, 0:128], in1=ub.t[:], op=ALU.mult), reads=[psa.r, ub.r], writes=[aT.r])
            pso = next_ps(c)
            for ec in range(4):
                for kh in range(2):
                    S.pe(lambda e, ec=ec, kh=kh: e.matmul(out=pso.t[:, ec * 128:(ec + 1) * 128], lhsT=snb.t[:, kh * 512 + ec * 128:kh * 512 + (ec + 1) * 128],
                                                          rhs=q_.t[:, kh, :], start=(kh == 0), stop=False), reads=[snb.r, q_.r], writes=[pso.r])
                S.pe(lambda e, ec=ec: e.matmul(out=pso.t[:, ec * 128:(ec + 1) * 128], lhsT=v_.t[:, ec * 128:(ec + 1) * 128], rhs=aT.t[:],
                                               start=False, stop=True), reads=[v_.r, aT.r], writes=[pso.r])
            ctxs[it] = pso

        def stage2(it):
            i, hh = iters[it]
            s = it % NS
            gr_, ob = grt[s], ogb[s]
            pso = ctxs.pop(it)
            S.act(lambda e: e.activation(out=sq.t[:], in_=pso.t[:], func=AF.Square), reads=[pso.r], writes=[sq.r])
            pss = next_ps(c)
            for ec in range(4):
                S.pe(lambda e, ec=ec: e.matmul(out=pss.t[:, 0:128], lhsT=c.onesb.t[:], rhs=sq.t[:, ec * 128:(ec + 1) * 128], start=(ec == 0), stop=(ec == 3)),
                     reads=[c.onesb.r, sq.r], writes=[pss.r])
            S.act(lambda e: e.activation(out=rstd.t[:], in_=pss.t[:, 0:128], func=AF.Sqrt, scale=1.0 / 512, bias=c.cst.t[:, 0:1]),
                  reads=[pss.r, c.cst.r], writes=[rstd.r])
            S.dve(lambda e: e.reciprocal(out=rstd.t[:], in_=rstd.t[:]), reads=[rstd.r], writes=[rstd.r])
            for ec in range(4):
                S.dve(lambda e, ec=ec: e.scalar_tensor_tensor(out=og.t[:, ec, :], in0=pso.t[:, ec * 128:(ec + 1) * 128], scalar=gng.t[:, ec:ec + 1],
                                                              in1=rstd.t[:], op0=ALU.mult, op1=ALU.mult), reads=[pso.r, gng.r, rstd.r], writes=[og.r])
            S.dve(lambda e: e.tensor_tensor(out=ob.t[:], in0=og.t[:], in1=gr_.t[:], op=ALU.mult), reads=[og.r, gr_.r], writes=[ob.r])
            S.dma(STORE_ENG, f'cog{s}', lambda e: e.dma_start(
                out=OGT[hh * 512:(hh + 1) * 512, i * 128:(i + 1) * 128].rearrange("(a p) t -> p a t", p=128), in_=ob.t[:]),
                reads=[ob.r], writes=[DR('OGT')])

        stage1(0)
        for it in range(len(iters)):
            if it + 1 < len(iters):
                stage1(it + 1)
            stage2(it)

    def phase_proj(src, srcn, W, gate_off, addsrc, dst, dstn, tag):
        c = setup_common(tag, npsf=6)
        make_wslots(c, 3)
        aT = kb.sb("p_aT", [128, 16, TOWN], BF16)
        for q in range(4):
            S.dma('sp', f'pa{q}', lambda e, q=q: e.dma_start(out=aT.t[:, q * 4:(q + 1) * 4, :],
                                                            in_=src[q * 512:(q + 1) * 512, :].rearrange("(a p) t -> p a t", p=128)),
                  reads=[DR(srcn)], writes=[aT.r])
        gt = [kb.sb(f"p_gt{i}", [128, 512], BF16) for i in range(3)]
        ad = [kb.sb(f"p_ad{i}", [128, 512], BF16) for i in range(3)]
        tmp = [kb.sb(f"p_tmp{i}", [128, 512], F32) for i in range(2)]
        st = [kb.sb(f"p_st{i}", [128, 512], BF16) for i in range(3)]
        cnt = [0]

        def epi(cb, t0, tn, ps, m):
            s = cnt[0] % 3
            cnt[0] += 1
            g_, a_, o_ = gt[s], ad[s], st[s]
            S.dma('sp', f'pg{s}', lambda e: e.dma_start(out=g_.t[:, 0:tn], in_=GT[gate_off + cb * 128:gate_off + (cb + 1) * 128, t0:t0 + tn]),
                  reads=[DR('GT')], writes=[g_.r])
            if addsrc is None:
                S.dve(lambda e: e.tensor_tensor(out=o_.t[:, 0:tn], in0=ps.t[:, 0:tn], in1=g_.t[:, 0:tn], op=ALU.mult),
                      reads=[ps.r, g_.r], writes=[o_.r])
            else:
                t_ = tmp[cnt[0] % 2]
                S.dma('sp', f'pd{s}', lambda e: e.dma_start(out=a_.t[:, 0:tn], in_=addsrc[0][cb * 128:(cb + 1) * 128, t0:t0 + tn]),
                      reads=[DR(addsrc[1])], writes=[a_.r])
                S.dve(lambda e: e.tensor_tensor(out=t_.t[:, 0:tn], in0=ps.t[:, 0:tn], in1=g_.t[:, 0:tn], op=ALU.mult),
                      reads=[ps.r, g_.r], writes=[t_.r])
                S.dve(lambda e: e.tensor_tensor(out=o_.t[:, 0:tn], in0=t_.t[:, 0:tn], in1=a_.t[:, 0:tn], op=ALU.add),
                      reads=[t_.r, a_.r], writes=[o_.r])
            S.dma(STORE_ENG, f'po{s}', lambda e: e.dma_start(out=dst[cb * 128:(cb + 1) * 128, t0:t0 + tn], in_=o_.t[:, 0:tn]),
                  reads=[o_.r], writes=[DR(dstn)])
        gemm_fm(c, W, 0, 2048, lambda kc, t0, tn: aT.t[:, kc, t0:t0 + tn], [aT.r], TOWN, epi)

    def phase_E():
        c = setup_common('E', npsf=7)
        ikt = kb.sb("e_ikt", [128, SEQ], BF16)
        for q in range(4):
            S.dma('sp', f'eik{q}', lambda e, q=q: e.dma_start(out=ikt.t[:, q * 2048:(q + 1) * 2048], in_=IKT[:, q * 2048:(q + 1) * 2048]),
                  reads=[DR('IKT')], writes=[ikt.r])
        mskd = kb.sb("e_mskd", [128, 512], F32)
        S.dma('sp', 'emk', lambda e: e.dma_start(out=mskd.t[:], in_=c_maskd), writes=[mskd.r])
        pw2 = kb.sb("e_pw2", [128, NIT + 2], F32)
        for k in range(NIT + 2):
            S.dve(lambda e, k=k: e.memset(pw2.t[:, k:k + 1], float(2.0 ** -(k + 1))), writes=[pw2.r])
        iq = [kb.sb(f"e_iq{i}", [128, 16, 128], BF16) for i in range(2)]
        iw = [kb.sb(f"e_iw{i}", [128, 16], F32) for i in range(2)]
        wab = [kb.sb(f"e_wab{i}", [128, 16], F32) for i in range(2)]
        wsg = [kb.sb(f"e_wsg{i}", [128, 16], F32) for i in range(2)]
        dg = [kb.sb(f"e_dg{i}", [128, 16, 128], BF16) for i in range(2)]
        rb = [kb.sb(f"e_rb{i}", [128, 512], BF16) for i in range(4)]
        Sb = [kb.sb(f"e_S{i}", [128, SEQ], F32) for i in range(2)]
        junk = kb.sb("e_junk", [128, SEQ], BF16)
        Mb = kb.sb("e_M", [128, SEQ], BF16)
        MT_ = [kb.sb(f"e_MT{i}", [128, NTALL, 128], BF16) for i in range(2)]
        bsl = [kb.sb(f"e_bs{i}", [128, 8], F32) for i in range(2)]
        skl = [kb.sb(f"e_sk{i}", [128, NIT + 2], F32) for i in range(2)]
        midl = [kb.sb(f"e_mid{i}", [128, NIT + 2], F32) for i in range(2)]
        cntl = [kb.sb(f"e_cnt{i}", [128, NIT + 2], F32) for i in range(2)]
        gl = [kb.sb(f"e_g{i}", [128, NIT + 2], F32) for i in range(2)]
        dq = [kb.sb(f"e_dq{i}", [128, 512], BF16) for i in range(2)]
        ktc = [kb.sb(f"e_ktc{i}", [128, 512], BF16) for i in range(3)]
        vc = [kb.sb(f"e_vc{i}", [128, 4, 128], BF16) for i in range(3)]
        eb = [kb.sb(f"e_eb{i}", [128, 512], BF16) for i in range(4)]
        rden = kb.sb("e_rden", [128, 512], F32)
        od = [kb.sb(f"e_od{i}", [128, 512], BF16) for i in range(2)]
        psS = c.ps[6]
        psO = c.ps[5]
        psD = c.ps[4]
        lg = c.ps[0:4]
        pbT = c.pb[0]
        hsc = float(128 ** -0.5)
        BIGM = 30000.0
        mbias = kb.sb("e_mbias", [128, 1], F32)
        S.dve(lambda e: e.memset(mbias.t[:], -BIGM), writes=[mbias.r])
        cnts = {'rb': 0, 'eb': 0, 'kv': 0, 'dq': 0, 'od': 0}

        def idx_steps(i):
            steps = []
            iq_, iw_, wab_, wsg_, dg_, Sb_ = iq[i % 2], iw[i % 2], wab[i % 2], wsg[i % 2], dg[i % 2], Sb[i % 2]

            def prep(bank):
                S.dma('sp', f'eiq{i % 2}', lambda e: e.dma_start(out=iq_.t[:], in_=IQT[:, i * 128:(i + 1) * 128].rearrange("(h p) t -> p h t", p=128)),
                      reads=[DR('IQT')], writes=[iq_.r])
                S.dma('sp', f'eiw{i % 2}', lambda e: e.dma_start(out=iw_.t[:], in_=IWS[i * 128:(i + 1) * 128, :]), reads=[DR('IWS')], writes=[iw_.r])
                S.act(lambda e: e.activation(out=wab_.t[:], in_=iw_.t[:], func=AF.Abs), reads=[iw_.r], writes=[wab_.r])
                S.act(lambda e: e.activation(out=wsg_.t[:], in_=iw_.t[:], func=AF.Sign), reads=[iw_.r], writes=[wsg_.r])
                for h in range(16):
                    S.pool(lambda e, h=h: e.tensor_scalar(out=dg_.t[:, h, :], in0=c.identb.t[:], scalar1=wsg_.t[:, h:h + 1], scalar2=None, op0=ALU.mult),
                           reads=[c.identb.r, wsg_.r], writes=[dg_.r])
            steps.append((prep, None, False))
            for k4 in range(i + 1):
                for h in range(16):
                    def A(bank, h=h, k4=k4):
                        S.pe(lambda e: e.matmul(out=bank.t[:], lhsT=iq_.t[:, h, :], rhs=ikt.t[:, k4 * 512:(k4 + 1) * 512], start=True, stop=True),
                             reads=[iq_.r, ikt.r], writes=[bank.r])

                    def BC(bank, h=h, k4=k4):
                        r_ = rb[cnts['rb'] % 4]
                        cnts['rb'] += 1
                        S.act(lambda e: e.activation(out=r_.t[:], in_=bank.t[:], func=AF.Relu, scale=wab_.t[:, h:h + 1]),
                              reads=[bank.r, wab_.r], writes=[r_.r])
                        S.pe(lambda e: e.matmul(out=psS.t[:], lhsT=dg_.t[:, h, :], rhs=r_.t[:], start=(h == 0), stop=(h == 15)),
                             reads=[dg_.r, r_.r], writes=[psS.r])
                        if h == 15:
                            S.act(lambda e: e.activation(out=Sb_.t[:, k4 * 512:(k4 + 1) * 512], in_=psS.t[:], func=AF.Copy), reads=[psS.r], writes=[Sb_.r])
                    steps.append((A, BC))
            return steps

        def bis_steps(i):
            steps = []
            nk = 512 * (i + 1)
            nkb = 4 * (i + 1)
            Sb_, bs, sk, mid, cnt, g_, MTo = Sb[i % 2], bsl[i % 2], skl[i % 2], midl[i % 2], cntl[i % 2], gl[i % 2], MT_[i % 2]

            def init(bank):
                S.dve(lambda e: e.tensor_reduce(out=bs.t[:, 0:1], in_=Sb_.t[:, 0:nk], axis=AX.X, op=ALU.min), reads=[Sb_.r], writes=[bs.r])
                S.dve(lambda e: e.tensor_reduce(out=bs.t[:, 1:2], in_=Sb_.t[:, 0:nk], axis=AX.X, op=ALU.max), reads=[Sb_.r], writes=[bs.r])
                S.dve(lambda e: e.tensor_tensor(out=Sb_.t[:, nk - 512:nk], in0=Sb_.t[:, nk - 512:nk], in1=mskd.t[:], op=ALU.add),
                      reads=[Sb_.r, mskd.r], writes=[Sb_.r])
                S.dve(lambda e: e.tensor_scalar(out=bs.t[:, 0:1], in0=bs.t[:, 0:1], scalar1=-1.0, scalar2=None, op0=ALU.add), reads=[bs.r], writes=[bs.r])
                S.dve(lambda e: e.tensor_tensor(out=bs.t[:, 2:3], in0=bs.t[:, 1:2], in1=bs.t[:, 0:1], op=ALU.subtract), reads=[bs.r], writes=[bs.r])
                S.dve(lambda e: e.tensor_scalar(out=sk.t[:], in0=pw2.t[:], scalar1=bs.t[:, 2:3], scalar2=None, op0=ALU.mult), reads=[pw2.r, bs.r], writes=[sk.r])
                S.dve(lambda e: e.tensor_tensor(out=mid.t[:, 0:1], in0=bs.t[:, 0:1], in1=sk.t[:, 0:1], op=ALU.add), reads=[bs.r, sk.r], writes=[mid.r])
            steps.append((None, init))
            for k in range(NIT):
                def it(bank, k=k):
                    S.dve(lambda e: e.tensor_scalar(out=junk.t[:, 0:nk], in0=Sb_.t[:, 0:nk], scalar1=mid.t[:, k:k + 1], scalar2=0.0, op0=ALU.is_ge, op1=ALU.add,
                                                    accum_out=cnt.t[:, k:k + 1]), reads=[Sb_.r, mid.r], writes=[junk.r, cnt.r])
                    S.dve(lambda e: e.tensor_scalar(out=g_.t[:, k:k + 1], in0=cnt.t[:, k:k + 1], scalar1=float(TOPK), scalar2=sk.t[:, k:k + 1],
                                                    op0=ALU.is_ge, op1=ALU.mult), reads=[cnt.r, sk.r], writes=[g_.r])
                    kk = k + 1 if k < NIT - 1 else k
                    S.dve(lambda e: e.scalar_tensor_tensor(out=mid.t[:, k + 1:k + 2], in0=mid.t[:, k:k + 1], scalar=sk.t[:, kk:kk + 1], in1=g_.t[:, k:k + 1],
                                                           op0=ALU.subtract, op1=ALU.add), reads=[mid.r, sk.r, g_.r], writes=[mid.r])
                steps.append((None, it))

            def mk(bank):
                S.dve(lambda e: e.tensor_scalar(out=Mb.t[:, 0:nk], in0=Sb_.t[:, 0:nk], scalar1=mid.t[:, NIT:NIT + 1], scalar2=None, op0=ALU.is_ge),
                      reads=[Sb_.r, mid.r], writes=[Mb.r])
            steps.append((None, mk))
            for k8 in range(0, nkb, 8):
                def tr(bank, k8=k8):
                    nn = min(8, nkb - k8)
                    for k in range(nn):
                        S.pe(lambda e, k=k: e.transpose(out=pbT.t[:, k * 128:(k + 1) * 128], in_=Mb.t[:, (k8 + k) * 128:(k8 + k + 1) * 128], identity=c.identb.t[:]),
                             reads=[Mb.r, c.identb.r], writes=[pbT.r])
                    S.act(lambda e: e.activation(out=MTo.t[:, k8:k8 + nn, :], in_=pbT.t[:, 0:nn * 128].rearrange("p (a b) -> p a b", b=128), func=AF.Identity,
                                                 scale=BIGM, bias=mbias.t[:, 0:1]), reads=[pbT.r, mbias.r], writes=[MTo.r])
                steps.append((None, tr))
            return steps

        def attn_steps(i):
            steps = []
            nkb = 4 * (i + 1)
            MTi = MT_[i % 2]
            for gq in range(4):
                st = {}
                for kbi in range(nkb):
                    k4, sub = kbi // 4, kbi % 4

                    def A(bank, gq=gq, kbi=kbi, k4=k4, sub=sub, st=st):
                        if kbi == 0:
                            st['dq'] = dq[cnts['dq'] % 2]
                            dslot = cnts['dq'] % 2
                            cnts['dq'] += 1
                            dq_ = st['dq']
                            S.dma('sp', f'edq{dslot}', lambda e: e.dma_start(
                                out=dq_.t[:].rearrange("p (h t) -> p h t", t=128), in_=DQT[gq * 512:(gq + 1) * 512, i * 128:(i + 1) * 128].rearrange("(h p) t -> p h t", p=128)),
                                reads=[DR('DQT')], writes=[dq_.r])
                        if sub == 0:
                            s3 = cnts['kv'] % 3
                            cnts['kv'] += 1
                            st[('kt', k4)] = ktc[s3]
                            st[('v', k4)] = vc[s3]
                            kt_, v_ = ktc[s3], vc[s3]
                            S.dma('sp', f'ekt{s3}', lambda e: e.dma_start(out=kt_.t[:], in_=KT[gq, :, k4 * 512:(k4 + 1) * 512]), reads=[DR('KT')], writes=[kt_.r])
                            S.dma('sp', f'evc{s3}', lambda e: e.dma_start(
                                out=v_.t[:], in_=VV[k4 * 512:(k4 + 1) * 512, gq * 128:(gq + 1) * 128].rearrange("(a p) d -> p a d", p=128)),
                                reads=[DR('VV')], writes=[v_.r])
                        kt_, dq_ = st[('kt', k4)], st['dq']
                        S.pe(lambda e: e.matmul(out=bank.t[:], lhsT=kt_.t[:, sub * 128:(sub + 1) * 128], rhs=dq_.t[:], start=True, stop=False),
                             reads=[kt_.r, dq_.r], writes=[bank.r])
                        for hq in range(4):
                            S.pe(lambda e, hq=hq: e.matmul(out=bank.t[:, hq * 128:(hq + 1) * 128], lhsT=c.identb.t[:], rhs=MTi.t[:, kbi, :], start=False, stop=(hq == 3)),
                                 reads=[c.identb.r, MTi.r], writes=[bank.r])

                    def BC(bank, gq=gq, kbi=kbi, k4=k4, sub=sub, st=st):
                        e_ = eb[cnts['eb'] % 4]
                        cnts['eb'] += 1
                        v_ = st[('v', k4)]
                        S.act(lambda e: e.activation(out=e_.t[:], in_=bank.t[:], func=AF.Exp, scale=hsc), reads=[bank.r], writes=[e_.r])
                        S.pe(lambda e: e.matmul(out=psO.t[:], lhsT=v_.t[:, sub, :], rhs=e_.t[:], start=(kbi == 0), stop=(kbi == nkb - 1)),
                             reads=[v_.r, e_.r], writes=[psO.r])
                        S.pe(lambda e: e.matmul(out=psD.t[:], lhsT=c.onesb.t[:], rhs=e_.t[:], start=(kbi == 0), stop=(kbi == nkb - 1)),
                             reads=[c.onesb.r, e_.r], writes=[psD.r])
                        if kbi == nkb - 1:
                            od_ = od[cnts['od'] % 2]
                            oslot = cnts['od'] % 2
                            cnts['od'] += 1
                            S.dve(lambda e: e.reciprocal(out=rden.t[:], in_=psD.t[:]), reads=[psD.r], writes=[rden.r])
                            S.dve(lambda e: e.tensor_tensor(out=od_.t[:], in0=psO.t[:], in1=rden.t[:], op=ALU.mult), reads=[psO.r, rden.r], writes=[od_.r])
                            S.dma(STORE_ENG, f'eod{oslot}', lambda e: e.dma_start(
                                out=ODT[gq * 512:(gq + 1) * 512, i * 128:(i + 1) * 128].rearrange("(h p) t -> p h t", p=128),
                                in_=od_.t[:].rearrange("p (h t) -> p h t", t=128)), reads=[od_.r], writes=[DR('ODT')])
                    steps.append((A, BC))
            return steps

        def merge(lists):
            lists = [l for l in lists if l]
            pos = [0] * len(lists)
            outl = []
            while True:
                best, bf_ = None, None
                for li, l in enumerate(lists):
                    if pos[li] < len(l):
                        f = (pos[li] + 1) / len(l)
                        if best is None or f < bf_:
                            best, bf_ = li, f
                if best is None:
                    break
                outl.append(lists[best][pos[best]])
                pos[best] += 1
            return outl

        DEPTH = 3
        bankctr = [0]

        def run_steps(steps):
            assigned = [None] * len(steps)
            for p in range(len(steps) + DEPTH):
                if p < len(steps):
                    A = steps[p][0]
                    if A is not None:
                        if len(steps[p]) > 2 and not steps[p][2]:
                            A(None)
                        else:
                            assigned[p] = lg[bankctr[0] % 4]
                            bankctr[0] += 1
                            A(assigned[p])
                q = p - DEPTH
                if q >= 0:
                    BC = steps[q][1]
                    if BC is not None:
                        BC(assigned[q])

        for ss in range(NOWN + 2):
            lists = []
            if ss < NOWN:
                lists.append(idx_steps(ss))
            if 1 <= ss <= NOWN:
                lists.append(bis_steps(ss - 1))
            if ss >= 2:
                lists.append(attn_steps(ss - 2))
            run_steps(merge(lists))

    def phase_G():
        c = setup_common('G', npsf=8)
        make_wslots(c, 3)
        mT = kb.sb("g_mT", [128, 16, TOWN], BF16)
        for q in range(4):
            S.dma('sp', f'ga{q}', lambda e, q=q: e.dma_start(out=mT.t[:, q * 4:(q + 1) * 4, :],
                                                            in_=MT2[q * 512:(q + 1) * 512, :].rearrange("(a p) t -> p a t", p=128)),
                  reads=[DR('MT2')], writes=[mT.r])
        xs = [kb.sb(f"g_xs{i}", [128, 512], F32) for i in range(3)]
        os_ = [kb.sb(f"g_os{i}", [128, 512], F32) for i in range(3)]
        cnt = [0]

        def epi(tt, g0, gn, ps):
            s = cnt[0] % 3
            cnt[0] += 1
            x_, o_ = xs[s], os_[s]
            S.dma('sp', f'gx{s}', lambda e: e.dma_start(out=x_.t[:], in_=xown[tt * 128:(tt + 1) * 128, g0:g0 + gn]), writes=[x_.r])
            S.dve(lambda e: e.tensor_tensor(out=o_.t[:], in0=ps.t[:], in1=x_.t[:], op=ALU.add), reads=[ps.r, x_.r], writes=[o_.r])
            S.dma(STORE_ENG, f'go{s}', lambda e: e.dma_start(out=X1[tt * 128:(tt + 1) * 128, g0:g0 + gn], in_=o_.t[:]), reads=[o_.r], writes=[DR('X1')])
        for t4 in range(0, NOWN, 4):
            gemm_tm(c, w_out, 0, 2048, 16, lambda kc, tt, t4=t4: mT.t[:, kc, (t4 + tt) * 128:(t4 + tt + 1) * 128], [mT.r], 4,
                    lambda tt, g0, gn, ps, t4=t4: epi(t4 + tt, g0, gn, ps))

    def phase_H():
        c = setup_common('H', npsf=6)
        make_wslots(c, 2)
        n = make_norm_ctx(c, 1)
        g2 = load_cols(c, "g2col", norm2_g, 16)
        h2 = kb.sb("h_h2T", [128, 16, TOWN], BF16)
        for i in range(NOWN):
            norm_T(c, n, X1[i * 128:(i + 1) * 128, :], g2, lambda k0, k1, i=i: h2.t[:, k0:k1, i * 128:(i + 1) * 128], h2.r)
        fT = kb.sb("h_fT", [128, 64, 512], BF16)
        rl = [kb.sb(f"h_rl{i}", [128, 512], F32) for i in range(2)]
        xs = [kb.sb(f"h_xs{i}", [128, 512], F32) for i in range(3)]
        os_ = [kb.sb(f"h_os{i}", [128, 512], F32) for i in range(3)]
        cnt = [0]
        for q in range(4):
            def epi1(cb, t0, tn, ps, m):
                r_ = rl[cnt[0] % 2]
                cnt[0] += 1
                S.act(lambda e: e.activation(out=r_.t[:], in_=ps.t[:], func=AF.Relu), reads=[ps.r], writes=[r_.r])
                S.dve(lambda e: e.tensor_tensor(out=fT.t[:, cb, :], in0=r_.t[:], in1=r_.t[:], op=ALU.mult), reads=[r_.r], writes=[fT.r])
            gemm_fm(c, w_ff1, 0, DFF, lambda kc, t0, tn, q=q: h2.t[:, kc, q * 512 + t0:q * 512 + t0 + tn], [h2.r], 512, epi1)

            def epi2(tt, g0, gn, ps, q=q):
                s = cnt[0] % 3
                cnt[0] += 1
                x_, o_ = xs[s], os_[s]
                row = (q * 4 + tt) * 128
                S.dma('sp', f'hx{s}', lambda e: e.dma_start(out=x_.t[:], in_=X1[row:row + 128, g0:g0 + gn]), reads=[DR('X1')], writes=[x_.r])
                S.dve(lambda e: e.tensor_tensor(out=o_.t[:], in0=ps.t[:], in1=x_.t[:], op=ALU.add), reads=[ps.r, x_.r], writes=[o_.r])
                S.dma(STORE_ENG, f'ho{s}', lambda e: e.dma_start(out=out[row:row + 128, g0:g0 + gn], in_=o_.t[:]), reads=[o_.r], writes=[DR('out')])
            gemm_tm(c, w_ff2, 0, 2048, 64, lambda kc, tt: fT.t[:, kc, tt * 128:(tt + 1) * 128], [fT.r], 4, epi2)

    def run_phase(fn, *a):
        with ExitStack() as es:
            kb.es = es
            fn(*a)
            n = S.emit(nc)
        dr.clear()
        return n

    stats = {}
    import os
    phases = os.environ.get("KPHASES", "A1,A2,B,C,D,E,F,G,H").split(",")
    if "A1" in phases:
        stats['A1'] = run_phase(phase_A1)
    if "A2" in phases:
        stats['A2'] = run_phase(phase_A2)
    if "B" in phases:
        stats['B'] = run_phase(phase_B)
    if "C" in phases:
        stats['C'] = run_phase(phase_C)
    if "D" in phases:
        stats['D'] = run_phase(phase_proj, OGT, 'OGT', w_proj_gla, 0, None, MT, 'MT', 'D')
    if "E" in phases:
        stats['E'] = run_phase(phase_E)
    if "F" in phases:
        stats['F'] = run_phase(phase_proj, ODT, 'ODT', w_proj_dsa, 2048, (MT, 'MT'), MT2, 'MT2', 'F')
    if "G" in phases:
        stats['G'] = run_phase(phase_G)
    if "H" in phases:
        stats['H'] = run_phase(phase_H)
    return nc, stats


def _rope_tables(pos, rot_dim, full_dim=128):
    half = rot_dim // 2
    inv = (10000.0 ** (-np.arange(half, dtype=np.float32) * 2.0 / rot_dim)).astype(np.float32)
    ang = pos.astype(np.float32)[None, :] * inv[:, None]
    cos = np.ones((full_dim, pos.shape[0]), np.float32)
    sin = np.zeros((full_dim, pos.shape[0]), np.float32)
    cos[0:half] = np.cos(ang)
    cos[half:rot_dim] = np.cos(ang)
    sin[0:half] = np.sin(ang)
    sin[half:rot_dim] = np.sin(ang)
    return cos, sin


def _pmatT(rot_dim, full_dim=128):
    half = rot_dim // 2
    P = np.zeros((full_dim, full_dim), np.float32)
    for d in range(half):
        P[d, d + half] = -1.0
        P[d + half, d] = 1.0
    return np.ascontiguousarray(P.T)


def make_in_maps(inputs):
    x = np.asarray(inputs['x'], np.float32)
    f = lambda k: np.ascontiguousarray(np.asarray(inputs[k], np.float32)[0])
    row = lambda k: np.ascontiguousarray(np.asarray(inputs[k], np.float32)[0][None, :])
    shared = {
        'w_in': f('w_in'), 'gla_wg2': f('gla_wg2'), 'gla_bg': row('gla_bg'), 'gla_norm_g': row('gla_norm_g'),
        'w_proj_gla': f('w_proj_gla'), 'q_norm_g': row('q_norm_g'), 'k_norm_g': row('k_norm_g'),
        'idx_k_norm_g': row('idx_k_norm_g'), 'w_proj_dsa': f('w_proj_dsa'), 'b_gate': row('b_gate'),
        'w_out': f('w_out'), 'norm1_g': row('norm1_g'), 'norm2_g': row('norm2_g'), 'w_ff1': f('w_ff1'), 'w_ff2': f('w_ff2'),
    }
    jj = np.arange(128)
    shared['c_ident'] = np.eye(128, dtype=np.float32)
    shared['c_uincl'] = (jj[:, None] <= jj[None, :]).astype(np.float32)
    shared['c_ustrict'] = (jj[:, None] > jj[None, :]).astype(np.float32)
    shared['c_pdsaT'] = _pmatT(128)
    shared['c_pidxT'] = _pmatT(64)
    posall = np.arange(SEQ)
    shared['c_cosK'], shared['c_sinK'] = _rope_tables(posall, 128)
    shared['c_cosIK'], shared['c_sinIK'] = _rope_tables(posall, 64)
    maps = []
    for core in range(8):
        b, j = core // 4, core % 4
        m = dict(shared)
        m['xfull'] = np.ascontiguousarray(x[b])
        tiles = [4 * i + j for i in range(NOWN)]
        rows = np.concatenate([np.arange(t * 128, (t + 1) * 128) for t in tiles])
        m['xown'] = np.ascontiguousarray(x[b][rows])
        fl = np.zeros((128, 4), np.float32)
        fl[:, j] = 1.0
        m['c_flags'] = fl
        t = np.arange(128)[:, None]
        sp = np.arange(512)[None, :]
        m['c_maskd'] = np.where(sp <= j * 128 + t, 0.0, NEG).astype(np.float32)
        m['c_cosQ'], m['c_sinQ'] = _rope_tables(rows, 128)
        m['c_cosIQ'], m['c_sinIQ'] = _rope_tables(rows, 64)
        maps.append(m)
    return maps


_CACHE = {}


def kernel(**inputs):
    if 'nc' not in _CACHE:
        _CACHE['nc'] = build_program(False)[0]
    nc = _CACHE['nc']
    maps = make_in_maps(inputs)
    res = run_bass_kernel_spmd(nc, maps, core_ids=list(range(8)))
    outf = np.zeros((2, SEQ, D), np.float32)
    for core in range(8):
        b, j = core // 4, core % 4
        o = np.asarray(res.results[core]['out'], np.float32)
        for i in range(NOWN):
            t = 4 * i + j
            outf[b, t * 128:(t + 1) * 128, :] = o[i * 128:(i + 1) * 128, :]
    return outf
```

```python
import math
import numpy as np
from contextlib import ExitStack
import concourse.bass as bass
import concourse.mybir as mybir
from concourse.bass_utils import run_bass_kernel_spmd

F32 = mybir.dt.float32
BF16 = mybir.dt.bfloat16
AF = mybir.ActivationFunctionType
ALU = mybir.AluOpType
AX = mybir.AxisListType

D = 2048
SEQ = 8192
NTALL = 64
NOWN = 16
TOWN = 2048
DIN = 15520
DFF = 8192
O_GQ, O_GK, O_GV, O_GR, O_GLR, O_DQ, O_DK, O_DV, O_IQ, O_IK, O_IW, O_GATES = (
    0, 1024, 2048, 4096, 6144, 6160, 8208, 8720, 9232, 11280, 11408, 11424)
EPS = 1e-6
NEG = -1.0e30
TOPK = 256
NIT = 18

EPOCH = 30000
DMA_EPOCH = 1800
ENGS = ('pe', 'act', 'dve', 'pool', 'sp')


class R:
    __slots__ = ('name', 'w', 'rs')

    def __init__(self, name=''):
        self.name = name
        self.w = None
        self.rs = []


class Op:
    __slots__ = ('eng', 'fn', 'deps', 'key', 'kidx', 'sig', 'sidx')

    def __init__(self, eng, fn, key):
        self.eng = eng
        self.fn = fn
        self.deps = []
        self.key = key
        self.kidx = 0
        self.sig = False
        self.sidx = 0


class Sched:
    _blkc = [0]

    def __init__(self):
        self.ops = {e: [] for e in ENGS}
        self.kcount = {}
        self.klast = {}
        self.allres = []

    def res(self, name=''):
        r = R(name)
        self.allres.append(r)
        return r

    def add(self, eng, fn, reads=(), writes=(), key=None):
        op = Op(eng, fn, key)
        deps = {}
        for r in reads:
            if r.w is not None:
                deps[id(r.w)] = r.w
        for w in writes:
            if w.w is not None:
                deps[id(w.w)] = w.w
            for o in w.rs:
                deps[id(o)] = o
        if key is not None:
            p = self.klast.get(key)
            if p is not None:
                deps[id(p)] = p
            self.klast[key] = op
            self.kcount[key] = self.kcount.get(key, 0) + 1
            op.kidx = self.kcount[key]
        for d in deps.values():
            if d is op:
                continue
            if d.key is None and d.eng == 'pe' and eng == 'pe' and key is None:
                continue
            op.deps.append(d)
            if d.key is None:
                d.sig = True
        for r in reads:
            r.rs.append(op)
        for w in writes:
            w.w = op
            w.rs = []
        self.ops[eng].append(op)
        return op

    def pe(self, fn, reads=(), writes=()):
        return self.add('pe', fn, reads, writes)

    def act(self, fn, reads=(), writes=()):
        return self.add('act', fn, reads, writes)

    def dve(self, fn, reads=(), writes=()):
        return self.add('dve', fn, reads, writes)

    def pool(self, fn, reads=(), writes=()):
        return self.add('pool', fn, reads, writes)

    def dma(self, eng, key, fn, reads=(), writes=()):
        return self.add(eng, fn, reads, writes, key=key)

    def emit(self, nc):
        blk = self._blkc[0]
        self._blkc[0] += 1
        nsig = {}
        for e in ENGS:
            c = 0
            if self.ops[e] and self.ops[e][-1].key is None:
                self.ops[e][-1].sig = True
            for op in self.ops[e]:
                if op.key is None and op.sig:
                    c += 1
                    op.sidx = c
            nsig[e] = c
        handles = []

        def alloc(name):
            h = nc.alloc_semaphore(name=name)
            handles.append(h)
            return h

        if True:
            esem = {}
            for e in ENGS:
                n = (nsig[e] + EPOCH - 1) // EPOCH
                esem[e] = [alloc(f"s{blk}_{e}_{i}") for i in range(n)]
            ksem = {}
            for k, cnt in self.kcount.items():
                n = (cnt + DMA_EPOCH - 1) // DMA_EPOCH
                ksem[k] = [alloc(f"k{blk}_{k}_{i}") for i in range(n)]

            def sem_of(d):
                if d.key is None:
                    i = d.sidx - 1
                    return esem[d.eng][i // EPOCH], (i % EPOCH) + 1
                i = d.kidx - 1
                return ksem[d.key][i // DMA_EPOCH], 16 * ((i % DMA_EPOCH) + 1)

            def run(ename, eng):
                waited = {}
                for op in self.ops[ename]:
                    for d in op.deps:
                        s, v = sem_of(d)
                        if waited.get(id(s), 0) >= v:
                            continue
                        waited[id(s)] = v
                        eng.wait_ge(s, v)
                    ins = op.fn(eng)
                    if op.key is not None:
                        s, v = sem_of(op)
                        ins.then_inc(s, 16)
                    elif op.sig:
                        s, v = sem_of(op)
                        ins.then_inc(s, 1)
                for e2 in ENGS:
                    if nsig[e2] > 0:
                        i = nsig[e2] - 1
                        s, v = esem[e2][i // EPOCH], (i % EPOCH) + 1
                        if waited.get(id(s), 0) < v:
                            eng.wait_ge(s, v)
                for k in self.kcount:
                    s, v = sem_of(self.klast[k])
                    if waited.get(id(s), 0) < v:
                        eng.wait_ge(s, v)

            with nc.Block() as block:
                block.tensor(lambda e: run('pe', e))
                block.scalar(lambda e: run('act', e))
                block.vector(lambda e: run('dve', e))
                block.gpsimd(lambda e: run('pool', e))
                block.sync(lambda e: run('sp', e))
        nc.all_engine_barrier()
        nc.clear_and_free_semaphores(handles)
        nc.all_engine_barrier()
        n_ops = {e: len(self.ops[e]) for e in ENGS}
        self.ops = {e: [] for e in ENGS}
        self.kcount = {}
        self.klast = {}
        for r in self.allres:
            r.w = None
            r.rs = []
        self.allres = []
        return n_ops


class Buf:
    __slots__ = ('t', 'r')

    def __init__(self, t, r):
        self.t = t
        self.r = r


STORE_ENG = 'act'


class K:
    def __init__(self, nc, debug=False):
        self.nc = nc
        self.S = Sched()
        self.es = None
        self.uid = 0
        self.debug = debug

    def sb(self, name, shape, dt):
        self.uid += 1
        t = self.es.enter_context(self.nc.sbuf_tensor(f"{name}_{self.uid}", shape, dt))
        return Buf(t, self.S.res(name))

    def psum(self, name, shape, dt):
        self.uid += 1
        t = self.es.enter_context(self.nc.psum_tensor(f"{name}_{self.uid}", shape, dt))
        return Buf(t, self.S.res(name))


def build_program(debug=False):
    nc = bass.Bass("TRN2", target_bir_lowering=False)
    kb = K(nc, debug)
    S = kb.S

    def din(name, shape, dt=F32):
        return nc.dram_tensor(name, shape, dt, kind="ExternalInput").ap()

    def dscr(name, shape, dt=BF16):
        kind = "ExternalOutput" if debug else "Internal"
        return nc.dram_tensor(name, shape, dt, kind=kind).ap()

    xfull = din("xfull", [SEQ, D])
    xown = din("xown", [TOWN, D])
    w_in = din("w_in", [D, DIN])
    gla_wg2 = din("gla_wg2", [16, 1024])
    gla_bg = din("gla_bg", [1, 1024])
    gla_norm_g = din("gla_norm_g", [1, 512])
    w_proj_gla = din("w_proj_gla", [D, D])
    q_norm_g = din("q_norm_g", [1, 128])
    k_norm_g = din("k_norm_g", [1, 128])
    idx_k_norm_g = din("idx_k_norm_g", [1, 128])
    w_proj_dsa = din("w_proj_dsa", [D, D])
    b_gate = din("b_gate", [1, 4096])
    w_out = din("w_out", [D, D])
    norm1_g = din("norm1_g", [1, D])
    norm2_g = din("norm2_g", [1, D])
    w_ff1 = din("w_ff1", [D, DFF])
    w_ff2 = din("w_ff2", [DFF, D])
    c_ident = din("c_ident", [128, 128])
    c_flags = din("c_flags", [128, 4])
    c_maskd = din("c_maskd", [128, 512])
    c_uincl = din("c_uincl", [128, 128])
    c_ustrict = din("c_ustrict", [128, 128])
    c_pdsaT = din("c_pdsaT", [128, 128])
    c_pidxT = din("c_pidxT", [128, 128])
    c_cosK = din("c_cosK", [128, SEQ])
    c_sinK = din("c_sinK", [128, SEQ])
    c_cosIK = din("c_cosIK", [128, SEQ])
    c_sinIK = din("c_sinIK", [128, SEQ])
    c_cosQ = din("c_cosQ", [128, TOWN])
    c_sinQ = din("c_sinQ", [128, TOWN])
    c_cosIQ = din("c_cosIQ", [128, TOWN])
    c_sinIQ = din("c_sinIQ", [128, TOWN])

    out = nc.dram_tensor("out", [TOWN, D], F32, kind="ExternalOutput").ap()

    KT = dscr("KT", [4, 128, SEQ])
    VV = dscr("VV", [SEQ, 512])
    IKT = dscr("IKT", [128, SEQ])
    SNAP = dscr("SNAP", [NTALL, 128, 4096])
    QT = dscr("QT", [1024, TOWN])
    KGT = dscr("KGT", [1024, TOWN])
    VG = dscr("VG", [TOWN, 2048])
    GRT = dscr("GRT", [2048, TOWN])
    DQT = dscr("DQT", [2048, TOWN])
    IQT = dscr("IQT", [2048, TOWN])
    IWS = dscr("IWS", [TOWN, 16], F32)
    GT = dscr("GT", [4096, TOWN])
    OGT = dscr("OGT", [2048, TOWN])
    MT = dscr("MT", [2048, TOWN])
    ODT = dscr("ODT", [2048, TOWN])
    MT2 = dscr("MT2", [2048, TOWN])
    X1 = dscr("X1", [TOWN, D], F32)
    HTS = dscr("HTS", [NTALL, 128, 2048])
    EQT = dscr("EQT", [1024, TOWN])
    EKT = dscr("EKT", [1024, TOWN])

    dr = {}

    def DR(name):
        if name not in dr:
            dr[name] = S.res(name)
        return dr[name]

    class Common:
        pass

    def setup_common(p, npsf=6):
        c = Common()
        c.p = p
        c.identf = kb.sb("identf", [128, 128], F32)
        c.identb = kb.sb("identb", [128, 128], BF16)
        c.onesb = kb.sb("onesb", [128, 128], BF16)
        c.cst = kb.sb("cst", [128, 8], F32)
        S.dma('sp', 'cid', lambda e: e.dma_start(out=c.identf.t[:], in_=c_ident), writes=[c.identf.r])
        S.dve(lambda e: e.tensor_copy(out=c.identb.t[:], in_=c.identf.t[:]), reads=[c.identf.r], writes=[c.identb.r])
        S.dve(lambda e: e.memset(c.onesb.t[:], 1.0), writes=[c.onesb.r])
        vals = [EPS, 1.0, math.log(1.0 / 16.0), 0.0, 0.5, float(TOPK), 0.0, 0.0]
        for i, v in enumerate(vals):
            S.dve(lambda e, i=i, v=v: e.memset(c.cst.t[:, i:i + 1], v), writes=[c.cst.r])
        c.ps = [kb.psum(f"ps{i}", [128, 512], F32) for i in range(npsf)]
        c.pb = [kb.psum(f"pb{i}", [128, 1024], BF16) for i in range(8 - npsf)]
        c.psi = 0
        c.wslots = None
        c.wi = 0
        return c

    def next_ps(c):
        b = c.ps[c.psi % len(c.ps)]
        c.psi += 1
        return b

    def load_cols(c, name, src_row, n):
        t = kb.sb(name, [128, n], F32)
        S.dma('sp', 'cols_' + name,
              lambda e: e.dma_start(out=t.t[:], in_=src_row.rearrange("o (c p) -> p (o c)", p=128),
                                    allow_slow_non_contiguous=True), writes=[t.r])
        return t

    def make_norm_ctx(c, nslots=2):
        n = Common()
        n.xt = [kb.sb(f"xt{i}", [128, D], F32) for i in range(nslots)]
        n.junk = kb.sb("junk", [128, D], BF16)
        n.ss = [kb.sb(f"ss{i}", [128, 4], F32) for i in range(nslots)]
        n.h = [kb.sb(f"h{i}", [128, D], BF16) for i in range(nslots)]
        n.i = 0
        return n

    def norm_pre(c, n, x_ap):
        sl = n.i % len(n.xt)
        n.i += 1
        xt, ss, h = n.xt[sl], n.ss[sl], n.h[sl]
        S.dma('sp', f'x{sl}', lambda e: e.dma_start(out=xt.t[:], in_=x_ap), writes=[xt.r])
        S.act(lambda e: e.activation(out=n.junk.t[:], in_=xt.t[:], func=AF.Square, accum_out=ss.t[:, 0:1]),
              reads=[xt.r], writes=[n.junk.r, ss.r])
        S.act(lambda e: e.activation(out=ss.t[:, 1:2], in_=ss.t[:, 0:1], func=AF.Ln, scale=1.0 / D,
                                     bias=c.cst.t[:, 0:1]), reads=[ss.r, c.cst.r], writes=[ss.r])
        S.act(lambda e: e.activation(out=ss.t[:, 2:3], in_=ss.t[:, 1:2], func=AF.Exp, scale=-0.5), reads=[ss.r], writes=[ss.r])
        S.dve(lambda e: e.tensor_scalar(out=h.t[:], in0=xt.t[:], scalar1=ss.t[:, 2:3], scalar2=None, op0=ALU.mult),
              reads=[xt.r, ss.r], writes=[h.r])
        return sl

    def norm_post(c, n, sl, gcol, dst_fn, dst_r):
        h = n.h[sl]
        for half in range(2):
            pb = c.pb[half % len(c.pb)]
            for k in range(8):
                kc = half * 8 + k
                S.pe(lambda e, k=k, kc=kc, pb=pb: e.transpose(out=pb.t[:, k * 128:(k + 1) * 128],
                                                              in_=h.t[:, kc * 128:(kc + 1) * 128], identity=c.identb.t[:]),
                     reads=[h.r, c.identb.r], writes=[pb.r])
            S.dve(lambda e, half=half, pb=pb: e.tensor_tensor(
                out=dst_fn(half * 8, half * 8 + 8), in0=pb.t[:, 0:1024].rearrange("p (a b) -> p a b", b=128),
                in1=gcol.t[:, half * 8:half * 8 + 8].unsqueeze(2).to_broadcast([128, 8, 128]), op=ALU.mult),
                reads=[pb.r, gcol.r], writes=[dst_r])

    def norm_T(c, n, x_ap, gcol, dst_fn, dst_r, xkeep=None):
        sl = norm_pre(c, n, x_ap)
        norm_post(c, n, sl, gcol, dst_fn, dst_r)

    def load_w(c, wt, W, r0, nkc, c0, ncols, key):
        src = W[r0:r0 + nkc * 128, c0:c0 + ncols].rearrange("(k p) c -> p k c", p=128)
        S.dma('pool', key, lambda e: e.dma_start(out=wt.t[:, 0:nkc, 0:ncols], in_=src), writes=[wt.r])

    def make_wslots(c, n=3, kc=16, cols=512):
        c.wslots = [kb.sb(f"wsl{i}", [128, kc, cols], BF16) for i in range(n)]
        c.wi = 0

    def next_w(c):
        i = c.wi % len(c.wslots)
        c.wi += 1
        return c.wslots[i], f"w{i}"

    def gemm_fm(c, W, c0, ncols, act_fn, act_rs, T, epi, grp=512):
        pending = None
        for g0 in range(0, ncols, grp):
            gn = min(grp, ncols - g0)
            wt, key = next_w(c)
            load_w(c, wt, W, 0, 16, c0 + g0, gn, key)
            for cb0 in range(0, gn, 128):
                m = min(128, gn - cb0)
                for t0 in range(0, T, 512):
                    tn = min(512, T - t0)
                    ps = next_ps(c)
                    for kc in range(16):
                        S.pe(lambda e, kc=kc, cb0=cb0, m=m, t0=t0, tn=tn, ps=ps, wt=wt:
                             e.matmul(out=ps.t[0:m, 0:tn], lhsT=wt.t[:, kc, cb0:cb0 + m], rhs=act_fn(kc, t0, tn),
                                      start=(kc == 0), stop=(kc == 15)),
                             reads=[wt.r] + list(act_rs), writes=[ps.r])
                    if pending is not None:
                        pending()
                    pending = (lambda cb=(g0 + cb0) // 128, t0=t0, tn=tn, ps=ps, m=m: epi(cb, t0, tn, ps, m))
        if pending is not None:
            pending()

    def gemm_tm(c, W, c0, ncols, KC, act_fn, act_rs, ntt, epi, grp=512):
        for g0 in range(0, ncols, grp):
            gn = min(grp, ncols - g0)
            pss = [next_ps(c) for _ in range(ntt)]
            for ks in range(0, KC, 16):
                wt, key = next_w(c)
                load_w(c, wt, W, ks * 128, 16, c0 + g0, gn, key)
                for tt in range(ntt):
                    ps = pss[tt]
                    for k in range(16):
                        kc = ks + k
                        S.pe(lambda e, kc=kc, k=k, tt=tt, ps=ps, wt=wt, gn=gn:
                             e.matmul(out=ps.t[:, 0:gn], lhsT=act_fn(kc, tt), rhs=wt.t[:, k, 0:gn],
                                      start=(kc == 0), stop=(kc == KC - 1)),
                             reads=[wt.r] + list(act_rs), writes=[ps.r])
            for tt in range(ntt):
                epi(tt, g0, gn, pss[tt])

    def make_rope_ctx(c, N):
        r = Common()
        r.N = N
        r.sq = kb.sb("r_sq", [128, N], BF16)
        r.rstd = kb.sb("r_rstd", [128, N], F32)
        r.xnb = kb.sb("r_xnb", [128, N], BF16)
        r.t1 = kb.sb("r_t1", [128, N], F32)
        r.t2 = kb.sb("r_t2", [128, N], F32)
        return r

    def headnorm_rope(c, r, ps, n, gain, cosb, sinb, cs_ap, pmatT, dst, dst_r, ps2=None, ps3=None):
        if gain is not None:
            S.act(lambda e: e.activation(out=r.sq.t[:, 0:n], in_=ps.t[:, 0:n], func=AF.Square), reads=[ps.r], writes=[r.sq.r])
            ps2 = ps2 if ps2 is not None else next_ps(c)
            S.pe(lambda e: e.matmul(out=ps2.t[:, 0:n], lhsT=c.onesb.t[:], rhs=r.sq.t[:, 0:n], start=True, stop=True),
                 reads=[c.onesb.r, r.sq.r], writes=[ps2.r])
            S.act(lambda e: e.activation(out=r.t1.t[:, 0:n], in_=ps2.t[:, 0:n], func=AF.Ln, scale=1.0 / 128,
                                         bias=c.cst.t[:, 0:1]), reads=[ps2.r, c.cst.r], writes=[r.t1.r])
            S.act(lambda e: e.activation(out=r.rstd.t[:, 0:n], in_=r.t1.t[:, 0:n], func=AF.Exp, scale=-0.5),
                  reads=[r.t1.r], writes=[r.rstd.r])
            S.dve(lambda e: e.scalar_tensor_tensor(out=r.xnb.t[:, 0:n], in0=ps.t[:, 0:n], scalar=gain.t[:, 0:1],
                                                   in1=r.rstd.t[:, 0:n], op0=ALU.mult, op1=ALU.mult),
                  reads=[ps.r, gain.r, r.rstd.r], writes=[r.xnb.r])
        else:
            S.act(lambda e: e.activation(out=r.xnb.t[:, 0:n], in_=ps.t[:, 0:n], func=AF.Copy), reads=[ps.r], writes=[r.xnb.r])
        ps3 = ps3 if ps3 is not None else next_ps(c)
        S.pe(lambda e: e.matmul(out=ps3.t[:, 0:n], lhsT=pmatT.t[:], rhs=r.xnb.t[:, 0:n], start=True, stop=True),
             reads=[pmatT.r, r.xnb.r], writes=[ps3.r])
        S.dve(lambda e: e.tensor_tensor(out=r.t1.t[:, 0:n], in0=r.xnb.t[:, 0:n], in1=cs_ap(cosb), op=ALU.mult),
              reads=[r.xnb.r, cosb.r], writes=[r.t1.r])
        S.dve(lambda e: e.tensor_tensor(out=r.t2.t[:, 0:n], in0=ps3.t[:, 0:n], in1=cs_ap(sinb), op=ALU.mult),
              reads=[ps3.r, sinb.r], writes=[r.t2.r])
        S.dve(lambda e: e.tensor_tensor(out=dst, in0=r.t1.t[:, 0:n], in1=r.t2.t[:, 0:n], op=ALU.add),
              reads=[r.t1.r, r.t2.r], writes=[dst_r])

    def load_pmat(c, name, src):
        f = kb.sb(name + "f", [128, 128], F32)
        b = kb.sb(name + "b", [128, 128], BF16)
        S.dma('sp', 'pm_' + name, lambda e: e.dma_start(out=f.t[:], in_=src), writes=[f.r])
        S.dve(lambda e: e.tensor_copy(out=b.t[:], in_=f.t[:]), reads=[f.r], writes=[b.r])
        return b

    def glr_logsig(c, g, hT_fn, hT_r):
        psg = next_ps(c)
        for kc in range(16):
            S.pe(lambda e, kc=kc: e.matmul(out=psg.t[0:16, 0:128], lhsT=g.wl.t[:, kc, :], rhs=hT_fn(kc),
                                           start=(kc == 0), stop=(kc == 15)), reads=[g.wl.r, hT_r], writes=[psg.r])
        S.act(lambda e: e.activation(out=g.glrT.t[:], in_=psg.t[0:16, 0:128], func=AF.Copy), reads=[psg.r], writes=[g.glrT.r])
        for hf in range(2):
            psz = next_ps(c)
            S.pe(lambda e, hf=hf, psz=psz: e.matmul(out=psz.t[:], lhsT=g.glrT.t[:], rhs=g.wg2b.t[:, hf * 512:(hf + 1) * 512],
                                                    start=True, stop=False), reads=[g.glrT.r, g.wg2b.r], writes=[psz.r])
            S.pe(lambda e, hf=hf, psz=psz: e.matmul(out=psz.t[:], lhsT=g.onesrow.t[:], rhs=g.bgb.t[:, hf * 512:(hf + 1) * 512],
                                                    start=False, stop=True), reads=[g.onesrow.r, g.bgb.r], writes=[psz.r])
            S.act(lambda e, hf=hf, psz=psz: e.activation(out=g.ez.t[:, hf * 512:(hf + 1) * 512], in_=psz.t[:], func=AF.Exp, scale=-1.0),
                  reads=[psz.r], writes=[g.ez.r])
        S.act(lambda e: e.activation(out=g.L.t[:], in_=g.ez.t[:], func=AF.Ln, bias=c.cst.t[:, 1:2], scale=1.0),
              reads=[g.ez.r, c.cst.r], writes=[g.L.r])

    def make_glr_ctx(c):
        g = Common()
        g.wl = kb.sb("g_wl", [128, 16, 16], BF16)
        load_w(c, g.wl, w_in, 0, 16, O_GLR, 16, 'wl')
        g.wg2b = kb.sb("g_wg2b", [16, 1024], BF16)
        S.dma('pool', 'wg2', lambda e: e.dma_start(out=g.wg2b.t[:], in_=gla_wg2), writes=[g.wg2b.r])
        g.bgb = kb.sb("g_bgb", [1, 1024], BF16)
        S.dma('pool', 'bg', lambda e: e.dma_start(out=g.bgb.t[:], in_=gla_bg), writes=[g.bgb.r])
        g.onesrow = kb.sb("g_onesrow", [1, 128], BF16)
        S.dve(lambda e: e.memset(g.onesrow.t[:], 1.0), writes=[g.onesrow.r])
        g.glrT = kb.sb("g_glrT", [16, 128], BF16)
        g.ez = kb.sb("g_ez", [128, 1024], F32)
        g.L = kb.sb("g_L", [128, 1024], F32)
        return g

    def phase_A1():
        c = setup_common('A1', npsf=6)
        n = make_norm_ctx(c)
        g = make_glr_ctx(c)
        g1 = load_cols(c, "g1col", norm1_g, 16)
        wk = kb.sb("a_wk", [128, 16, 1024], BF16)
        wv = kb.sb("a_wv", [128, 16, 2048], BF16)
        for q in range(2):
            src = w_in[:, O_GK + q * 512:O_GK + (q + 1) * 512].rearrange("(k p) c -> p k c", p=128)
            S.dma('pool', f'wk{q}', lambda e, q=q, src=src: e.dma_start(out=wk.t[:, :, q * 512:(q + 1) * 512], in_=src), writes=[wk.r])
        for q in range(4):
            src = w_in[:, O_GV + q * 512:O_GV + (q + 1) * 512].rearrange("(k p) c -> p k c", p=128)
            S.dma('pool', f'wv{q}', lambda e, q=q, src=src: e.dma_start(out=wv.t[:, :, q * 512:(q + 1) * 512], in_=src), writes=[wv.r])
        ustr = kb.sb("a_ustr", [128, 128], F32)
        S.dma('sp', 'ustr', lambda e: e.dma_start(out=ustr.t[:], in_=c_ustrict), writes=[ustr.r])
        onesc = kb.sb("a_onesc", [128, 1], F32)
        S.dve(lambda e: e.memset(onesc.t[:], 1.0), writes=[onesc.r])
        hT = [kb.sb(f"a_hT{i}", [128, 16, 128], BF16) for i in range(2)]
        state = kb.sb("a_state", [128, 8, 512], F32)
        S.dve(lambda e: e.memset(state.t[:], 0.0), writes=[state.r])
        stb = [kb.sb(f"a_stb{i}", [128, 8, 512], BF16) for i in range(2)]
        e1 = kb.sb("a_e1", [128, 1024], F32)
        khat = kb.sb("a_khat", [128, 1024], BF16)
        gvb = kb.sb("a_gvb", [128, 2048], BF16)
        dec = kb.sb("a_dec", [128, 8], F32)
        B = c.ps
        pending_state = [None]
        nsl = norm_pre(c, n, xfull[0:128, :])
        for tt in range(NTALL):
            h = hT[tt % 2]
            norm_post(c, n, nsl, g1, lambda k0, k1, h=h: h.t[:, k0:k1, :], h.r)
            if tt + 1 < NTALL:
                nsl = norm_pre(c, n, xfull[(tt + 1) * 128:(tt + 2) * 128, :])
            if pending_state[0] is not None:
                pending_state[0]()
                pending_state[0] = None
            S.dma(STORE_ENG, f'hts{tt % 2}', lambda e, h=h, tt=tt: e.dma_start(out=HTS[tt], in_=h.t[:].rearrange("p a b -> p (a b)")),
                  reads=[h.r], writes=[DR('HTS')])
            for kc in range(16):
                S.pe(lambda e, kc=kc, h=h: e.matmul(out=B[2].t[0:16, 0:128], lhsT=g.wl.t[:, kc, :], rhs=h.t[:, kc, :], start=(kc == 0), stop=(kc == 15)),
                     reads=[g.wl.r, h.r], writes=[B[2].r])
            S.act(lambda e: e.activation(out=g.glrT.t[:], in_=B[2].t[0:16, 0:128], func=AF.Copy), reads=[B[2].r], writes=[g.glrT.r])
            for hf in range(2):
                for kc in range(16):
                    S.pe(lambda e, kc=kc, hf=hf, h=h: e.matmul(out=B[hf].t[:], lhsT=h.t[:, kc, :], rhs=wk.t[:, kc, hf * 512:(hf + 1) * 512],
                                                               start=(kc == 0), stop=(kc == 15)), reads=[h.r, wk.r], writes=[B[hf].r])
            for hf in range(2):
                psz = B[2 + hf]
                S.pe(lambda e, hf=hf, psz=psz: e.matmul(out=psz.t[:], lhsT=g.glrT.t[:], rhs=g.wg2b.t[:, hf * 512:(hf + 1) * 512], start=True, stop=False),
                     reads=[g.glrT.r, g.wg2b.r], writes=[psz.r])
                S.pe(lambda e, hf=hf, psz=psz: e.matmul(out=psz.t[:], lhsT=g.onesrow.t[:], rhs=g.bgb.t[:, hf * 512:(hf + 1) * 512], start=False, stop=True),
                     reads=[g.onesrow.r, g.bgb.r], writes=[psz.r])
                S.act(lambda e, hf=hf, psz=psz: e.activation(out=g.ez.t[:, hf * 512:(hf + 1) * 512], in_=psz.t[:], func=AF.Exp, scale=-1.0),
                      reads=[psz.r], writes=[g.ez.r])
            for q in range(4):
                ps = B[4 + q % 2]
                for kc in range(16):
                    S.pe(lambda e, kc=kc, q=q, ps=ps, h=h: e.matmul(out=ps.t[:], lhsT=h.t[:, kc, :], rhs=wv.t[:, kc, q * 512:(q + 1) * 512],
                                                                    start=(kc == 0), stop=(kc == 15)), reads=[h.r, wv.r], writes=[ps.r])
                if q % 2 == 0:
                    S.act(lambda e, q=q, ps=ps: e.activation(out=gvb.t[:, q * 512:(q + 1) * 512], in_=ps.t[:], func=AF.Copy), reads=[ps.r], writes=[gvb.r])
                else:
                    S.dve(lambda e, q=q, ps=ps: e.tensor_copy(out=gvb.t[:, q * 512:(q + 1) * 512], in_=ps.t[:]), reads=[ps.r], writes=[gvb.r])
            S.act(lambda e: e.activation(out=g.L.t[:], in_=g.ez.t[:], func=AF.Ln, bias=c.cst.t[:, 1:2], scale=1.0),
                  reads=[g.ez.r, c.cst.r], writes=[g.L.r])
            for hf in range(2):
                ps = B[2 + hf]
                S.pe(lambda e, hf=hf, ps=ps: e.matmul(out=ps.t[:], lhsT=ustr.t[:], rhs=g.L.t[:, hf * 512:(hf + 1) * 512], start=True, stop=True),
                     reads=[ustr.r, g.L.r], writes=[ps.r])
                S.act(lambda e, hf=hf, ps=ps: e.activation(out=e1.t[:, hf * 512:(hf + 1) * 512], in_=ps.t[:], func=AF.Exp, scale=-1.0 / 16),
                      reads=[ps.r], writes=[e1.r])
            psd = B[4]
            for cc in range(8):
                S.pe(lambda e, cc=cc: e.matmul(out=psd.t[:, cc:cc + 1], lhsT=g.L.t[:, cc * 128:(cc + 1) * 128], rhs=onesc.t[:], start=True, stop=True),
                     reads=[g.L.r, onesc.r], writes=[psd.r])
            S.act(lambda e: e.activation(out=dec.t[:], in_=psd.t[:, 0:8], func=AF.Exp, scale=-1.0 / 16), reads=[psd.r], writes=[dec.r])
            for hf in range(2):
                S.dve(lambda e, hf=hf: e.tensor_tensor(out=khat.t[:, hf * 512:(hf + 1) * 512], in0=B[hf].t[:], in1=e1.t[:, hf * 512:(hf + 1) * 512], op=ALU.mult),
                      reads=[B[hf].r, e1.r], writes=[khat.r])
            def state_step(tt=tt):
                sb_ = stb[tt % 2]
                S.act(lambda e: e.activation(out=sb_.t[:, 0:4, :], in_=state.t[:, 0:4, :], func=AF.Copy), reads=[state.r], writes=[sb_.r])
                S.dve(lambda e: e.tensor_copy(out=sb_.t[:, 4:8, :], in_=state.t[:, 4:8, :]), reads=[state.r], writes=[sb_.r])
                S.dma(STORE_ENG, f'snap{tt % 2}', lambda e: e.dma_start(out=SNAP[tt], in_=sb_.t[:].rearrange("p a b -> p (a b)")),
                      reads=[sb_.r], writes=[DR('SNAP')])
                rot = [B[2], B[3], B[5], B[4]]
                for ix in range(8):
                    hh = ix // 2
                    ps = rot[ix % 4]
                    S.pe(lambda e, ix=ix, hh=hh, ps=ps: e.matmul(out=ps.t[:], lhsT=khat.t[:, ix * 128:(ix + 1) * 128], rhs=gvb.t[:, hh * 512:(hh + 1) * 512],
                                                                 start=True, stop=True), reads=[khat.r, gvb.r], writes=[ps.r])
                    S.dve(lambda e, ix=ix, ps=ps: e.scalar_tensor_tensor(out=state.t[:, ix, :], in0=state.t[:, ix, :], scalar=dec.t[:, ix:ix + 1],
                                                                         in1=ps.t[:], op0=ALU.mult, op1=ALU.add),
                          reads=[state.r, dec.r, ps.r], writes=[state.r])
            pending_state[0] = state_step
        pending_state[0]()

    def phase_A2():
        c = setup_common('A2', npsf=6)
        r = make_rope_ctx(c, 512)
        kng = load_cols(c, "kng", k_norm_g, 1)
        ikng = load_cols(c, "ikng", idx_k_norm_g, 1)
        pd = load_pmat(c, "pdsa", c_pdsaT)
        pi = load_pmat(c, "pidx", c_pidxT)
        wdk = kb.sb("b_wdk", [128, 16, 512], BF16)
        wdv = kb.sb("b_wdv", [128, 16, 512], BF16)
        wik = kb.sb("b_wik", [128, 16, 128], BF16)
        load_w(c, wdk, w_in, 0, 16, O_DK, 512, 'wdk')
        load_w(c, wdv, w_in, 0, 16, O_DV, 512, 'wdv')
        load_w(c, wik, w_in, 0, 16, O_IK, 128, 'wik')
        hT = [kb.sb(f"b_hT{i}", [128, 16, 512], BF16) for i in range(2)]
        tabs = [[kb.sb(f"b_tab{i}_{s}", [128, 512], F32) for i in range(4)] for s in range(2)]
        vst = [kb.sb(f"b_vst{i}", [128, 512], BF16) for i in range(2)]
        kst = [kb.sb(f"b_kst{i}", [128, 5, 512], BF16) for i in range(2)]
        B = c.ps
        vcnt = 0
        for st4 in range(NTALL // 4):
            h = hT[st4 % 2]
            tb = tabs[st4 % 2]
            for t in range(4):
                tt = st4 * 4 + t
                S.dma('sp', f'hld{st4 % 2}_{t}', lambda e, h=h, t=t, tt=tt: e.dma_start(
                    out=h.t[:, :, t * 128:(t + 1) * 128], in_=HTS[tt].rearrange("p (a b) -> p a b", b=128)), reads=[DR('HTS')], writes=[h.r])
            for i, src in enumerate((c_cosK, c_sinK, c_cosIK, c_sinIK)):
                S.dma('sp', f'tab{i}_{st4 % 2}', lambda e, i=i, src=src, tb=tb, st4=st4: e.dma_start(out=tb[i].t[:], in_=src[:, st4 * 512:(st4 + 1) * 512]),
                      writes=[tb[i].r])
            ks = kst[st4 % 2]

            def proj(gg, h=h):
                ps = B[gg % 2]
                wr = wdk.r if gg < 4 else wik.r
                for kc in range(16):
                    S.pe(lambda e, kc=kc, ps=ps, gg=gg: e.matmul(
                        out=ps.t[:, 0:512], lhsT=(wdk.t[:, kc, gg * 128:(gg + 1) * 128] if gg < 4 else wik.t[:, kc, :]),
                        rhs=h.t[:, kc, :], start=(kc == 0), stop=(kc == 15)), reads=[h.r, wr], writes=[ps.r])

            def epil(gg, tb=tb, ks=ks):
                ps = B[gg % 2]
                if gg < 4:
                    headnorm_rope(c, r, ps, 512, kng, tb[0], tb[1], lambda b: b.t[:], pd, ks.t[:, gg, :], ks.r, ps2=B[2], ps3=B[3])
                else:
                    headnorm_rope(c, r, ps, 512, ikng, tb[2], tb[3], lambda b: b.t[:], pi, ks.t[:, gg, :], ks.r, ps2=B[2], ps3=B[3])

            def vproj(t, h=h, st4=st4):
                nonlocal vcnt
                tt = st4 * 4 + t
                ps = B[4 + vcnt % 2]
                vs = vst[vcnt % 2]
                slot = vcnt % 2
                vcnt += 1
                for kc in range(16):
                    S.pe(lambda e, kc=kc, ps=ps, t=t: e.matmul(out=ps.t[:], lhsT=h.t[:, kc, t * 128:(t + 1) * 128], rhs=wdv.t[:, kc, :], start=(kc == 0), stop=(kc == 15)),
                         reads=[h.r, wdv.r], writes=[ps.r])
                S.act(lambda e, ps=ps, vs=vs: e.activation(out=vs.t[:], in_=ps.t[:], func=AF.Copy), reads=[ps.r], writes=[vs.r])
                S.dma(STORE_ENG, f'vst{slot}', lambda e, vs=vs, tt=tt: e.dma_start(out=VV[tt * 128:(tt + 1) * 128, :], in_=vs.t[:]),
                      reads=[vs.r], writes=[DR('VV')])

            proj(0)
            for gg in range(5):
                if gg + 1 < 5:
                    proj(gg + 1)
                if gg < 4:
                    vproj(gg)
                epil(gg)
            S.dma(STORE_ENG, f'kst{st4 % 2}', lambda e, ks=ks, st4=st4: e.dma_start(
                out=KT[:, :, st4 * 512:(st4 + 1) * 512].rearrange("g p t -> p g t"), in_=ks.t[:, 0:4, :]), reads=[ks.r], writes=[DR('KT')])
            S.dma(STORE_ENG, f'ikst{st4 % 2}', lambda e, ks=ks, st4=st4: e.dma_start(out=IKT[:, st4 * 512:(st4 + 1) * 512], in_=ks.t[:, 4, :]),
                  reads=[ks.r], writes=[DR('IKT')])


    def phase_B():
        c = setup_common('B', npsf=6)
        make_wslots(c, 2)
        n = make_norm_ctx(c, 1)
        g = make_glr_ctx(c)
        r = make_rope_ctx(c, 512)
        g1 = load_cols(c, "g1col", norm1_g, 16)
        qng = load_cols(c, "qng", q_norm_g, 1)
        bgc = load_cols(c, "bgc", b_gate, 32)
        pd = load_pmat(c, "pdsa", c_pdsaT)
        pi = load_pmat(c, "pidx", c_pidxT)
        uincl = kb.sb("c_uincl", [128, 128], F32)
        S.dma('sp', 'uincl', lambda e: e.dma_start(out=uincl.t[:], in_=c_uincl), writes=[uincl.r])
        hT = kb.sb("hTown", [128, 16, TOWN], BF16)
        eqs = [kb.sb(f"eqs{i}", [128, 8, 128], BF16) for i in range(2)]
        eks = [kb.sb(f"eks{i}", [128, 8, 128], BF16) for i in range(2)]
        for i in range(NOWN):
            norm_T(c, n, xown[i * 128:(i + 1) * 128, :], g1, lambda k0, k1, i=i: hT.t[:, k0:k1, i * 128:(i + 1) * 128], hT.r)
        for i in range(NOWN):
            glr_logsig(c, g, lambda kc, i=i: hT.t[:, kc, i * 128:(i + 1) * 128], hT.r)
            for hf in range(2):
                ps = next_ps(c)
                for q in range(4):
                    cc = hf * 4 + q
                    S.pe(lambda e, cc=cc, q=q, ps=ps: e.matmul(out=ps.t[:, q * 128:(q + 1) * 128], lhsT=g.L.t[:, cc * 128:(cc + 1) * 128], rhs=uincl.t[:],
                                                               start=True, stop=True), reads=[g.L.r, uincl.r], writes=[ps.r])
                eq, ek = eqs[i % 2], eks[i % 2]
                S.act(lambda e, hf=hf, ps=ps, eq=eq: e.activation(out=eq.t[:, hf * 4:(hf + 1) * 4, :],
                                                                in_=ps.t[:].rearrange("p (a b) -> p a b", b=128), func=AF.Exp,
                                                                scale=-1.0 / 16, bias=c.cst.t[:, 2:3]), reads=[ps.r, c.cst.r], writes=[eq.r])
                S.act(lambda e, hf=hf, ps=ps, ek=ek: e.activation(out=ek.t[:, hf * 4:(hf + 1) * 4, :],
                                                                in_=ps.t[:].rearrange("p (a b) -> p a b", b=128), func=AF.Exp,
                                                                scale=1.0 / 16), reads=[ps.r], writes=[ek.r])
            S.dma(STORE_ENG, f'eqs{i % 2}', lambda e, eq=eq, i=i: e.dma_start(out=EQT[:, i * 128:(i + 1) * 128].rearrange("(a p) t -> p a t", p=128), in_=eq.t[:]),
                  reads=[eq.r], writes=[DR('EQT')])
            S.dma(STORE_ENG, f'eks{i % 2}', lambda e, ek=ek, i=i: e.dma_start(out=EKT[:, i * 128:(i + 1) * 128].rearrange("(a p) t -> p a t", p=128), in_=ek.t[:]),
                  reads=[ek.r], writes=[DR('EKT')])
        act_fn = lambda kc, t0, tn: hT.t[:, kc, t0:t0 + tn]
        stg = [kb.sb(f"stg{i}", [128, 512], BF16) for i in range(3)]
        stc = [0]

        def stage():
            b = stg[stc[0] % 3]
            k = f"stg{stc[0] % 3}"
            stc[0] += 1
            return b, k

        def store_fm(dst, drn, cb, t0, tn, st, key, m=128):
            S.dma(STORE_ENG, key, lambda e: e.dma_start(out=dst[cb * 128:cb * 128 + m, t0:t0 + tn], in_=st.t[0:m, 0:tn]),
                  reads=[st.r], writes=[DR(drn)])

        esl = [kb.sb(f"esl{i}", [128, 512], BF16) for i in range(3)]
        ecnt = [0]

        def epi_mul(E, En, dst, drn):
            def epi(cb, t0, tn, ps, m):
                st, key = stage()
                sl = ecnt[0] % 3
                ecnt[0] += 1
                eb_ = esl[sl]
                S.dma('sp', f'esl{sl}', lambda e: e.dma_start(out=eb_.t[:, 0:tn], in_=E[cb * 128:(cb + 1) * 128, t0:t0 + tn]),
                      reads=[DR(En)], writes=[eb_.r])
                S.dve(lambda e: e.tensor_tensor(out=st.t[:, 0:tn], in0=ps.t[:, 0:tn], in1=eb_.t[:, 0:tn], op=ALU.mult),
                      reads=[ps.r, eb_.r], writes=[st.r])
                store_fm(dst, drn, cb, t0, tn, st, key)
            return epi

        gemm_fm(c, w_in, O_GQ, 1024, act_fn, [hT.r], TOWN, epi_mul(EQT, 'EQT', QT, 'QT'))
        gemm_fm(c, w_in, O_GK, 1024, act_fn, [hT.r], TOWN, epi_mul(EKT, 'EKT', KGT, 'KGT'))

        def epi_silu(cb, t0, tn, ps, m):
            st, key = stage()
            S.act(lambda e: e.activation(out=st.t[:, 0:tn], in_=ps.t[:, 0:tn], func=AF.Silu), reads=[ps.r], writes=[st.r])
            store_fm(GRT, 'GRT', cb, t0, tn, st, key)
        gemm_fm(c, w_in, O_GR, 2048, act_fn, [hT.r], TOWN, epi_silu)

        def epi_gate(cb, t0, tn, ps, m):
            st, key = stage()
            S.act(lambda e: e.activation(out=st.t[:, 0:tn], in_=ps.t[:, 0:tn], func=AF.Sigmoid, bias=bgc.t[:, cb:cb + 1], scale=1.0),
                  reads=[ps.r, bgc.r], writes=[st.r])
            store_fm(GT, 'GT', cb, t0, tn, st, key)
        gemm_fm(c, w_in, O_GATES, 4096, act_fn, [hT.r], TOWN, epi_gate)

        tabq = [[kb.sb(f"tabq{i}_{s}", [128, 512], F32) for i in range(2)] for s in range(2)]
        tcnt = [0]

        def epi_rope(gain, csrc, ssrc, pm, dst, drn):
            def epi(cb, t0, tn, ps, m):
                st, key = stage()
                sl = tcnt[0] % 2
                tcnt[0] += 1
                tb = tabq[sl]
                S.dma('sp', f'tq0_{sl}', lambda e: e.dma_start(out=tb[0].t[:, 0:tn], in_=csrc[:, t0:t0 + tn]), writes=[tb[0].r])
                S.dma('sp', f'tq1_{sl}', lambda e: e.dma_start(out=tb[1].t[:, 0:tn], in_=ssrc[:, t0:t0 + tn]), writes=[tb[1].r])
                headnorm_rope(c, r, ps, tn, gain, tb[0], tb[1], lambda b: b.t[:, 0:tn], pm, st.t[:, 0:tn], st.r)
                store_fm(dst, drn, cb, t0, tn, st, key)
            return epi
        gemm_fm(c, w_in, O_DQ, 2048, act_fn, [hT.r], TOWN, epi_rope(qng, c_cosQ, c_sinQ, pd, DQT, 'DQT'))
        gemm_fm(c, w_in, O_IQ, 2048, act_fn, [hT.r], TOWN, epi_rope(None, c_cosIQ, c_sinIQ, pi, IQT, 'IQT'))

        act_tm = lambda kc, tt: hT.t[:, kc, tt * 128:(tt + 1) * 128]

        def epi_gv(tt, g0, gn, ps):
            st, key = stage()
            S.act(lambda e: e.activation(out=st.t[:, 0:gn], in_=ps.t[:, 0:gn], func=AF.Copy), reads=[ps.r], writes=[st.r])
            S.dma(STORE_ENG, key, lambda e: e.dma_start(out=VG[tt * 128:(tt + 1) * 128, g0:g0 + gn], in_=st.t[:, 0:gn]),
                  reads=[st.r], writes=[DR('VG')])
        for t4 in range(0, NOWN, 4):
            gemm_tm(c, w_in, O_GV, 2048, 16, lambda kc, tt, t4=t4: act_tm(kc, t4 + tt), [hT.r], 4,
                    lambda tt, g0, gn, ps, t4=t4: epi_gv(t4 + tt, g0, gn, ps))
        iwst = kb.sb("iwst", [128, NOWN, 16], F32)

        def epi_iw(tt, g0, gn, ps):
            S.act(lambda e: e.activation(out=iwst.t[:, tt, :], in_=ps.t[:, 0:16], func=AF.Copy, scale=float(16 ** -0.5 * 128 ** -0.5)),
                  reads=[ps.r], writes=[iwst.r])
        for t4 in range(0, NOWN, 4):
            gemm_tm(c, w_in, O_IW, 16, 16, lambda kc, tt, t4=t4: act_tm(kc, t4 + tt), [hT.r], 4,
                    lambda tt, g0, gn, ps, t4=t4: epi_iw(t4 + tt, g0, gn, ps))
        S.dma(STORE_ENG, 'iws', lambda e: e.dma_start(out=IWS.rearrange("(a p) h -> p a h", p=128), in_=iwst.t[:]),
              reads=[iwst.r], writes=[DR('IWS')])

    def phase_C():
        c = setup_common('C', npsf=7)
        gng = load_cols(c, "gng", gla_norm_g, 4)
        flg = kb.sb("flg", [128, 4], F32)
        S.dma('sp', 'flg', lambda e: e.dma_start(out=flg.t[:], in_=c_flags), writes=[flg.r])
        uf = kb.sb("uf", [128, 128], F32)
        ub = kb.sb("ub", [128, 128], BF16)
        S.dma('sp', 'uincl', lambda e: e.dma_start(out=uf.t[:], in_=c_uincl), writes=[uf.r])
        S.dve(lambda e: e.tensor_copy(out=ub.t[:], in_=uf.t[:]), reads=[uf.r], writes=[ub.r])
        NS = 2
        qT = [kb.sb(f"c_qT{i}", [128, 2, 128], BF16) for i in range(NS)]
        kT = [kb.sb(f"c_kT{i}", [128, 2, 128], BF16) for i in range(NS)]
        v = [kb.sb(f"c_v{i}", [128, 512], BF16) for i in range(NS)]
        cand = [kb.sb(f"c_cand{i}", [128, 4, 1024], BF16) for i in range(NS)]
        qf = [kb.sb(f"c_qf{i}", [128, 4, 2, 128], BF16) for i in range(NS)]
        aT = kb.sb("c_aT", [128, 128], BF16)
        sq = kb.sb("c_sq", [128, 512], BF16)
        rstd = kb.sb("c_rstd", [128, 128], F32)
        grt = [kb.sb(f"c_grt{i}", [128, 4, 128], BF16) for i in range(NS)]
        og = kb.sb("c_og", [128, 4, 128], F32)
        ogb = [kb.sb(f"c_ogb{i}", [128, 4, 128], BF16) for i in range(NS)]
        iters = [(i, hh) for i in range(NOWN) for hh in range(4)]
        ctxs = {}

        def stage1(it):
            i, hh = iters[it]
            s = it % NS
            q_, k_, v_, cd, gr_ = qT[s], kT[s], v[s], cand[s], grt[s]
            S.dma('sp', f'cq{s}', lambda e: e.dma_start(
                out=q_.t[:], in_=QT[hh * 256:(hh + 1) * 256, i * 128:(i + 1) * 128].rearrange("(a p) t -> p a t", p=128)),
                reads=[DR('QT')], writes=[q_.r])
            S.dma('sp', f'ck{s}', lambda e: e.dma_start(
                out=k_.t[:], in_=KGT[hh * 256:(hh + 1) * 256, i * 128:(i + 1) * 128].rearrange("(a p) t -> p a t", p=128)),
                reads=[DR('KGT')], writes=[k_.r])
            S.dma('sp', f'cv{s}', lambda e: e.dma_start(out=v_.t[:], in_=VG[i * 128:(i + 1) * 128, hh * 512:(hh + 1) * 512]),
                  reads=[DR('VG')], writes=[v_.r])
            S.dma('sp', f'cc{s}', lambda e: e.dma_start(
                out=cd.t[:], in_=SNAP[4 * i:4 * i + 4, :, hh * 1024:(hh + 1) * 1024].rearrange("c p f -> p c f")),
                reads=[DR('SNAP')], writes=[cd.r])
            S.dma('sp', f'cg{s}', lambda e: e.dma_start(
                out=gr_.t[:], in_=GRT[hh * 512:(hh + 1) * 512, i * 128:(i + 1) * 128].rearrange("(a p) t -> p a t", p=128)),
                reads=[DR('GRT')], writes=[gr_.r])
            qf_ = qf[s]
            for cc in range(4):
                S.dve(lambda e, cc=cc: e.tensor_scalar(out=qf_.t[:, cc, :, :], in0=q_.t[:], scalar1=flg.t[:, cc:cc + 1], scalar2=None, op0=ALU.mult),
                      reads=[q_.r, flg.r], writes=[qf_.r])
            psa = next_ps(c)
            for kh in range(2):
                S.pe(lambda e, kh=kh: e.matmul(out=psa.t[:, 0:128], lhsT=k_.t[:, kh, :], rhs=q_.t[:, kh, :], start=(kh == 0), stop=(kh == 1)),
                     reads=[q_.r, k_.r], writes=[psa.r])
            S.dve(lambda e: e.tensor_tensor(out=aT.t[:], in0=psa.t[:, 0:128], in1=ub.t[:], op=ALU.mult), reads=[psa.r, ub.r], writes=[aT.r])
            pso = next_ps(c)
            for ec in range(4):
                for cc in range(4):
                    for kh in range(2):
                        S.pe(lambda e, ec=ec, kh=kh, cc=cc: e.matmul(out=pso.t[:, ec * 128:(ec + 1) * 128],
                                                                     lhsT=cd.t[:, cc, kh * 512 + ec * 128:kh * 512 + (ec + 1) * 128],
                                                                     rhs=qf_.t[:, cc, kh, :], start=(kh == 0 and cc == 0), stop=False),
                             reads=[cd.r, qf_.r], writes=[pso.r])
                S.pe(lambda e, ec=ec: e.matmul(out=pso.t[:, ec * 128:(ec + 1) * 128], lhsT=v_.t[:, ec * 128:(ec + 1) * 128], rhs=aT.t[:],
                                               start=False, stop=True), reads=[v_.r, aT.r], writes=[pso.r])
            ctxs[it] = pso

        def stage2(it):
            i, hh = iters[it]
            s = it % NS
            gr_, ob = grt[s], ogb[s]
            pso = ctxs.pop(it)
            S.act(lambda e: e.activation(out=sq.t[:], in_=pso.t[:], func=AF.Square), reads=[pso.r], writes=[sq.r])
            pss = next_ps(c)
            for ec in range(4):
                S.pe(lambda e, ec=ec: e.matmul(out=pss.t[:, 0:128], lhsT=c.onesb.t[:], rhs=sq.t[:, ec * 128:(ec + 1) * 128], start=(ec == 0), stop=(ec == 3)),
                     reads=[c.onesb.r, sq.r], writes=[pss.r])
            S.act(lambda e: e.activation(out=rstd.t[:], in_=pss.t[:, 0:128], func=AF.Sqrt, scale=1.0 / 512, bias=c.cst.t[:, 0:1]),
                  reads=[pss.r, c.cst.r], writes=[rstd.r])
            S.dve(lambda e: e.reciprocal(out=rstd.t[:], in_=rstd.t[:]), reads=[rstd.r], writes=[rstd.r])
            for ec in range(4):
                S.dve(lambda e, ec=ec: e.scalar_tensor_tensor(out=og.t[:, ec, :], in0=pso.t[:, ec * 128:(ec + 1) * 128], scalar=gng.t[:, ec:ec + 1],
                                                              in1=rstd.t[:], op0=ALU.mult, op1=ALU.mult), reads=[pso.r, gng.r, rstd.r], writes=[og.r])
            S.dve(lambda e: e.tensor_tensor(out=ob.t[:], in0=og.t[:], in1=gr_.t[:], op=ALU.mult), reads=[og.r, gr_.r], writes=[ob.r])
            S.dma(STORE_ENG, f'cog{s}', lambda e: e.dma_start(
                out=OGT[hh * 512:(hh + 1) * 512, i * 128:(i + 1) * 128].rearrange("(a p) t -> p a t", p=128), in_=ob.t[:]),
                reads=[ob.r], writes=[DR('OGT')])

        stage1(0)
        for it in range(len(iters)):
            if it + 1 < len(iters):
                stage1(it + 1)
            stage2(it)

    def phase_proj(src, srcn, W, gate_off, addsrc, dst, dstn, tag):
        c = setup_common(tag, npsf=6)
        make_wslots(c, 3)
        aT = kb.sb("p_aT", [128, 16, TOWN], BF16)
        for q in range(4):
            S.dma('sp', f'pa{q}', lambda e, q=q: e.dma_start(out=aT.t[:, q * 4:(q + 1) * 4, :],
                                                            in_=src[q * 512:(q + 1) * 512, :].rearrange("(a p) t -> p a t", p=128)),
                  reads=[DR(srcn)], writes=[aT.r])
        gt = [kb.sb(f"p_gt{i}", [128, 512], BF16) for i in range(3)]
        ad = [kb.sb(f"p_ad{i}", [128, 512], BF16) for i in range(3)]
        tmp = [kb.sb(f"p_tmp{i}", [128, 512], F32) for i in range(2)]
        st = [kb.sb(f"p_st{i}", [128, 512], BF16) for i in range(3)]
        cnt = [0]

        def epi(cb, t0, tn, ps, m):
            s = cnt[0] % 3
            cnt[0] += 1
            g_, a_, o_ = gt[s], ad[s], st[s]
            S.dma('sp', f'pg{s}', lambda e: e.dma_start(out=g_.t[:, 0:tn], in_=GT[gate_off + cb * 128:gate_off + (cb + 1) * 128, t0:t0 + tn]),
                  reads=[DR('GT')], writes=[g_.r])
            if addsrc is None:
                S.dve(lambda e: e.tensor_tensor(out=o_.t[:, 0:tn], in0=ps.t[:, 0:tn], in1=g_.t[:, 0:tn], op=ALU.mult),
                      reads=[ps.r, g_.r], writes=[o_.r])
            else:
                t_ = tmp[cnt[0] % 2]
                S.dma('sp', f'pd{s}', lambda e: e.dma_start(out=a_.t[:, 0:tn], in_=addsrc[0][cb * 128:(cb + 1) * 128, t0:t0 + tn]),
                      reads=[DR(addsrc[1])], writes=[a_.r])
                S.dve(lambda e: e.tensor_tensor(out=t_.t[:, 0:tn], in0=ps.t[:, 0:tn], in1=g_.t[:, 0:tn], op=ALU.mult),
                      reads=[ps.r, g_.r], writes=[t_.r])
                S.dve(lambda e: e.tensor_tensor(out=o_.t[:, 0:tn], in0=t_.t[:, 0:tn], in1=a_.t[:, 0:tn], op=ALU.add),
                      reads=[t_.r, a_.r], writes=[o_.r])
            S.dma(STORE_ENG, f'po{s}', lambda e: e.dma_start(out=dst[cb * 128:(cb + 1) * 128, t0:t0 + tn], in_=o_.t[:, 0:tn]),
                  reads=[o_.r], writes=[DR(dstn)])
        gemm_fm(c, W, 0, 2048, lambda kc, t0, tn: aT.t[:, kc, t0:t0 + tn], [aT.r], TOWN, epi)

    def phase_E():
        c = setup_common('E', npsf=7)
        ikt = kb.sb("e_ikt", [128, SEQ], BF16)
        for q in range(4):
            S.dma('sp', f'eik{q}', lambda e, q=q: e.dma_start(out=ikt.t[:, q * 2048:(q + 1) * 2048], in_=IKT[:, q * 2048:(q + 1) * 2048]),
                  reads=[DR('IKT')], writes=[ikt.r])
        mskd = kb.sb("e_mskd", [128, 512], F32)
        S.dma('sp', 'emk', lambda e: e.dma_start(out=mskd.t[:], in_=c_maskd), writes=[mskd.r])
        pw2 = kb.sb("e_pw2", [128, NIT + 2], F32)
        for k in range(NIT + 2):
            S.dve(lambda e, k=k: e.memset(pw2.t[:, k:k + 1], float(2.0 ** -(k + 1))), writes=[pw2.r])
        iq = [kb.sb(f"e_iq{i}", [128, 16, 128], BF16) for i in range(2)]
        iw = [kb.sb(f"e_iw{i}", [128, 16], F32) for i in range(2)]
        wab = [kb.sb(f"e_wab{i}", [128, 16], F32) for i in range(2)]
        wsg = [kb.sb(f"e_wsg{i}", [128, 16], F32) for i in range(2)]
        dg = [kb.sb(f"e_dg{i}", [128, 16, 128], BF16) for i in range(2)]
        rb = [kb.sb(f"e_rb{i}", [128, 512], BF16) for i in range(4)]
        Sb = [kb.sb(f"e_S{i}", [128, SEQ], F32) for i in range(2)]
        junk = kb.sb("e_junk", [128, SEQ], BF16)
        Mb = kb.sb("e_M", [128, SEQ], BF16)
        MT_ = [kb.sb(f"e_MT{i}", [128, NTALL, 128], BF16) for i in range(2)]
        bsl = [kb.sb(f"e_bs{i}", [128, 8], F32) for i in range(2)]
        skl = [kb.sb(f"e_sk{i}", [128, NIT + 2], F32) for i in range(2)]
        midl = [kb.sb(f"e_mid{i}", [128, NIT + 2], F32) for i in range(2)]
        cntl = [kb.sb(f"e_cnt{i}", [128, NIT + 2], F32) for i in range(2)]
        gl = [kb.sb(f"e_g{i}", [128, NIT + 2], F32) for i in range(2)]
        dq = [kb.sb(f"e_dq{i}", [128, 512], BF16) for i in range(2)]
        ktc = [kb.sb(f"e_ktc{i}", [128, 512], BF16) for i in range(3)]
        vc = [kb.sb(f"e_vc{i}", [128, 4, 128], BF16) for i in range(3)]
        eb = [kb.sb(f"e_eb{i}", [128, 512], BF16) for i in range(4)]
        rden = kb.sb("e_rden", [128, 512], F32)
        od = [kb.sb(f"e_od{i}", [128, 512], BF16) for i in range(2)]
        psS = c.ps[6]
        psO = c.ps[5]
        psD = c.ps[4]
        lg = c.ps[0:4]
        pbT = c.pb[0]
        hsc = float(128 ** -0.5)
        BIGM = 30000.0
        mbias = kb.sb("e_mbias", [128, 1], F32)
        S.dve(lambda e: e.memset(mbias.t[:], -BIGM), writes=[mbias.r])
        cnts = {'rb': 0, 'eb': 0, 'kv': 0, 'dq': 0, 'od': 0}

        def idx_steps(i):
            steps = []
            iq_, iw_, wab_, wsg_, dg_, Sb_ = iq[i % 2], iw[i % 2], wab[i % 2], wsg[i % 2], dg[i % 2], Sb[i % 2]

            def prep(bank):
                S.dma('sp', f'eiq{i % 2}', lambda e: e.dma_start(out=iq_.t[:], in_=IQT[:, i * 128:(i + 1) * 128].rearrange("(h p) t -> p h t", p=128)),
                      reads=[DR('IQT')], writes=[iq_.r])
                S.dma('sp', f'eiw{i % 2}', lambda e: e.dma_start(out=iw_.t[:], in_=IWS[i * 128:(i + 1) * 128, :]), reads=[DR('IWS')], writes=[iw_.r])
                S.act(lambda e: e.activation(out=wab_.t[:], in_=iw_.t[:], func=AF.Abs), reads=[iw_.r], writes=[wab_.r])
                S.act(lambda e: e.activation(out=wsg_.t[:], in_=iw_.t[:], func=AF.Sign), reads=[iw_.r], writes=[wsg_.r])
                for h in range(16):
                    S.pool(lambda e, h=h: e.tensor_scalar(out=dg_.t[:, h, :], in0=c.identb.t[:], scalar1=wsg_.t[:, h:h + 1], scalar2=None, op0=ALU.mult),
                           reads=[c.identb.r, wsg_.r], writes=[dg_.r])
            steps.append((prep, None, False))
            for k4 in range(i + 1):
                for h in range(16):
                    def A(bank, h=h, k4=k4):
                        S.pe(lambda e: e.matmul(out=bank.t[:], lhsT=iq_.t[:, h, :], rhs=ikt.t[:, k4 * 512:(k4 + 1) * 512], start=True, stop=True),
                             reads=[iq_.r, ikt.r], writes=[bank.r])

                    def BC(bank, h=h, k4=k4):
                        r_ = rb[cnts['rb'] % 4]
                        cnts['rb'] += 1
                        S.act(lambda e: e.activation(out=r_.t[:], in_=bank.t[:], func=AF.Relu, scale=wab_.t[:, h:h + 1]),
                              reads=[bank.r, wab_.r], writes=[r_.r])
                        S.pe(lambda e: e.matmul(out=psS.t[:], lhsT=dg_.t[:, h, :], rhs=r_.t[:], start=(h == 0), stop=(h == 15)),
                             reads=[dg_.r, r_.r], writes=[psS.r])
                        if h == 15:
                            S.act(lambda e: e.activation(out=Sb_.t[:, k4 * 512:(k4 + 1) * 512], in_=psS.t[:], func=AF.Copy), reads=[psS.r], writes=[Sb_.r])
                    steps.append((A, BC))
            return steps

        def bis_steps(i):
            steps = []
            nk = 512 * (i + 1)
            nkb = 4 * (i + 1)
            Sb_, bs, sk, mid, cnt, g_, MTo = Sb[i % 2], bsl[i % 2], skl[i % 2], midl[i % 2], cntl[i % 2], gl[i % 2], MT_[i % 2]

            def init(bank):
                S.dve(lambda e: e.tensor_reduce(out=bs.t[:, 0:1], in_=Sb_.t[:, 0:nk], axis=AX.X, op=ALU.min), reads=[Sb_.r], writes=[bs.r])
                S.dve(lambda e: e.tensor_reduce(out=bs.t[:, 1:2], in_=Sb_.t[:, 0:nk], axis=AX.X, op=ALU.max), reads=[Sb_.r], writes=[bs.r])
                S.dve(lambda e: e.tensor_tensor(out=Sb_.t[:, nk - 512:nk], in0=Sb_.t[:, nk - 512:nk], in1=mskd.t[:], op=ALU.add),
                      reads=[Sb_.r, mskd.r], writes=[Sb_.r])
                S.dve(lambda e: e.tensor_scalar(out=bs.t[:, 0:1], in0=bs.t[:, 0:1], scalar1=-1.0, scalar2=None, op0=ALU.add), reads=[bs.r], writes=[bs.r])
                S.dve(lambda e: e.tensor_tensor(out=bs.t[:, 2:3], in0=bs.t[:, 1:2], in1=bs.t[:, 0:1], op=ALU.subtract), reads=[bs.r], writes=[bs.r])
                S.dve(lambda e: e.tensor_scalar(out=sk.t[:], in0=pw2.t[:], scalar1=bs.t[:, 2:3], scalar2=None, op0=ALU.mult), reads=[pw2.r, bs.r], writes=[sk.r])
                S.dve(lambda e: e.tensor_tensor(out=mid.t[:, 0:1], in0=bs.t[:, 0:1], in1=sk.t[:, 0:1], op=ALU.add), reads=[bs.r, sk.r], writes=[mid.r])
            steps.append((None, init))
            for k in range(NIT):
                def it(bank, k=k):
                    S.dve(lambda e: e.tensor_scalar(out=junk.t[:, 0:nk], in0=Sb_.t[:, 0:nk], scalar1=mid.t[:, k:k + 1], scalar2=0.0, op0=ALU.is_ge, op1=ALU.add,
                                                    accum_out=cnt.t[:, k:k + 1]), reads=[Sb_.r, mid.r], writes=[junk.r, cnt.r])
                    S.dve(lambda e: e.tensor_scalar(out=g_.t[:, k:k + 1], in0=cnt.t[:, k:k + 1], scalar1=float(TOPK), scalar2=sk.t[:, k:k + 1],
                                                    op0=ALU.is_ge, op1=ALU.mult), reads=[cnt.r, sk.r], writes=[g_.r])
                    kk = k + 1 if k < NIT - 1 else k
                    S.dve(lambda e: e.scalar_tensor_tensor(out=mid.t[:, k + 1:k + 2], in0=mid.t[:, k:k + 1], scalar=sk.t[:, kk:kk + 1], in1=g_.t[:, k:k + 1],
                                                           op0=ALU.subtract, op1=ALU.add), reads=[mid.r, sk.r, g_.r], writes=[mid.r])
                steps.append((None, it))

            def mk(bank):
                S.dve(lambda e: e.tensor_scalar(out=Mb.t[:, 0:nk], in0=Sb_.t[:, 0:nk], scalar1=mid.t[:, NIT:NIT + 1], scalar2=None, op0=ALU.is_ge),
                      reads=[Sb_.r, mid.r], writes=[Mb.r])
            steps.append((None, mk))
            for k8 in range(0, nkb, 8):
                def tr(bank, k8=k8):
                    nn = min(8, nkb - k8)
                    for k in range(nn):
                        S.pe(lambda e, k=k: e.transpose(out=pbT.t[:, k * 128:(k + 1) * 128], in_=Mb.t[:, (k8 + k) * 128:(k8 + k + 1) * 128], identity=c.identb.t[:]),
                             reads=[Mb.r, c.identb.r], writes=[pbT.r])
                    S.act(lambda e: e.activation(out=MTo.t[:, k8:k8 + nn, :], in_=pbT.t[:, 0:nn * 128].rearrange("p (a b) -> p a b", b=128), func=AF.Identity,
                                                 scale=BIGM, bias=mbias.t[:, 0:1]), reads=[pbT.r, mbias.r], writes=[MTo.r])
                steps.append((None, tr))
            return steps

        def attn_steps(i):
            steps = []
            nkb = 4 * (i + 1)
            MTi = MT_[i % 2]
            for gq in range(4):
                st = {}
                for kbi in range(nkb):
                    k4, sub = kbi // 4, kbi % 4

                    def A(bank, gq=gq, kbi=kbi, k4=k4, sub=sub, st=st):
                        if kbi == 0:
                            st['dq'] = dq[cnts['dq'] % 2]
                            dslot = cnts['dq'] % 2
                            cnts['dq'] += 1
                            dq_ = st['dq']
                            S.dma('sp', f'edq{dslot}', lambda e: e.dma_start(
                                out=dq_.t[:].rearrange("p (h t) -> p h t", t=128), in_=DQT[gq * 512:(gq + 1) * 512, i * 128:(i + 1) * 128].rearrange("(h p) t -> p h t", p=128)),
                                reads=[DR('DQT')], writes=[dq_.r])
                        if sub == 0:
                            s3 = cnts['kv'] % 3
                            cnts['kv'] += 1
                            st[('kt', k4)] = ktc[s3]
                            st[('v', k4)] = vc[s3]
                            kt_, v_ = ktc[s3], vc[s3]
                            S.dma('sp', f'ekt{s3}', lambda e: e.dma_start(out=kt_.t[:], in_=KT[gq, :, k4 * 512:(k4 + 1) * 512]), reads=[DR('KT')], writes=[kt_.r])
                            S.dma('sp', f'evc{s3}', lambda e: e.dma_start(
                                out=v_.t[:], in_=VV[k4 * 512:(k4 + 1) * 512, gq * 128:(gq + 1) * 128].rearrange("(a p) d -> p a d", p=128)),
                                reads=[DR('VV')], writes=[v_.r])
                        kt_, dq_ = st[('kt', k4)], st['dq']
                        S.pe(lambda e: e.matmul(out=bank.t[:], lhsT=kt_.t[:, sub * 128:(sub + 1) * 128], rhs=dq_.t[:], start=True, stop=False),
                             reads=[kt_.r, dq_.r], writes=[bank.r])
                        for hq in range(4):
                            S.pe(lambda e, hq=hq: e.matmul(out=bank.t[:, hq * 128:(hq + 1) * 128], lhsT=c.identb.t[:], rhs=MTi.t[:, kbi, :], start=False, stop=(hq == 3)),
                                 reads=[c.identb.r, MTi.r], writes=[bank.r])

                    def BC(bank, gq=gq, kbi=kbi, k4=k4, sub=sub, st=st):
                        e_ = eb[cnts['eb'] % 4]
                        cnts['eb'] += 1
                        v_ = st[('v', k4)]
                        S.act(lambda e: e.activation(out=e_.t[:], in_=bank.t[:], func=AF.Exp, scale=hsc), reads=[bank.r], writes=[e_.r])
                        S.pe(lambda e: e.matmul(out=psO.t[:], lhsT=v_.t[:, sub, :], rhs=e_.t[:], start=(kbi == 0), stop=(kbi == nkb - 1)),
                             reads=[v_.r, e_.r], writes=[psO.r])
                        S.pe(lambda e: e.matmul(out=psD.t[:], lhsT=c.onesb.t[:], rhs=e_.t[:], start=(kbi == 0), stop=(kbi == nkb - 1)),
                             reads=[c.onesb.r, e_.r], writes=[psD.r])
                        if kbi == nkb - 1:
                            od_ = od[cnts['od'] % 2]
                            oslot = cnts['od'] % 2
                            cnts['od'] += 1
                            S.dve(lambda e: e.reciprocal(out=rden.t[:], in_=psD.t[:]), reads=[psD.r], writes=[rden.r])
                            S.dve(lambda e: e.tensor_tensor(out=od_.t[:], in0=psO.t[:], in1=rden.t[:], op=ALU.mult), reads=[psO.r, rden.r], writes=[od_.r])
                            S.dma(STORE_ENG, f'eod{oslot}', lambda e: e.dma_start(
                                out=ODT[gq * 512:(gq + 1) * 512, i * 128:(i + 1) * 128].rearrange("(h p) t -> p h t", p=128),
                                in_=od_.t[:].rearrange("p (h t) -> p h t", t=128)), reads=[od_.r], writes=[DR('ODT')])
                    steps.append((A, BC))
            return steps

        def merge(lists):
            lists = [l for l in lists if l]
            pos = [0] * len(lists)
            outl = []
            while True:
                best, bf_ = None, None
                for li, l in enumerate(lists):
                    if pos[li] < len(l):
                        f = (pos[li] + 1) / len(l)
                        if best is None or f < bf_:
                            best, bf_ = li, f
                if best is None:
                    break
                outl.append(lists[best][pos[best]])
                pos[best] += 1
            return outl

        DEPTH = 3
        bankctr = [0]

        def run_steps(steps):
            assigned = [None] * len(steps)
            for p in range(len(steps) + DEPTH):
                if p < len(steps):
                    A = steps[p][0]
                    if A is not None:
                        if len(steps[p]) > 2 and not steps[p][2]:
                            A(None)
                        else:
                            assigned[p] = lg[bankctr[0] % 4]
                            bankctr[0] += 1
                            A(assigned[p])
                q = p - DEPTH
                if q >= 0:
                    BC = steps[q][1]
                    if BC is not None:
                        BC(assigned[q])

        for ss in range(NOWN + 2):
            lists = []
            if ss < NOWN:
                lists.append(idx_steps(ss))
            if 1 <= ss <= NOWN:
                lists.append(bis_steps(ss - 1))
            if ss >= 2:
                lists.append(attn_steps(ss - 2))
            run_steps(merge(lists))

    def phase_G():
        c = setup_common('G', npsf=8)
        make_wslots(c, 3)
        mT = kb.sb("g_mT", [128, 16, TOWN], BF16)
        for q in range(4):
            S.dma('sp', f'ga{q}', lambda e, q=q: e.dma_start(out=mT.t[:, q * 4:(q + 1) * 4, :],
                                                            in_=MT2[q * 512:(q + 1) * 512, :].rearrange("(a p) t -> p a t", p=128)),
                  reads=[DR('MT2')], writes=[mT.r])
        xs = [kb.sb(f"g_xs{i}", [128, 512], F32) for i in range(3)]
        os_ = [kb.sb(f"g_os{i}", [128, 512], F32) for i in range(3)]
        cnt = [0]

        def epi(tt, g0, gn, ps):
            s = cnt[0] % 3
            cnt[0] += 1
            x_, o_ = xs[s], os_[s]
            S.dma('sp', f'gx{s}', lambda e: e.dma_start(out=x_.t[:], in_=xown[tt * 128:(tt + 1) * 128, g0:g0 + gn]), writes=[x_.r])
            S.dve(lambda e: e.tensor_tensor(out=o_.t[:], in0=ps.t[:], in1=x_.t[:], op=ALU.add), reads=[ps.r, x_.r], writes=[o_.r])
            S.dma(STORE_ENG, f'go{s}', lambda e: e.dma_start(out=X1[tt * 128:(tt + 1) * 128, g0:g0 + gn], in_=o_.t[:]), reads=[o_.r], writes=[DR('X1')])
        for t4 in range(0, NOWN, 4):
            gemm_tm(c, w_out, 0, 2048, 16, lambda kc, tt, t4=t4: mT.t[:, kc, (t4 + tt) * 128:(t4 + tt + 1) * 128], [mT.r], 4,
                    lambda tt, g0, gn, ps, t4=t4: epi(t4 + tt, g0, gn, ps))

    def phase_H():
        c = setup_common('H', npsf=6)
        make_wslots(c, 2)
        n = make_norm_ctx(c, 1)
        g2 = load_cols(c, "g2col", norm2_g, 16)
        h2 = kb.sb("h_h2T", [128, 16, TOWN], BF16)
        for i in range(NOWN):
            norm_T(c, n, X1[i * 128:(i + 1) * 128, :], g2, lambda k0, k1, i=i: h2.t[:, k0:k1, i * 128:(i + 1) * 128], h2.r)
        fT = kb.sb("h_fT", [128, 64, 512], BF16)
        rl = [kb.sb(f"h_rl{i}", [128, 512], F32) for i in range(2)]
        xs = [kb.sb(f"h_xs{i}", [128, 512], F32) for i in range(3)]
        os_ = [kb.sb(f"h_os{i}", [128, 512], F32) for i in range(3)]
        cnt = [0]
        for q in range(4):
            def epi1(cb, t0, tn, ps, m):
                r_ = rl[cnt[0] % 2]
                cnt[0] += 1
                S.act(lambda e: e.activation(out=r_.t[:], in_=ps.t[:], func=AF.Relu), reads=[ps.r], writes=[r_.r])
                S.dve(lambda e: e.tensor_tensor(out=fT.t[:, cb, :], in0=r_.t[:], in1=r_.t[:], op=ALU.mult), reads=[r_.r], writes=[fT.r])
            gemm_fm(c, w_ff1, 0, DFF, lambda kc, t0, tn, q=q: h2.t[:, kc, q * 512 + t0:q * 512 + t0 + tn], [h2.r], 512, epi1)

            def epi2(tt, g0, gn, ps, q=q):
                s = cnt[0] % 3
                cnt[0] += 1
                x_, o_ = xs[s], os_[s]
                row = (q * 4 + tt) * 128
                S.dma('sp', f'hx{s}', lambda e: e.dma_start(out=x_.t[:], in_=X1[row:row + 128, g0:g0 + gn]), reads=[DR('X1')], writes=[x_.r])
                S.dve(lambda e: e.tensor_tensor(out=o_.t[:], in0=ps.t[:], in1=x_.t[:], op=ALU.add), reads=[ps.r, x_.r], writes=[o_.r])
                S.dma(STORE_ENG, f'ho{s}', lambda e: e.dma_start(out=out[row:row + 128, g0:g0 + gn], in_=o_.t[:]), reads=[o_.r], writes=[DR('out')])
            gemm_tm(c, w_ff2, 0, 2048, 64, lambda kc, tt: fT.t[:, kc, tt * 128:(tt + 1) * 128], [fT.r], 4, epi2)

    def run_phase(fn, *a):
        with ExitStack() as es:
            kb.es = es
            fn(*a)
            n = S.emit(nc)
        dr.clear()
        return n

    stats = {}
    import os
    phases = os.environ.get("KPHASES", "A1,A2,B,C,D,E,F,G,H").split(",")
    if "A1" in phases:
        stats['A1'] = run_phase(phase_A1)
    if "A2" in phases:
        stats['A2'] = run_phase(phase_A2)
    if "B" in phases:
        stats['B'] = run_phase(phase_B)
    if "C" in phases:
        stats['C'] = run_phase(phase_C)
    if "D" in phases:
        stats['D'] = run_phase(phase_proj, OGT, 'OGT', w_proj_gla, 0, None, MT, 'MT', 'D')
    if "E" in phases:
        stats['E'] = run_phase(phase_E)
    if "F" in phases:
        stats['F'] = run_phase(phase_proj, ODT, 'ODT', w_proj_dsa, 2048, (MT, 'MT'), MT2, 'MT2', 'F')
    if "G" in phases:
        stats['G'] = run_phase(phase_G)
    if "H" in phases:
        stats['H'] = run_phase(phase_H)
    return nc, stats


def _rope_tables(pos, rot_dim, full_dim=128):
    half = rot_dim // 2
    inv = (10000.0 ** (-np.arange(half, dtype=np.float32) * 2.0 / rot_dim)).astype(np.float32)
    ang = pos.astype(np.float32)[None, :] * inv[:, None]
    cos = np.ones((full_dim, pos.shape[0]), np.float32)
    sin = np.zeros((full_dim, pos.shape[0]), np.float32)
    cos[0:half] = np.cos(ang)
    cos[half:rot_dim] = np.cos(ang)
    sin[0:half] = np.sin(ang)
    sin[half:rot_dim] = np.sin(ang)
    return cos, sin


def _pmatT(rot_dim, full_dim=128):
    half = rot_dim // 2
    P = np.zeros((full_dim, full_dim), np.float32)
    for d in range(half):
        P[d, d + half] = -1.0
        P[d + half, d] = 1.0
    return np.ascontiguousarray(P.T)


def make_in_maps(inputs):
    x = np.asarray(inputs['x'], np.float32)
    f = lambda k: np.ascontiguousarray(np.asarray(inputs[k], np.float32)[0])
    row = lambda k: np.ascontiguousarray(np.asarray(inputs[k], np.float32)[0][None, :])
    shared = {
        'w_in': f('w_in'), 'gla_wg2': f('gla_wg2'), 'gla_bg': row('gla_bg'), 'gla_norm_g': row('gla_norm_g'),
        'w_proj_gla': f('w_proj_gla'), 'q_norm_g': row('q_norm_g'), 'k_norm_g': row('k_norm_g'),
        'idx_k_norm_g': row('idx_k_norm_g'), 'w_proj_dsa': f('w_proj_dsa'), 'b_gate': row('b_gate'),
        'w_out': f('w_out'), 'norm1_g': row('norm1_g'), 'norm2_g': row('norm2_g'), 'w_ff1': f('w_ff1'), 'w_ff2': f('w_ff2'),
    }
    jj = np.arange(128)
    shared['c_ident'] = np.eye(128, dtype=np.float32)
    shared['c_uincl'] = (jj[:, None] <= jj[None, :]).astype(np.float32)
    shared['c_ustrict'] = (jj[:, None] > jj[None, :]).astype(np.float32)
    shared['c_pdsaT'] = _pmatT(128)
    shared['c_pidxT'] = _pmatT(64)
    posall = np.arange(SEQ)
    shared['c_cosK'], shared['c_sinK'] = _rope_tables(posall, 128)
    shared['c_cosIK'], shared['c_sinIK'] = _rope_tables(posall, 64)
    maps = []
    for core in range(8):
        b, j = core // 4, core % 4
        m = dict(shared)
        m['xfull'] = np.ascontiguousarray(x[b])
        tiles = [4 * i + j for i in range(NOWN)]
        rows = np.concatenate([np.arange(t * 128, (t + 1) * 128) for t in tiles])
        m['xown'] = np.ascontiguousarray(x[b][rows])
        fl = np.zeros((128, 4), np.float32)
        fl[:, j] = 1.0
        m['c_flags'] = fl
        t = np.arange(128)[:, None]
        sp = np.arange(512)[None, :]
        m['c_maskd'] = np.where(sp <= j * 128 + t, 0.0, NEG).astype(np.float32)
        m['c_cosQ'], m['c_sinQ'] = _rope_tables(rows, 128)
        m['c_cosIQ'], m['c_sinIQ'] = _rope_tables(rows, 64)
        maps.append(m)
    return maps


_CACHE = {}


def kernel(**inputs):
    if 'nc' not in _CACHE:
        _CACHE['nc'] = build_program(False)[0]
    nc = _CACHE['nc']
    maps = make_in_maps(inputs)
    res = run_bass_kernel_spmd(nc, maps, core_ids=list(range(8)))
    outf = np.zeros((2, SEQ, D), np.float32)
    for core in range(8):
        b, j = core // 4, core % 4
        o = np.asarray(res.results[core]['out'], np.float32)
        for i in range(NOWN):
            t = 4 * i + j
            outf[b, t * 128:(t + 1) * 128, :] = o[i * 128:(i + 1) * 128, :]
    return outf
```
